# Optimizing a Trainium2 kernel written in Bass

```python
import math
import jax
import jax.numpy as jnp
from jax import lax
import numpy as np


D_MODEL = 1024
BATCH = 4
SEQ = 4096
DEPTH = 4

GRID_W = 64
CTX_LEN = 256
N_BRANCH = 3
BR_W = D_MODEL // 2
A_GROUPS = 4
A_GW = BR_W // A_GROUPS
A_CHUNK = 128
B_HD = 64
B_VD = 2 * B_HD
B_HEADS = BR_W // (2 * B_HD)
ATTN_BLOCK = 128
ROPE_BASE = 10000.0
C_HD = 64
C_HEADS = BR_W // C_HD
N_DIR = 2
W_LORA = 64
A_LORA = 64
NORM_EPS = 1e-6
LN_EPS = 1e-5
GN_EPS = 64e-5
STATE_SIZES = (BR_W, BR_W, BR_W, BR_W, N_DIR * W_LORA, N_DIR * A_LORA)
OUT_SIZES = (BR_W,) * 7 + (N_BRANCH * D_MODEL,)
STATE_COLS = 4 * BR_W + N_DIR * (W_LORA + A_LORA)
OUT_COLS = 7 * BR_W + N_BRANCH * D_MODEL
IN_COLS = STATE_COLS + OUT_COLS

kernel_name = 'hybrid_gmlp_diffattn_rwkv7_flow_block'


def rms_norm(x, g):
    xf = x.astype(jnp.float32)
    y = xf * lax.rsqrt(jnp.mean(xf * xf, axis=-1, keepdims=True) + NORM_EPS)
    return (y * g).astype(x.dtype)


def layer_norm(x, g, b):
    xf = x.astype(jnp.float32)
    mu = jnp.mean(xf, axis=-1, keepdims=True)
    var = jnp.mean(jnp.square(xf - mu), axis=-1, keepdims=True)
    return ((xf - mu) * lax.rsqrt(var + LN_EPS) * g + b).astype(x.dtype)


def split_cols(t, sizes):
    return jnp.split(t, np.cumsum(sizes)[:-1].tolist(), axis=-1)


def rope_table(pos):
    nf = B_HD // 4
    inv = ROPE_BASE ** (-jnp.arange(nf, dtype=jnp.float32) / nf)
    ang = pos.astype(jnp.float32)[:, None] * inv[None, :]
    return jnp.cos(ang)[:, None, None, :], jnp.sin(ang)[:, None, None, :]


def rotate(x, cos, sin):
    x1, x2 = jnp.split(x, 2, axis=-1)
    cos = cos.astype(x.dtype)
    sin = sin.astype(x.dtype)
    return jnp.concatenate([x1 * cos - x2 * sin, x1 * sin + x2 * cos], axis=-1)


def axial_rope(x, rope):
    (rc, rs), (cc, cs) = rope
    x_row, x_col = jnp.split(x, 2, axis=-1)
    return jnp.concatenate([rotate(x_row, rc, rs), rotate(x_col, cc, cs)], axis=-1)


def token_shift(x, taps):
    xp = jnp.pad(x, ((0, 0), (1, 1), (0, 0)))
    return xp[:, :-2] * taps[0] + xp[:, 1:-1] * taps[1] + xp[:, 2:] * taps[2]


def dir_stack(t):
    t = jnp.moveaxis(t, 2, 0)
    return jnp.stack([t[0], jnp.flip(t[1], axis=1)])


def dir_share(t):
    return jnp.stack([t, jnp.flip(t, axis=1)])


def state_side(ps, p, rope):
    b_, t_ = ps.shape[:2]
    dk, dv, kr, vr, wl, al = split_cols(ps, STATE_SIZES)
    k_att = rms_norm(dk.reshape(b_, t_, B_HEADS, 2, B_HD), p['d_knorm'])
    if rope is not None:
        k_att = axial_rope(k_att, rope)
    v_att = dv.reshape(b_, t_, B_HEADS, B_VD)
    k = token_shift(kr, p['r_conv'][1])
    v = token_shift(vr, p['r_conv'][2])
    wl = wl.reshape(b_, t_, N_DIR, W_LORA)
    al = al.reshape(b_, t_, N_DIR, A_LORA)
    w_log = (p['r_w0'] + jnp.einsum('btdr,drc->btdc', jnp.tanh(wl), p['r_w2'])).astype(jnp.float32)
    decay = jnp.exp(-jnp.exp(-jax.nn.softplus(-w_log) - 0.5))
    a = jax.nn.sigmoid(p['r_a0'] + jnp.einsum('btdr,drc->btdc', al, p['r_a2']))
    kk = (k * p['r_kk']).reshape(b_, t_, C_HEADS, C_HD).astype(jnp.float32)
    kk = kk / jnp.maximum(jnp.sqrt(jnp.sum(kk * kk, axis=-1, keepdims=True)), 1e-12)
    k_dir = k[:, :, None, :] * (1.0 + (a - 1.0) * p['r_ka'])
    heads = lambda z: z.reshape(b_, t_, N_DIR, C_HEADS, C_HD)
    k_h = k.reshape(b_, t_, C_HEADS, C_HD)
    v_h = v.reshape(b_, t_, C_HEADS, C_HD)
    return dict(k_att=k_att, v_att=v_att, k_h=k_h, v_h=v_h,
                kk=dir_share(kk), a=dir_stack(heads(a)), w=dir_stack(heads(decay)),
                k=dir_stack(heads(k_dir)), v=dir_share(v_h))


def out_side(po, p, rope):
    b_, t_ = po.shape[:2]
    dq, r, u, va, za, zb, zc, gl = split_cols(po, OUT_SIZES)
    q = rms_norm(dq.reshape(b_, t_, B_HEADS, 2, B_HD), p['d_qnorm'])
    if rope is not None:
        q = axial_rope(q, rope)
    r_h = token_shift(r, p['r_conv'][0]).reshape(b_, t_, C_HEADS, C_HD)
    return dict(q=q, r_h=r_h, u=jax.nn.gelu(u), va=jax.nn.gelu(va), za=za, zb=zb, zc=zc, gl=gl)


def chunk_gmlp(u, v, p):
    b_, t_, _ = v.shape
    v = layer_norm(v, p['a_ln_g'], p['a_ln_b'])
    vc = v.reshape(b_, t_ // A_CHUNK, A_CHUNK, A_GROUPS, A_GW)
    s = jnp.einsum('gpq,bnqgc->bnpgc', p['a_ws'], vc) + jnp.swapaxes(p['a_bs'], 0, 1)[None, None, :, :, None]
    return u * s.reshape(b_, t_, BR_W)


def diff_lambda(lam_p, lam_init):
    lp = lam_p.astype(jnp.float32)
    return jnp.exp(jnp.sum(lp[0] * lp[1])) - jnp.exp(jnp.sum(lp[2] * lp[3])) + lam_init


def diff_attention(q, k, v, lam):
    b_, t_ = q.shape[:2]
    nb = t_ // ATTN_BLOCK
    qb = jnp.moveaxis(q.reshape(b_, nb, ATTN_BLOCK, B_HEADS, 2, B_HD), 1, 0)
    scale = B_HD ** -0.5

    def one_block(qi):
        s = jnp.einsum('bqhjd,bkhjd->bhjqk', qi, k).astype(jnp.float32) * scale
        pr = jax.nn.softmax(s, axis=-1)
        pd = pr[:, :, 0] - lam * pr[:, :, 1]
        return jnp.einsum('bhqk,bkhe->bqhe', pd.astype(v.dtype), v)

    out = lax.map(one_block, qb)
    return jnp.moveaxis(out, 0, 1).reshape(b_, t_, B_HEADS, B_VD)


def rwkv_update(s, kk, a, w, k, v):
    sa = jnp.einsum('dbhvk,dbhk->dbhv', s, -kk)
    return s * w[..., None, :] + sa[..., :, None] * (kk * a)[..., None, :] + v[..., :, None] * k[..., None, :]


def rwkv_scan(s0, st, r=None):
    xs = tuple(jnp.moveaxis(st[n].astype(jnp.float32), 2, 0) for n in ('kk', 'a', 'w', 'k', 'v'))
    if r is None:
        def step_state(s, inp):
            return rwkv_update(s, *inp), None
        s_fin, _ = lax.scan(step_state, s0, xs)
        return s_fin, None

    def step(s, inp):
        s = rwkv_update(s, *inp[:5])
        return s, jnp.einsum('dbhvk,dbhk->dbhv', s, inp[5])

    s_fin, ys = lax.scan(step, s0, xs + (jnp.moveaxis(r.astype(jnp.float32), 2, 0),))
    return s_fin, jnp.moveaxis(ys, 0, 2)


def rwkv_readout(yd, r_h, k_h, v_h, p):
    b_, t_ = r_h.shape[:2]
    y = yd[0] + jnp.flip(yd[1], axis=1)
    mu = jnp.mean(y, axis=-1, keepdims=True)
    var = jnp.mean(jnp.square(y - mu), axis=-1, keepdims=True)
    y = ((y - mu) * lax.rsqrt(var + GN_EPS)).reshape(b_, t_, BR_W) * p['r_ln_g'] + p['r_ln_b']
    bonus = jnp.sum(r_h * k_h * p['r_rk'], axis=-1, keepdims=True) * v_h
    return (y + bonus.reshape(b_, t_, BR_W).astype(jnp.float32)).astype(r_h.dtype)


def merge(ya, yb, yc, gl, w_br, w_out):
    b_, t_ = ya.shape[:2]
    ys = jnp.stack([ya, yb, yc], axis=2)
    up = jnp.einsum('btic,icd->btid', ys, w_br)
    g = jax.nn.sigmoid(gl.reshape(b_, t_, N_BRANCH, D_MODEL))
    return jnp.einsum('btd,de->bte', jnp.sum(g * up, axis=2), w_out)


def mixer_out(so, st, k_all, v_all, s0, lam, lam_init, p):
    b_, t_ = so['u'].shape[:2]
    ya = chunk_gmlp(so['u'], so['va'], p) * jax.nn.silu(so['za'])
    att = diff_attention(so['q'], k_all, v_all, lam)
    att = rms_norm(att, p['d_subln_g']) * (1.0 - lam_init)
    yb = att.reshape(b_, t_, BR_W) * jax.nn.silu(so['zb'])
    s_fin, yd = rwkv_scan(s0, st, dir_share(so['r_h']))
    yc = rwkv_readout(yd, so['r_h'], st['k_h'], st['v_h'], p) * jax.nn.silu(so['zc'])
    return merge(ya, yb, yc, so['gl'], p['w_br'], p['w_out']), s_fin


def hybrid_layer(x, xc, c_act, cc_act, rope, lam_init, p, update_ctx):
    d = D_MODEL
    shift, scale, gate = jnp.split((c_act @ p['w_mod'] + p['b_mod'])[:, None, :], 3, axis=-1)
    n_mod_c = 3 if update_ctx else 2
    mod_c = cc_act @ p['w_mod'][:, :n_mod_c * d] + p['b_mod'][:n_mod_c * d]
    h = rms_norm(x, p['norm_g']) * (1.0 + scale) + shift
    hc = rms_norm(xc, p['norm_g']) * (1.0 + mod_c[d:2 * d]) + mod_c[:d]
    lam = diff_lambda(p['d_lam'], lam_init)
    s_zero = jnp.zeros((N_DIR, xc.shape[0], C_HEADS, C_HD, C_HD), jnp.float32)

    if update_ctx:
        pc = hc @ p['w_in']
        st_c = state_side(pc[..., :STATE_COLS], p, None)
        so_c = out_side(pc[..., STATE_COLS:], p, None)
        out_c, s_ctx = mixer_out(so_c, st_c, st_c['k_att'], st_c['v_att'], s_zero, lam, lam_init, p)
        xc_next = xc + mod_c[2 * d:] * out_c
    else:
        st_c = state_side(hc @ p['w_in'][:, :STATE_COLS], p, None)
        s_ctx, _ = rwkv_scan(s_zero, st_c)
        xc_next = None

    pl = h @ p['w_in']
    st = state_side(pl[..., :STATE_COLS], p, rope)
    so = out_side(pl[..., STATE_COLS:], p, rope)
    k_all = jnp.concatenate([st['k_att'], st_c['k_att']], axis=1)
    v_all = jnp.concatenate([st['v_att'], st_c['v_att']], axis=1)
    out, _ = mixer_out(so, st, k_all, v_all, s_ctx, lam, lam_init, p)
    return x + gate * out, xc_next


def setup_inputs(seed: int = 0) -> dict:
    key = jax.random.key(seed)
    keys = jax.random.split(key, 28)
    nrm = lambda i, shape, s=1.0: jax.random.normal(keys[i], shape, jnp.float32) * s
    L, D = DEPTH, D_MODEL
    taps = jnp.array([0.2, 0.6, 0.2], jnp.float32)[None, None, :, None]
    return {
        'x': nrm(0, (BATCH, SEQ, D)),
        'c': nrm(1, (BATCH, D)),
        'ctx': nrm(2, (BATCH, CTX_LEN, D)),
        'c_ctx': nrm(3, (D,)),
        'w_mod': nrm(4, (L, D, 3 * D), 0.5 * D ** -0.5),
        'b_mod': nrm(5, (L, 3 * D), 0.02),
        'norm_g': 1.0 + nrm(6, (L, D), 0.02),
        'w_in': nrm(7, (L, D, IN_COLS), D ** -0.5),
        'a_ln_g': 1.0 + nrm(8, (L, BR_W), 0.02),
        'a_ln_b': nrm(9, (L, BR_W), 0.02),
        'a_ws': nrm(10, (L, A_GROUPS, A_CHUNK, A_CHUNK), A_CHUNK ** -0.5),
        'a_bs': 1.0 + nrm(11, (L, A_GROUPS, A_CHUNK), 0.02),
        'd_qnorm': 1.0 + nrm(12, (L, B_HD), 0.02),
        'd_knorm': 1.0 + nrm(13, (L, B_HD), 0.02),
        'd_lam': nrm(14, (L, 4, B_HD), 0.1),
        'd_subln_g': 1.0 + nrm(15, (L, B_VD), 0.02),
        'r_conv': taps + nrm(16, (L, 3, 3, BR_W), 0.05),
        'r_w0': jnp.linspace(-6.0, -1.0, BR_W, dtype=jnp.float32) + nrm(17, (L, N_DIR, BR_W), 0.1),
        'r_w2': nrm(18, (L, N_DIR, W_LORA, BR_W), 0.3 * W_LORA ** -0.5),
        'r_a0': nrm(19, (L, N_DIR, BR_W), 0.1),
        'r_a2': nrm(20, (L, N_DIR, A_LORA, BR_W), 0.3 * A_LORA ** -0.5),
        'r_kk': 0.85 + nrm(21, (L, BR_W), 0.02),
        'r_ka': 1.0 + nrm(22, (L, BR_W), 0.02),
        'r_rk': nrm(23, (L, C_HEADS, C_HD), 0.1),
        'r_ln_g': 1.0 + nrm(24, (L, BR_W), 0.02),
        'r_ln_b': nrm(25, (L, BR_W), 0.02),
        'w_br': nrm(26, (L, N_BRANCH, BR_W, D), BR_W ** -0.5),
        'w_out': nrm(27, (L, D, D), D ** -0.5),
    }


def reference(x, c, ctx, c_ctx, w_mod, b_mod, norm_g, w_in, a_ln_g, a_ln_b, a_ws, a_bs,
              d_qnorm, d_knorm, d_lam, d_subln_g, r_conv, r_w0, r_w2, r_a0, r_a2,
              r_kk, r_ka, r_rk, r_ln_g, r_ln_b, w_br, w_out):
    n_tok = x.shape[1]
    rows = n_tok // GRID_W
    row = jnp.broadcast_to(jnp.arange(rows)[:, None], (rows, GRID_W)).reshape(-1)
    col = jnp.broadcast_to(jnp.arange(GRID_W)[None, :], (rows, GRID_W)).reshape(-1)
    rope = (rope_table(row), rope_table(col))
    c_act = jax.nn.silu(c)
    cc_act = jax.nn.silu(c_ctx)
    xc = ctx
    for l in range(DEPTH):
        p = dict(w_mod=w_mod[l], b_mod=b_mod[l], norm_g=norm_g[l], w_in=w_in[l],
                 a_ln_g=a_ln_g[l], a_ln_b=a_ln_b[l], a_ws=a_ws[l], a_bs=a_bs[l],
                 d_qnorm=d_qnorm[l], d_knorm=d_knorm[l], d_lam=d_lam[l], d_subln_g=d_subln_g[l],
                 r_conv=r_conv[l], r_w0=r_w0[l], r_w2=r_w2[l], r_a0=r_a0[l], r_a2=r_a2[l],
                 r_kk=r_kk[l], r_ka=r_ka[l], r_rk=r_rk[l], r_ln_g=r_ln_g[l], r_ln_b=r_ln_b[l],
                 w_br=w_br[l], w_out=w_out[l])
        lam_init = 0.8 - 0.6 * math.exp(-0.3 * l)
        x, xc = hybrid_layer(x, xc, c_act, cc_act, rope, lam_init, p, l < DEPTH - 1)
    return x
```

```python
import numpy as np
import concourse.bass as bass
import concourse.mybir as mybir
from concourse.bass_utils import run_bass_kernel_spmd

F32 = mybir.dt.float32
BF16 = mybir.dt.bfloat16
AF = mybir.ActivationFunctionType
ALU = mybir.AluOpType
AX = mybir.AxisListType

D = 1024
NCTX = 256
TLAT = 4096
TALL = NCTX + TLAT
NTILE = TALL // 128
DEPTH = 4
INC = 8960
C0 = 0.6065306597126334
BLOCKS = [(0, 256)] + [(256 + 512 * i, 512) for i in range(8)]
NPP = 104
NBC = 2944
NCF = 1028
NCB = 2308
NDS = 32


class Buf:
    __slots__ = ("w", "r", "excl")

    def __init__(self):
        self.w = None
        self.r = {}
        self.excl = False


class TileT:
    def __init__(self, h, nb=1):
        self.h = h
        self.b = Buf()

    def __getitem__(self, k):
        return self.h[k]


class Sched:
    def __init__(self, nc):
        self.nc = nc
        self.sems = []
        self.eng = {}
        for name, h in (("pe", nc.tensor), ("act", nc.scalar), ("dve", nc.vector), ("pool", nc.gpsimd), ("sp", nc.sync)):
            sid = len(self.sems)
            self.sems.append(nc.alloc_semaphore("s_" + name))
            self.eng[name] = dict(h=h, sid=sid, n=0, waited={})
        self.dq = {}
        for q in ("sp", "pool", "act"):
            ids = []
            for i in range(NDS):
                ids.append(len(self.sems))
                self.sems.append(nc.alloc_semaphore("d%s%d" % (q, i)))
            self.dq[q] = dict(ids=ids, nxt=0)
        self.dcnt = {}
        self.nins = 0
        import os as _os2
        self.maxops = int(_os2.environ.get('MAXOPS', '1000000000'))
        self.k = 0

    def _wait(self, e, toks):
        E = self.eng[e]
        need = {}
        for t in toks:
            if t is None:
                continue
            sid, val = t
            if sid == E["sid"] and e == "pe":
                continue
            if need.get(sid, 0) < val:
                need[sid] = val
        for sid, val in need.items():
            if E["waited"].get(sid, 0) < val:
                E["h"].wait_ge(self.sems[sid], val)
                E["waited"][sid] = val
                self.nins += 1

    def _deps(self, reads, writes):
        toks = []
        for b in reads:
            toks.append(b.w)
        for b in writes:
            toks.append(b.w)
            toks.extend(b.r.items())
        return toks

    def _mark(self, tok, reads, writes):
        for b in reads:
            if b.r.get(tok[0], 0) < tok[1]:
                b.r[tok[0]] = tok[1]
        for b in writes:
            b.w = tok
            b.r = {}

    def op(self, e, emit, reads=(), writes=()):
        self.k += 1
        if self.k > self.maxops:
            return
        reads = [x.b if isinstance(x, TileT) else x for x in reads]
        writes = [x.b if isinstance(x, TileT) else x for x in writes]
        writes = writes + [b for b in reads if b.excl]
        reads = [b for b in reads if not b.excl]
        self._wait(e, self._deps(reads, writes))
        E = self.eng[e]
        E["n"] += 1
        ins = emit(E["h"])
        ins.then_inc(self.sems[E["sid"]], 1)
        self.nins += 1
        self._mark((E["sid"], E["n"]), reads, writes)

    def dma(self, q, out, in_, reads=(), writes=()):
        self.k += 1
        if self.k > self.maxops:
            return
        reads = [x.b if isinstance(x, TileT) else x for x in reads]
        writes = [x.b if isinstance(x, TileT) else x for x in writes]
        self._wait(q, self._deps(reads, writes))
        Q = self.dq[q]
        sid = Q["ids"][Q["nxt"] % NDS]
        Q["nxt"] += 1
        prev = self.dcnt.get(sid, 0)
        if prev > 0:
            self._wait(q, [(sid, 16 * prev)])
        self.dcnt[sid] = prev + 1
        self.eng[q]["h"].dma_start(out=out, in_=in_).then_inc(self.sems[sid], 16)
        self.nins += 1
        self._mark((sid, 16 * (prev + 1)), reads, writes)

    def barrier(self):
        toks = [(E["sid"], E["n"]) for E in self.eng.values() if E["n"] > 0]
        toks += [(sid, 16 * c) for sid, c in self.dcnt.items()]
        for e in self.eng:
            E = self.eng[e]
            for sid, val in toks:
                if sid == E["sid"]:
                    continue
                if E["waited"].get(sid, 0) < val:
                    E["h"].wait_ge(self.sems[sid], val)
                    E["waited"][sid] = val


class Alloc:
    def __init__(self, nc, S):
        self.nc = nc
        self.S = S
        self.stack = []
        self.cnt = 0

    def push(self):
        self.stack.append([])

    def pop(self):
        self.S.barrier()
        for cm in reversed(self.stack.pop()):
            cm.__exit__(None, None, None)

    def sb(self, shape, dt, name=None):
        self.cnt += 1
        cm = self.nc.sbuf_tensor("%s_%d" % (name or "t", self.cnt), list(shape), dt)
        h = cm.__enter__()
        self.stack[-1].append(cm)
        return TileT(h)


def build(NL=DEPTH, dbg=None, upto=9):
    import os as _os
    lo = int(_os.environ.get('SKIP_TO', '0'))
    RWCUT = float(_os.environ.get('RW_CUT', '99'))
    nc = bass.Bass("TRN2", target_bir_lowering=False)
    S = Sched(nc)
    A = Alloc(nc, S)

    def din(name, shape, dt=F32):
        return nc.dram_tensor(name, list(shape), dt, kind="ExternalInput").ap()

    def dscr(name, shape, dt=F32):
        return nc.dram_tensor(name, list(shape), dt, kind="Internal").ap()

    xT0 = din("xT0", [D, TALL])
    cc_d = din("cc", [128, 8, 2])
    pp_d = din("pp", [DEPTH, 128, NPP])
    bc_d = din("bc", [DEPTH, 128, NBC])
    cf_d = din("cf", [128, NCF])
    cb_d = din("cbf", [128, NCB])
    ropeC_d = din("ropeC", [128, TALL])
    ropeS_d = din("ropeS", [128, TALL])
    w_mod = din("w_mod", [DEPTH, D, 3 * D])
    w_in = din("w_in", [DEPTH, D, INC])
    a_wsT = din("a_wsT", [DEPTH, 4, 128, 128])
    r_w2 = din("r_w2", [DEPTH, 2, 64, 512])
    r_a2 = din("r_a2", [DEPTH, 2, 64, 512])
    w_br = din("w_br", [DEPTH, 3, 512, D])
    w_out = din("w_out", [DEPTH, D, D])
    yT = nc.dram_tensor("yT", [D, TLAT], F32, kind="ExternalOutput").ap()

    xTs = [dscr("xTa", [D, TALL]), dscr("xTb", [D, TALL])]
    qT = dscr("qT", [4, 128, TALL], BF16)
    kT = dscr("kT", [4, 128, TALL], BF16)
    Vd = dscr("Vd", [TALL, 512], BF16)
    krT = dscr("krT", [512, TALL])
    vrT = dscr("vrT", [512, TALL])
    rrT = dscr("rrT", [512, TALL])
    wlT = dscr("wlT", [128, TALL])
    alT = dscr("alT", [128, TALL])
    uzT = dscr("uzT", [512, TALL], BF16)
    vln = dscr("vln", [TALL, 512], BF16)
    zbd = dscr("zbd", [TALL, 512], BF16)
    zcd = dscr("zcd", [TALL, 512], BF16)
    gT = dscr("gT", [3 * D, TALL], BF16)
    yaT = dscr("yaT", [512, TALL], BF16)
    ybT = dscr("ybT", [512, TALL], BF16)
    ycT = dscr("ycT", [512, TALL], BF16)
    yfw = dscr("yfw", [TALL, 512])
    dbg_out = {}
    if dbg:
        for nm in dbg:
            src = dict(qT=qT, kT=kT, Vd=Vd, krT=krT, vrT=vrT, rrT=rrT, wlT=wlT, alT=alT, uzT=uzT, vln=vln, zbd=zbd,
                       zcd=zcd, gT=gT, yaT=yaT, ybT=ybT, ycT=ycT, yfw=yfw, xTa=xTs[0], xTb=xTs[1])[nm]
            dbg_out[nm] = (src, nc.dram_tensor("dbg_" + nm, list(src.shape), src.dtype, kind="ExternalOutput").ap())

    banks = [TileT(nc.alloc_psum_tensor("ps%d" % i, [128, 512], F32)) for i in range(8)]
    for b_k in banks:
        b_k.b.excl = True
    pstate = dict(i=0)

    def psum():
        b = banks[pstate["i"] % 6]
        pstate["i"] += 1
        return b

    def mm(out, lhsT, rhs, R, W, start=True, stop=True):
        S.op("pe", lambda h: h.matmul(out, lhsT, rhs, start=start, stop=stop), R, W)

    def tr(out, in_, ident, R, W):
        S.op("pe", lambda h: h.transpose(out, in_, ident), R, W)

    def act(out, in_, func, R, W, bias=None, scale=None, accum=None, eng="act"):
        kw = {}
        if bias is not None:
            kw["bias"] = bias
        if scale is not None:
            kw["scale"] = scale
        if accum is not None:
            kw["accum_out"] = accum
        S.op("act", lambda h: h.activation(out, in_, func, **kw), R, W)

    def tt(e, out, in0, in1, op, R, W):
        S.op(e, lambda h: h.tensor_tensor(out, in0, in1, op), R, W)

    def ts(e, out, in0, s1, s2, op0, op1, R, W):
        if s2 is None:
            S.op(e, lambda h: h.tensor_scalar(out, in0, s1, None, op0), R, W)
        else:
            S.op(e, lambda h: h.tensor_scalar(out, in0, s1, s2, op0, op1), R, W)

    def stt(out, in0, sc, in1, op0, op1, R, W):
        S.op("dve", lambda h: h.scalar_tensor_tensor(out, in0, sc, in1, op0, op1), R, W)

    def cp(e, out, in_, R, W):
        if e == "act":
            S.op("act", lambda h: h.copy(out, in_), R, W)
        else:
            S.op(e, lambda h: h.tensor_copy(out, in_), R, W)

    def recip(out, in_, R, W):
        S.op("dve", lambda h: h.reciprocal(out, in_), R, W)

    def memset(e, ap, val, W):
        S.op(e, lambda h: h.memset(ap, val), [], W)

    class Rot:
        def __init__(self, tiles):
            self.t = tiles
            self.i = 0

        def get(self):
            t = self.t[self.i % len(self.t)]
            self.i += 1
            return t

    A.push()
    cf = A.sb([128, NCF], F32, "cf")
    cb = A.sb([128, NCB], BF16, "cb")
    cc = A.sb([128, 8, 2], F32, "cc")
    cact = A.sb([128, 8, 2], F32, "cact")
    epsT = A.sb([128, 4], F32, "eps")
    onesf = A.sb([128, 128], F32, "onesf")
    S.dma("sp", cf[:], cf_d[:, :], [], [cf])
    S.dma("pool", cb[:], cb_d[:, :], [], [cb])
    S.dma("sp", cc[:], cc_d[:, :, :], [], [cc])
    act(cact[:], cc[:], AF.Silu, [cc], [cact])
    memset("dve", epsT[:, 0:1], 1e-6, [epsT])
    memset("dve", epsT[:, 1:2], 1e-5, [epsT])
    memset("dve", epsT[:, 2:3], 64e-5, [epsT])
    memset("dve", epsT[:, 3:4], 0.0, [epsT])
    memset("dve", onesf[:], 1.0, [onesf])
    ident_f = lambda ps, cs: cf.h[ps, cs]
    blk64_f = cf.h[:, 128:256]
    hsel_f = cf.h[:, 256:258]
    maskT = [cf.h[:, 260:516], cf.h[:, 516:772]]
    maskL = [cf.h[:, 772:900], cf.h[:, 900:1028]]
    ident_b = cb.h[:, 0:128]
    ones_b = cb.h[:, 128:256]
    blk64_b = cb.h[:, 256:384]
    rotT_b = cb.h[:, 384:512]
    hsel_b = cb.h[:, 512:514]

    def mLT(d, lev):
        c0 = 516 + (d * 7 + lev) * 128
        return cb.h[:, c0:c0 + 128]

    for l in range(NL):
        lam_init = 0.8 - 0.6 * float(np.exp(-0.3 * l))
        xcur = xT0 if l == 0 else xTs[(l - 1) % 2]
        xnext = xTs[l % 2]
        last = l == DEPTH - 1
        A.push()
        pp = A.sb([128, NPP], F32, "pp")
        bcs = A.sb([128, NBC], F32, "bc")
        mod = A.sb([128, 24, 2], F32, "mod")
        gs = A.sb([128, 8, 2], F32, "gs")
        lamt = A.sb([128, 8], F32, "lam")
        sublng = A.sb([128, 128], F32, "sublng")
        omka = A.sb([128, 4], F32, "omka")
        S.dma("sp", pp[:], pp_d[l, :, :], [], [pp])
        S.dma("sp", bcs[:], bc_d[l, :, :], [], [bcs])

        A.push()
        wm = Rot([A.sb([128, 8, 512], F32, "wm") for _ in range(2)])
        pb = psum()
        wmv = w_mod[l].rearrange("(kc p) n -> p kc n", p=128)
        for g6 in range(6):
            w = wm.get()
            S.dma("sp", w[:], wmv[:, :, g6 * 512:(g6 + 1) * 512], [], [w])
            for nt4 in range(4):
                nt = g6 * 4 + nt4
                for kc in range(8):
                    mm(pb[:, nt * 2:nt * 2 + 2], w[:, kc, nt4 * 128:(nt4 + 1) * 128], cact[:, kc, :], [w, cact], [pb],
                       start=(kc == 0), stop=(kc == 7))
        tt("dve", mod[:], pb[:, 0:48].rearrange("p (a b) -> p a b", b=2),
           pp[:, 0:24].unsqueeze(2).to_broadcast([128, 24, 2]), ALU.add, [pb, pp], [mod])
        ts("dve", gs[:], mod[:, 8:16, :], 1.0, None, ALU.add, None, [mod], [gs])
        tt("dve", gs[:], gs[:], pp[:, 24:32].unsqueeze(2).to_broadcast([128, 8, 2]), ALU.mult, [gs, pp], [gs])
        lamtmp = A.sb([128, 2, 64], F32, "lamtmp")
        dl = bcs[:, 2688:2944].rearrange("p (a b) -> p a b", b=64)
        tt("dve", lamtmp[:, 0, :], dl[:, 0, :], dl[:, 1, :], ALU.mult, [bcs], [lamtmp])
        tt("dve", lamtmp[:, 1, :], dl[:, 2, :], dl[:, 3, :], ALU.mult, [bcs], [lamtmp])
        S.op("dve", lambda h: h.tensor_reduce(lamt[:, 0:2], lamtmp[:], AX.X, ALU.add), [lamtmp], [lamt])
        act(lamt[:, 2:4], lamt[:, 0:2], AF.Exp, [lamt], [lamt])
        tt("dve", lamt[:, 4:5], lamt[:, 2:3], lamt[:, 3:4], ALU.subtract, [lamt], [lamt])
        ts("dve", lamt[:, 5:6], lamt[:, 4:5], lam_init, -1.0, ALU.add, ALU.mult, [lamt], [lamt])
        ts("dve", sublng[:], bcs[:, 1536:1664], 1.0 - lam_init, None, ALU.mult, None, [bcs], [sublng])
        ts("dve", omka[:], pp[:, 90:94], -1.0, 1.0, ALU.mult, ALU.add, [pp], [omka])
        neg_lam = lamt.h[:, 5:6]
        A.pop()

        A.push()
        hT = A.sb([128, 8, TALL], BF16, "hT")
        hTb = [Buf() for _ in BLOCKS]
        A.push()
        xin = Rot([A.sb([128, 8, 512], F32, "xin") for _ in range(2)])
        sqs = Rot([A.sb([128, 8, 512], BF16, "sq") for _ in range(2)])
        rstds = Rot([A.sb([128, 512], F32, "rstd") for _ in range(2)])
        tmps = Rot([A.sb([128, 512], F32, "ntmp") for _ in range(3)])
        xv = xcur.rearrange("(kc p) t -> p kc t", p=128)
        for bi, (t0, n) in enumerate(BLOCKS if lo <= 1 <= upto else []):
            j = 1 if bi == 0 else 0
            x_ = xin.get()
            S.dma("sp", x_[:, :, :n], xv[:, :, t0:t0 + n], [], [x_])
            sq = sqs.get()
            act(sq[:, :, :n], x_[:, :, :n], AF.Square, [x_], [sq])
            pb = psum()
            for kc in range(8):
                mm(pb[:, :n], ones_b, sq[:, kc, :n], [cb, sq], [pb], start=(kc == 0), stop=(kc == 7))
            rs = rstds.get()
            act(rs[:, :n], pb[:, :n], AF.Sqrt, [pb, epsT], [rs], bias=epsT[:, 0:1], scale=1.0 / D)
            recip(rs[:, :n], rs[:, :n], [rs], [rs])
            for kc in range(8):
                tm = tmps.get()
                stt(tm[:, :n], x_[:, kc, :n], gs[:, kc, j:j + 1], rs[:, :n], ALU.mult, ALU.mult, [x_, gs, rs], [tm])
                act(hT[:, kc, t0:t0 + n], tm[:, :n], AF.Identity, [tm, mod], [hTb[bi]], bias=mod[:, kc, j:j + 1])
        A.pop()

        A.push()
        wbufs = Rot([A.sb([128, 8, 512], BF16, "wbuf") for _ in range(4)])
        ostf = Rot([A.sb([128, 512], F32, "ostf") for _ in range(4)])
        ostb = Rot([A.sb([128, 512], BF16, "ostb") for _ in range(4)])
        ptmp = Rot([A.sb([128, 512], F32, "ptmp") for _ in range(6)])
        ptmpb = Rot([A.sb([128, 512], BF16, "ptmpb") for _ in range(4)])
        ropeCs = Rot([A.sb([128, 512], F32, "rC") for _ in range(2)])
        ropeSs = Rot([A.sb([128, 512], F32, "rS") for _ in range(2)])
        stat = Rot([A.sb([128, 16], F32, "stat") for _ in range(4)])
        winv = w_in[l].rearrange("(kc p) n -> p kc n", p=128)

        def loadw(c0, ncols=512):
            w = wbufs.get()
            S.dma("pool", w[:, :, :ncols], winv[:, :, c0:c0 + ncols], [], [w])
            return w

        def fm_mm(w, ct, bi, t0, n):
            pb = psum()
            for kc in range(8):
                mm(pb[:, :n], w[:, kc, ct * 128:(ct + 1) * 128], hT[:, kc, t0:t0 + n], [w, hTb[bi]], [pb],
                   start=(kc == 0), stop=(kc == 7))
            return pb

        def tm_mm(w, t0):
            pb = psum()
            bi = 0 if t0 < 256 else 1 + (t0 - 256) // 512
            for kc in range(8):
                mm(pb[:, :], hT[:, kc, t0:t0 + 128], w[:, kc, :], [w, hTb[bi]], [pb], start=(kc == 0), stop=(kc == 7))
            return pb

        def fm_raw(c0, ncols, dst, alt=[0]):
            w = loadw(c0, ncols)
            for bi, (t0, n) in enumerate(BLOCKS):
                for ct in range(ncols // 128):
                    pb = fm_mm(w, ct, bi, t0, n)
                    o = ostf.get()
                    alt[0] ^= 1
                    cp("act" if alt[0] else "dve", o[:, :n], pb[:, :n], [pb], [o])
                    S.dma("pool", dst[ct * 128:(ct + 1) * 128, t0:t0 + n], o[:, :n], [o], [])

        def fm_qk(c0, dst, gcol):
            w = loadw(c0)
            for bi, (t0, n) in enumerate(BLOCKS):
                rC = ropeCs.get()
                rS = ropeSs.get()
                S.dma("sp", rC[:, :n], ropeC_d[:, t0:t0 + n], [], [rC])
                S.dma("sp", rS[:, :n], ropeS_d[:, t0:t0 + n], [], [rS])
                for ct in range(4):
                    pb = fm_mm(w, ct, bi, t0, n)
                    sqb = ptmpb.get()
                    act(sqb[:, :n], pb[:, :n], AF.Square, [pb], [sqb])
                    p2 = psum()
                    mm(p2[:, :n], blk64_b, sqb[:, :n], [cb, sqb], [p2])
                    rs = ptmp.get()
                    act(rs[:, :n], p2[:, :n], AF.Sqrt, [p2, epsT], [rs], bias=epsT[:, 0:1], scale=1.0 / 64)
                    recip(rs[:, :n], rs[:, :n], [rs], [rs])
                    xn = ptmpb.get()
                    stt(xn[:, :n], pb[:, :n], pp[:, gcol:gcol + 1], rs[:, :n], ALU.mult, ALU.mult, [pb, pp, rs], [xn])
                    p3 = psum()
                    mm(p3[:, :n], rotT_b, xn[:, :n], [cb, xn], [p3])
                    t1 = ptmp.get()
                    tt("pool", t1[:, :n], xn[:, :n], rC[:, :n], ALU.mult, [xn, rC], [t1])
                    t2 = ptmp.get()
                    tt("dve", t2[:, :n], p3[:, :n], rS[:, :n], ALU.mult, [p3, rS], [t2])
                    o = ostb.get()
                    tt("pool", o[:, :n], t1[:, :n], t2[:, :n], ALU.add, [t1, t2], [o])
                    S.dma("pool", dst[ct, :, t0:t0 + n], o[:, :n], [o], [])

        def tm_group(c0, kind, dst):
            w = loadw(c0)
            for ti in range(NTILE):
                t0 = ti * 128
                pb = tm_mm(w, t0)
                o = ostb.get()
                if kind == "copy":
                    cp("act" if ti % 2 else "dve", o[:], pb[:], [pb], [o])
                elif kind == "silu":
                    act(o[:], pb[:], AF.Silu, [pb], [o])
                else:
                    tA = ptmp.get()
                    tB = ptmp.get()
                    g = ptmp.get()
                    act(tA[:], pb[:], AF.Square, [pb], [tA])
                    ts("dve", tA[:], tA[:], 0.044715, 1.0, ALU.mult, ALU.add, [tA], [tA])
                    tt("dve", tA[:], tA[:], pb[:], ALU.mult, [tA, pb], [tA])
                    act(tB[:], tA[:], AF.Sigmoid, [tA], [tB], scale=1.5957691216057308)
                    tt("dve", g[:], tB[:], pb[:], ALU.mult, [tB, pb], [g])
                    st = stat.get()
                    S.op("dve", lambda h: h.bn_stats(st[:, 0:6], g[:]), [g], [st])
                    S.op("dve", lambda h: h.bn_aggr(st[:, 8:10], st[:, 0:6]), [st], [st])
                    act(st[:, 10:11], st[:, 9:10], AF.Sqrt, [st, epsT], [st], bias=epsT[:, 1:2], scale=1.0)
                    recip(st[:, 11:12], st[:, 10:11], [st], [st])
                    ts("dve", g[:], g[:], st[:, 8:9], st[:, 11:12], ALU.subtract, ALU.mult, [g, st], [g])
                    tt("pool", g[:], g[:], bcs[:, 0:512], ALU.mult, [g, bcs], [g])
                    tt("pool", o[:], g[:], bcs[:, 512:1024], ALU.add, [g, bcs], [o])
                S.dma("pool", dst[t0:t0 + 128, :], o[:], [o], [])

        def fm_uz():
            wu = loadw(3328)
            wz = loadw(4352)
            for bi, (t0, n) in enumerate(BLOCKS):
                for ct in range(4):
                    pu = fm_mm(wu, ct, bi, t0, n)
                    pz = fm_mm(wz, ct, bi, t0, n)
                    tA = ptmp.get()
                    tB = ptmp.get()
                    g = ptmp.get()
                    act(tA[:, :n], pu[:, :n], AF.Square, [pu], [tA])
                    ts("dve", tA[:, :n], tA[:, :n], 0.044715, 1.0, ALU.mult, ALU.add, [tA], [tA])
                    tt("dve", tA[:, :n], tA[:, :n], pu[:, :n], ALU.mult, [tA, pu], [tA])
                    act(tB[:, :n], tA[:, :n], AF.Sigmoid, [tA], [tB], scale=1.5957691216057308)
                    tt("dve", g[:, :n], tB[:, :n], pu[:, :n], ALU.mult, [tB, pu], [g])
                    sz = ptmp.get()
                    act(sz[:, :n], pz[:, :n], AF.Silu, [pz], [sz])
                    o = ostb.get()
                    tt("pool", o[:, :n], g[:, :n], sz[:, :n], ALU.mult, [g, sz], [o])
                    S.dma("pool", uzT[ct * 128:(ct + 1) * 128, t0:t0 + n], o[:, :n], [o], [])

        def fm_gl(c0, gi):
            w = loadw(c0)
            for bi, (t0, n) in enumerate(BLOCKS):
                for ct in range(4):
                    pb = fm_mm(w, ct, bi, t0, n)
                    o = ostb.get()
                    act(o[:, :n], pb[:, :n], AF.Sigmoid, [pb], [o])
                    r0 = gi * 512 + ct * 128
                    S.dma("pool", gT[r0:r0 + 128, t0:t0 + n], o[:, :n], [o], [])

        if lo <= 2 <= upto:
            fm_qk(0, kT, 33)
            tm_group(512, "copy", Vd)
            fm_raw(1024, 512, krT)
            fm_raw(1536, 512, vrT)
            fm_raw(2048, 128, wlT)
            fm_raw(2176, 128, alT)
            fm_qk(2304, qT, 32)
            fm_raw(2816, 512, rrT)
            fm_uz()
            tm_group(3840, "va", vln)
            tm_group(4864, "silu", zbd)
            tm_group(5376, "silu", zcd)
            for gi in range(6):
                fm_gl(5888 + gi * 512, gi)
        A.pop()
        A.pop()

        A.push()
        wsT = A.sb([128, 4, 128], BF16, "wsT")
        S.dma("pool", wsT[:], a_wsT[l].rearrange("g q p -> q g p"), [], [wsT])
        vlb = Rot([A.sb([128, 4, 512], BF16, "vlb") for _ in range(2)])
        uzb = Rot([A.sb([128, 4, 512], BF16, "uzb") for _ in range(2)])
        gtmp = Rot([A.sb([128, 512], F32, "gtmp") for _ in range(3)])
        gost = Rot([A.sb([128, 512], BF16, "gost") for _ in range(3)])
        vlnv = vln.rearrange("(t p) c -> p t c", p=128)
        uzv = uzT.rearrange("(ct p) t -> p ct t", p=128)
        for bi, (t0, n) in enumerate(BLOCKS if lo <= 3 <= upto else []):
            nt = n // 128
            vb_ = vlb.get()
            ub_ = uzb.get()
            S.dma("sp", vb_[:, :nt, :], vlnv[:, t0 // 128:t0 // 128 + nt, :], [], [vb_])
            S.dma("sp", ub_[:, :, :n], uzv[:, :, t0:t0 + n], [], [ub_])
            for g in range(4):
                pb = psum()
                for ti in range(nt):
                    mm(pb[:, ti * 128:(ti + 1) * 128], vb_[:, ti, g * 128:(g + 1) * 128], wsT[:, g, :], [vb_, wsT], [pb])
                tg = gtmp.get()
                tt("dve", tg[:, :n].rearrange("p (a b) -> p a b", b=128), pb[:, :n].rearrange("p (a b) -> p a b", b=128),
                   bcs[:, 1024 + g * 128:1024 + (g + 1) * 128].unsqueeze(1).to_broadcast([128, nt, 128]), ALU.add,
                   [pb, bcs], [tg])
                o = gost.get()
                tt("pool", o[:, :n], tg[:, :n], ub_[:, g, :n], ALU.mult, [tg, ub_], [o])
                S.dma("pool", yaT[g * 128:(g + 1) * 128, t0:t0 + n], o[:, :n], [o], [])
        A.pop()

        A.push()
        kTh = Rot([A.sb([128, TALL], BF16, "kTh") for _ in range(2)])
        Vh = Rot([A.sb([128, NTILE, 132], BF16, "Vh") for _ in range(2)])
        for v_ in Vh.t:
            memset("pool", v_[:, :, 128:132], 1.0, [v_])
        PTs = Rot([A.sb([128, NTILE, 512], BF16, "PT") for _ in range(3)])
        qTb = Rot([A.sb([128, 512], BF16, "qTb") for _ in range(2)])
        zbt = Rot([A.sb([128, 4, 128], BF16, "zbt") for _ in range(2)])
        ao = Rot([A.sb([128, 128], F32, "ao") for _ in range(6)])
        ast = Rot([A.sb([128, 8], F32, "ast") for _ in range(4)])
        ayb = Rot([A.sb([128, 128], BF16, "ayb") for _ in range(3)])
        aost = Rot([A.sb([128, 512], BF16, "aost") for _ in range(2)])
        Vv = Vd.rearrange("(t p) c -> p t c", p=128)
        zbv = zbd.rearrange("(t p) c -> p t c", p=128)
        for h_ in range(4 if lo <= 4 <= upto else 0):
            kh = kTh.get()
            vh = Vh.get()
            S.dma("sp", kh[:], kT[h_, :, :], [], [kh])
            for q0 in range(0, NTILE, 9):
                q1 = min(NTILE, q0 + 9)
                S.dma("sp", vh[:, q0:q1, 0:128], Vv[:, q0:q1, h_ * 128:(h_ + 1) * 128], [], [vh])
            for bi, (t0, n) in enumerate(BLOCKS):
                nt = n // 128
                kts = [0, 1] if bi == 0 else list(range(NTILE))
                qb = qTb.get()
                S.dma("sp", qb[:, :n], qT[h_, :, t0:t0 + n], [], [qb])
                zt = zbt.get()
                S.dma("sp", zt[:, :nt, :], zbv[:, t0 // 128:t0 // 128 + nt, h_ * 128:(h_ + 1) * 128], [], [zt])
                PT = [PTs.get(), PTs.get()]
                for j in range(2):
                    js = slice(j * 64, (j + 1) * 64)
                    for kt in kts:
                        pb = psum()
                        mm(pb[:, :n], kh[js, kt * 128:(kt + 1) * 128], qb[js, :n], [kh, qb], [pb])
                        act(PT[j][:, kt, :n], pb[:, :n], AF.Exp, [pb], [PT[j]], scale=0.125)
                ob = aost.get()
                for ti in range(nt):
                    oj = []
                    for j in range(2):
                        pb = psum()
                        for ki, kt in enumerate(kts):
                            mm(pb[:, 0:129], PT[j][:, kt, ti * 128:(ti + 1) * 128], vh[:, kt, 0:129], [PT[j], vh], [pb],
                               start=(ki == 0), stop=(ki == len(kts) - 1))
                        st = ast.get()
                        recip(st[:, 0:1], pb[:, 128:129], [pb], [st])
                        o_ = ao.get()
                        ts("dve", o_[:], pb[:, 0:128], st[:, 0:1], None, ALU.mult, None, [pb, st], [o_])
                        oj.append(o_)
                    od = ao.get()
                    stt(od[:], oj[1][:], neg_lam, oj[0][:], ALU.mult, ALU.add, [oj[1], oj[0], lamt], [od])
                    st = ast.get()
                    junk = ao.get()
                    act(junk[:], od[:], AF.Square, [od], [junk, st], accum=st[:, 0:1])
                    act(st[:, 1:2], st[:, 0:1], AF.Sqrt, [st, epsT], [st], bias=epsT[:, 0:1], scale=1.0 / 128)
                    recip(st[:, 2:3], st[:, 1:2], [st], [st])
                    stt(od[:], od[:], st[:, 2:3], sublng[:], ALU.mult, ALU.mult, [od, st, sublng], [od])
                    yb_ = ayb.get()
                    tt("dve", yb_[:], od[:], zt[:, ti, :], ALU.mult, [od, zt], [yb_])
                    pb = psum()
                    pbb = pb.h[:].bitcast(BF16)
                    tr(pbb[:, 0:128], yb_[:], ident_b, [yb_, cb], [pb])
                    cp("act", ob[:, ti * 128:(ti + 1) * 128], pbb[:, 0:128], [pb], [ob])
                S.dma("pool", ybT[h_ * 128:(h_ + 1) * 128, t0:t0 + n], ob[:, :n], [ob], [])
        A.pop()

        A.push()
        w2a2 = [A.sb([128, 512], BF16, "w2a2") for _ in range(2)]
        for d_ in range(2):
            S.dma("pool", w2a2[d_][0:64, :], r_w2[l, d_, :, :], [], [w2a2[d_]])
            S.dma("pool", w2a2[d_][64:128, :], r_a2[l, d_, :, :], [], [w2a2[d_]])
        STb = [A.sb([128, 4, 64], BF16, "STb") for _ in range(2)]
        NW = 2

        def wsf(name, shape=(128, 128), dt=F32, k=NW):
            return Rot([A.sb(list(shape), dt, name) for _ in range(k)])

        W_kr = wsf("kr", (128, 130), k=3)
        W_vr = wsf("vr", (128, 130), k=3)
        W_rr = wsf("rr", (128, 130), k=3)
        W_wla = wsf("wla")
        W_lab = wsf("lab", dt=BF16)
        W_k = wsf("k")
        W_v = wsf("v")
        W_r = wsf("r")
        W_sg = wsf("sg")
        W_a = wsf("a")
        W_Gs = wsf("Gs")
        W_G2 = wsf("G2")
        W_Gx = wsf("Gx")
        W_eG = wsf("eG")
        W_enG = wsf("enG")
        W_eGx = wsf("eGx")
        W_eGC = wsf("eGC")
        W_kk = wsf("kk")
        W_sq = wsf("sq")
        W_sqb = wsf("sqb", dt=BF16)
        W_b = wsf("b")
        W_kd = wsf("kd")
        W_BhT = wsf("BhT", dt=BF16)
        W_KhT = wsf("KhT", dt=BF16)
        W_vb = wsf("vb", dt=BF16)
        W_rkk = wsf("rkk", dt=BF16)
        PS = [[dict(AR=A.sb([128, 2, 128], BF16, "AR"), Bt=A.sb([128, 128], BF16, "Bt"), Kt=A.sb([128, 128], BF16, "Kt"),
                    TK=A.sb([128, 3, 128], BF16, "TK"), Rtf=A.sb([128, 128], F32, "Rtf"), sm=A.sb([128, 8], F32, "sm"))
               for ct in range(4)] for par in range(2)]
        VTs = [A.sb([128, 4, 128], BF16, "VT") for par in range(2)]
        betas = [A.sb([128, 8], F32, "beta") for par in range(2)]
        SL = [dict(M1=A.sb([128, 256], BF16, "M1"), M2=A.sb([128, 256], BF16, "M2"), X=A.sb([128, 128], BF16, "X"),
                   T=[A.sb([128, 128], BF16, "T") for _ in range(2)], TT=[A.sb([128, 128], BF16, "TT") for _ in range(2)],
                   Lm=[A.sb([128, 128], BF16, "Lm") for _ in range(8)], W=[A.sb([128, 128], BF16, "W") for _ in range(2)],
                   UA=A.sb([128, 128], BF16, "UA"), UP=A.sb([128, 128], BF16, "UP"), Phi=A.sb([128, 64], BF16, "Phi"),
                   Psi=A.sb([128, 64], F32, "Psi"), QT=A.sb([128, 128], BF16, "QT")) for _ in range(8)]
        W_yo = wsf("yo", (128, 512), k=2)
        W_y = wsf("y", (128, 512), k=2)
        W_ysq = wsf("ysq", (128, 512), k=2)
        W_st = wsf("rst", (128, 48), k=2)
        W_zc = wsf("zc", (128, 512), BF16, 2)
        W_ycz = wsf("ycz", (128, 512), BF16, 2)
        W_yco = wsf("yco", (128, 4, 128), BF16, 2)
        zcv = zcd.rearrange("(t p) c -> p t c", p=128)
        ycTv = ycT.rearrange("(ct p) t -> p ct t", p=128)

        def conv(dst, src, which, ct):
            base = 34 + which * 12 + ct
            ts("dve", dst[:], src[:, 0:128], pp[:, base:base + 1], None, ALU.mult, None, [src, pp], [dst])
            stt(dst[:], src[:, 1:129], pp[:, base + 4:base + 5], dst[:], ALU.mult, ALU.add, [src, pp, dst], [dst])
            stt(dst[:], src[:, 2:130], pp[:, base + 8:base + 9], dst[:], ALU.mult, ALU.add, [src, pp, dst], [dst])

        def load_halo(tile, src, ct, t0):
            left = not (t0 == 0 or t0 == 256)
            right = not (t0 + 128 == 256 or t0 + 128 == TALL)
            lo = t0 - (1 if left else 0)
            hi = t0 + 128 + (1 if right else 0)
            c_lo = 1 - (t0 - lo)
            if not left:
                memset("pool", tile[:, 0:1], 0.0, [tile])
            if not right:
                memset("pool", tile[:, 129:130], 0.0, [tile])
            S.dma("sp", tile[:, c_lo:c_lo + (hi - lo)], src[ct * 128:(ct + 1) * 128, lo:hi], [], [tile])

        def prep_gen(d_, c_, par):
            t0 = c_ * 128
            wla = W_wla.get()
            S.dma("sp", wla[0:64, :], wlT[d_ * 64:(d_ + 1) * 64, t0:t0 + 128], [], [wla])
            S.dma("sp", wla[64:128, :], alT[d_ * 64:(d_ + 1) * 64, t0:t0 + 128], [], [wla])
            lab = W_lab.get()
            act(lab[0:64, :], wla[0:64, :], AF.Tanh, [wla], [lab])
            cp("dve", lab[64:128, :], wla[64:128, :], [wla], [lab])
            VT = VTs[par]
            beta = betas[par]
            yield
            for ct in range(4):
                P = PS[par][ct]
                AR, Bt, Kt, TK, Rtf, sm = P["AR"], P["Bt"], P["Kt"], P["TK"], P["Rtf"], P["sm"]
                krc = W_kr.get()
                vrc = W_vr.get()
                rrc = W_rr.get()
                load_halo(krc, krT, ct, t0)
                load_halo(vrc, vrT, ct, t0)
                load_halo(rrc, rrT, ct, t0)
                k_ = W_k.get()
                v_ = W_v.get()
                r_ = W_r.get()
                conv(r_, rrc, 0, ct)
                conv(k_, krc, 1, ct)
                conv(v_, vrc, 2, ct)
                yield
                pw = psum()
                mm(pw[:, 0:128], w2a2[d_][0:64, ct * 128:(ct + 1) * 128], lab[0:64, :], [w2a2[d_], lab], [pw])
                pw2 = psum()
                mm(pw2[:, 128:256], w2a2[d_][64:128, ct * 128:(ct + 1) * 128], lab[64:128, :], [w2a2[d_], lab], [pw2])
                sg = W_sg.get()
                a_ = W_a.get()
                act(sg[:], pw[:, 0:128], AF.Sigmoid, [pw, pp], [sg], bias=pp[:, 70 + d_ * 4 + ct:71 + d_ * 4 + ct])
                act(a_[:], pw2[:, 128:256], AF.Sigmoid, [pw2, pp], [a_], bias=pp[:, 78 + d_ * 4 + ct:79 + d_ * 4 + ct])
                yield
                Gs = W_Gs.get()
                S.op("dve", lambda h: h.tensor_tensor_scan(Gs[:], onesf[:], sg[:], 0.0, ALU.mult, ALU.add), [onesf, sg], [Gs])
                if d_ == 0:
                    G = Gs
                else:
                    G = W_G2.get()
                    stt(G[:], Gs[:], -1.0, sg[:], ALU.mult, ALU.add, [Gs, sg], [G])
                    ts("dve", G[:], G[:], Gs[:, 127:128], None, ALU.add, None, [G, Gs], [G])
                ts("dve", sm[:, 0:1], Gs[:, 127:128], -C0, None, ALU.mult, None, [Gs], [sm])
                Gx = W_Gx.get()
                tt("pool", Gx[:], G[:], sg[:], ALU.subtract, [G, sg], [Gx])
                eG = W_eG.get()
                enG = W_enG.get()
                eGx = W_eGx.get()
                eGC = W_eGC.get()
                act(eG[:], G[:], AF.Exp, [G], [eG], scale=-C0)
                act(enG[:], G[:], AF.Exp, [G], [enG], scale=C0)
                act(eGx[:], Gx[:], AF.Exp, [Gx], [eGx], scale=-C0)
                act(eGC[:], G[:], AF.Exp, [G, sm], [eGC], scale=C0, bias=sm[:, 0:1])
                act(sm[:, 1:2], sm[:, 0:1], AF.Exp, [sm], [sm])
                yield
                kk = W_kk.get()
                sq = W_sq.get()
                ts("dve", kk[:], k_[:], pp[:, 86 + ct:87 + ct], None, ALU.mult, None, [k_, pp], [kk])
                sqb = W_sqb.get()
                act(sqb[:], kk[:], AF.Square, [kk], [sqb])
                pn = psum()
                mm(pn[:, 0:128], blk64_b, sqb[:], [cb, sqb], [pn])
                act(sq[:], pn[:, 0:128], AF.Sqrt, [pn], [sq])
                ts("dve", sq[:], sq[:], 1e-12, None, ALU.max, None, [sq], [sq])
                recip(sq[:], sq[:], [sq], [sq])
                tt("dve", kk[:], kk[:], sq[:], ALU.mult, [kk, sq], [kk])
                yield
                b_ = W_b.get()
                kd = W_kd.get()
                tt("pool", b_[:], kk[:], a_[:], ALU.mult, [kk, a_], [b_])
                ts("dve", kd[:], a_[:], pp[:, 90 + ct:91 + ct], omka[:, ct:ct + 1], ALU.mult, ALU.add, [a_, pp, omka], [kd])
                tt("pool", kd[:], kd[:], k_[:], ALU.mult, [kd, k_], [kd])
                stt(AR[:, 0, :], kk[:], -1.0, eGx[:], ALU.mult, ALU.mult, [kk, eGx], [AR])
                tt("dve", Rtf[:], r_[:], eG[:], ALU.mult, [r_, eG], [Rtf])
                cp("act", AR[:, 1, :], Rtf[:], [Rtf], [AR])
                yield
                BhT = W_BhT.get()
                KhT = W_KhT.get()
                vb = W_vb.get()
                tt("pool", Bt[:], b_[:], enG[:], ALU.mult, [b_, enG], [Bt])
                tt("dve", Kt[:], kd[:], enG[:], ALU.mult, [kd, enG], [Kt])
                tt("pool", BhT[:], b_[:], eGC[:], ALU.mult, [b_, eGC], [BhT])
                tt("dve", KhT[:], kd[:], eGC[:], ALU.mult, [kd, eGC], [KhT])
                cp("act", vb[:], v_[:], [v_], [vb])
                yield
                ptr = psum()
                ptb = ptr.h[:].bitcast(BF16)
                tr(ptb[:, 0:128], AR[:, 0, :], ident_b, [AR, cb], [ptr])
                tr(ptb[:, 128:256], BhT[:], ident_b, [BhT, cb], [ptr])
                tr(ptb[:, 256:384], KhT[:], ident_b, [KhT, cb], [ptr])
                tr(ptb[:, 384:512], vb[:], ident_b, [vb, cb], [ptr])
                cp("dve", TK[:].rearrange("p a b -> p (a b)"), ptb[:, 0:384], [ptr], [TK])
                cp("dve", VT[:, ct, :], ptb[:, 384:512], [ptr], [VT])
                if d_ == 1:
                    rkk = W_rkk.get()
                    stt(rkk[:], r_[:], pp[:, 98 + ct:99 + ct], k_[:], ALU.mult, ALU.mult, [r_, pp, k_], [rkk])
                    pbt = psum()
                    mm(pbt[:, 0:2], rkk[:], hsel_b, [rkk, cb], [pbt])
                    cp("act", beta[:, ct * 2:ct * 2 + 2], pbt[:, 0:2], [pbt], [beta])
                yield

        def head_gen(d_, par, ct, hh, ybk):
            P = PS[par][ct]
            AR, Bt, Kt, TK, Rtf, sm = P["AR"], P["Bt"], P["Kt"], P["TK"], P["Rtf"], P["sm"]
            VT = VTs[par]
            Q = SL[ct * 2 + hh]
            M1, M2, X = Q["M1"], Q["M2"], Q["X"]
            hp = slice(hh * 64, (hh + 1) * 64)
            hc = hp
            ARf = AR[:].rearrange("p a b -> p (a b)")
            p1 = psum()
            mm(p1[:, 0:256], Bt[hp, :], ARf[hp, :], [Bt, AR], [p1])
            tt("dve", M1[:], p1[:, 0:256], maskT[d_], ALU.mult, [p1, cf], [M1])
            yield
            p2 = psum()
            mm(p2[:, 0:256], Kt[hp, :], ARf[hp, :], [Kt, AR], [p2])
            tt("dve", M2[:], p2[:, 0:256], maskT[d_], ALU.mult, [p2, cf], [M2])
            yield
            p3 = psum()
            mm(p3[:, 0:128], AR[hp, 0, :], Bt[hp, :], [AR, Bt], [p3])
            tt("dve", X[:], p3[:, 0:128], maskL[d_], ALU.mult, [p3, cf], [X])
            yield
            LTap = M1[:, 0:128]
            T_ = Q["T"][0]
            TT = Q["TT"][0]
            Lm = Q["Lm"]
            tt("pool", Lm[6][:], LTap, mLT(d_, 0), ALU.mult, [M1, cb], [Lm[6]])
            tt("pool", TT[:], Lm[6][:], ident_b, ALU.add, [Lm[6], cb], [TT])
            tt("pool", Lm[7][:], X[:], mLT(1 - d_, 0), ALU.mult, [X, cb], [Lm[7]])
            tt("pool", T_[:], Lm[7][:], ident_b, ALU.add, [Lm[7], cb], [T_])
            for lev in range(1, 7):
                tt("pool", Lm[lev - 1][:], LTap, mLT(d_, lev), ALU.mult, [M1, cb], [Lm[lev - 1]])
            yield
            for lev in range(1, 7):
                LmT = Lm[lev - 1]
                pa = psum()
                mm(pa[:, 0:128], LmT[:], T_[:], [LmT, T_], [pa])
                Wt = Q["W"][lev % 2]
                cp("act", Wt[:], pa[:, 0:128], [pa], [Wt])
                yield
                if lev < 6:
                    pT = psum()
                    mm(pT[:, 0:128], TT[:], Wt[:], [TT, Wt], [pT])
                    Tn = Q["T"][lev % 2]
                    tt("dve", Tn[:], pT[:, 0:128], T_[:], ALU.add, [pT, T_], [Tn])
                pTT = psum()
                mm(pTT[:, 0:128], Wt[:], TT[:], [Wt, TT], [pTT])
                TTn = Q["TT"][lev % 2]
                tt("dve", TTn[:], pTT[:, 0:128], TT[:], ALU.add, [pTT, TT], [TTn])
                TT = TTn
                if lev < 6:
                    T_ = Tn
                yield
            UA, UP, Phi, Psi, QT = Q["UA"], Q["UP"], Q["Phi"], Q["Psi"], Q["QT"]
            pwq = psum()
            mm(pwq[:, 0:64], M2[:, 0:128], VT[:, ct, hc], [M2, VT], [pwq])
            cp("act", UA[:, 0:64], pwq[:, 0:64], [pwq], [UA])
            cp("pool", UA[:, 64:128], TK[:, 0, hc], [TK], [UA])
            yield
            pu = psum()
            mm(pu[:, 0:128], TT[:], UA[:], [TT, UA], [pu])
            cp("act", UP[:], pu[:, 0:128], [pu], [UP])
            yield
            pf = psum()
            mm(pf[hp, 0:64], UP[:, 64:128], TK[:, 1, hc], [UP, TK], [pf])
            mm(pf[hp, 64:128], TK[:, 1, hc], UP[:, 0:64], [TK, UP], [pf], start=True, stop=False)
            mm(pf[hp, 64:128], TK[:, 2, hc], VT[:, ct, hc], [TK, VT], [pf], start=False, stop=True)
            mm(pf[hp, 128:256], UP[:, 64:128], M1[:, 128:256], [UP, M1], [pf])
            stt(Phi[hp, :], cf.h[hp, hc], sm[hp, 1:2], pf[hp, 0:64], ALU.mult, ALU.add, [cf, sm, pf], [Phi])
            cp("dve", Psi[hp, :], pf[hp, 64:128], [pf], [Psi])
            tt("dve", QT[hp, :], pf[hp, 128:256], Rtf[hp, :], ALU.add, [pf, Rtf], [QT])
            yield
            yc0 = ct * 128 + hh * 64
            mm(ybk[:, yc0:yc0 + 64], M1[:, 128:256], UP[:, 0:64], [M1, UP], [ybk], start=True, stop=False)
            mm(ybk[:, yc0:yc0 + 64], M2[:, 128:256], VT[:, ct, hc], [M2, VT], [ybk], start=False, stop=False)
            mm(ybk[:, yc0:yc0 + 64], QT[hp, :], STb[d_][hp, ct, :], [QT, STb[d_]], [ybk], start=False, stop=True)
            pS = psum()
            mm(pS[hp, 0:64], Phi[hp, :], STb[d_][hp, ct, :], [Phi, STb[d_]], [pS])
            tt("dve", STb[d_][hp, ct, :], pS[hp, 0:64], Psi[hp, :], ALU.add, [pS, Psi], [STb[d_]])
            yield

        def readout(d_, c_, par, ybk):
            t0 = c_ * 128
            VT = VTs[par]
            beta = betas[par]
            if d_ == 0:
                yo = W_yo.get()
                cp("act", yo[:], ybk[:], [ybk], [yo])
                S.dma("pool", yfw[t0:t0 + 128, :], yo[:], [yo], [])
                return
            yo = W_yo.get()
            S.dma("sp", yo[:], yfw[t0:t0 + 128, :], [], [yo])
            zc_ = W_zc.get()
            S.dma("sp", zc_[:], zcv[:, c_, :], [], [zc_])
            y = W_y.get()
            tt("dve", y[:], ybk[:], yo[:], ALU.add, [ybk, yo], [y])
            y3 = y[:].rearrange("p (a b) -> p a b", b=64)
            st = W_st.get()
            S.op("dve", lambda h: h.tensor_reduce(st[:, 0:8], y3, AX.X, ALU.add), [y], [st])
            ysq = W_ysq.get()
            act(ysq[:], y[:], AF.Square, [y], [ysq])
            S.op("dve", lambda h: h.tensor_reduce(st[:, 8:16], ysq[:].rearrange("p (a b) -> p a b", b=64), AX.X, ALU.add),
                 [ysq], [st])
            ts("dve", st[:, 0:16], st[:, 0:16], 1.0 / 64, None, ALU.mult, None, [st], [st])
            tt("dve", st[:, 16:24], st[:, 0:8], st[:, 0:8], ALU.mult, [st], [st])
            tt("dve", st[:, 24:32], st[:, 8:16], st[:, 16:24], ALU.subtract, [st], [st])
            act(st[:, 32:40], st[:, 24:32], AF.Sqrt, [st, epsT], [st], bias=epsT[:, 2:3], scale=1.0)
            recip(st[:, 40:48], st[:, 32:40], [st], [st])
            tt("dve", y3, y3, st[:, 0:8].unsqueeze(2).to_broadcast([128, 8, 64]), ALU.subtract, [y, st], [y])
            tt("dve", y3, y3, st[:, 40:48].unsqueeze(2).to_broadcast([128, 8, 64]), ALU.mult, [y, st], [y])
            tt("pool", y[:], y[:], bcs[:, 1664:2176], ALU.mult, [y, bcs], [y])
            tt("pool", y[:], y[:], bcs[:, 2176:2688], ALU.add, [y, bcs], [y])
            bon = ysq
            tt("dve", bon[:].rearrange("p (a b) -> p a b", b=64), VT[:].rearrange("p c (h e) -> p (c h) e", e=64),
               beta[:, 0:8].unsqueeze(2).to_broadcast([128, 8, 64]), ALU.mult, [VT, beta], [bon])
            tt("pool", y[:], y[:], bon[:], ALU.add, [y, bon], [y])
            ycz = W_ycz.get()
            tt("dve", ycz[:], y[:], zc_[:], ALU.mult, [y, zc_], [ycz])
            ptr = psum()
            ptb = ptr.h[:].bitcast(BF16)
            for ct in range(4):
                tr(ptb[:, ct * 128:(ct + 1) * 128], ycz[:, ct * 128:(ct + 1) * 128], ident_b, [ycz, cb], [ptr])
            yco = W_yco.get()
            cp("act", yco[:].rearrange("p a b -> p (a b)"), ptb[:, 0:512], [ptr], [yco])
            S.dma("pool", ycTv[:, :, t0:t0 + 128], yco[:], [yco], [])

        def drain(g):
            for _ in g:
                pass

        ybank_i = 0
        for d_ in range(2 if upto >= 5 else 0):
            memset("dve", STb[d_][:], 0.0, [STb[d_]])
            order = [0, 1] + list(range(2, NTILE)) if d_ == 0 else [1, 0] + list(range(NTILE - 1, 1, -1))
            order = order[:int(_os.environ.get('RW_CHUNKS', '99'))]
            drain(prep_gen(d_, order[0], 0))
            for i, c_ in enumerate(order):
                par = i % 2
                ybk = banks[6 + (ybank_i % 2)]
                ybank_i += 1
                heads = [head_gen(d_, par, ct, hh, ybk) for ct in range(4) for hh in range(2)]
                nprep = prep_gen(d_, order[i + 1], 1 - par) if i + 1 < len(order) else None
                while heads:
                    nxt = []
                    for g in heads:
                        try:
                            next(g)
                            nxt.append(g)
                        except StopIteration:
                            pass
                    heads = nxt
                    if nprep is not None:
                        for _ in range(3):
                            try:
                                next(nprep)
                            except StopIteration:
                                nprep = None
                                break
                if nprep is not None:
                    drain(nprep)
                readout(d_, c_, par, ybk)
            S.barrier()
        A.pop()

        A.push()
        wbr = A.sb([128, 3, 4, D], BF16, "wbr")
        wo = A.sb([128, 8, D], BF16, "wo")
        for i in range(3):
            S.dma("pool", wbr[:, i, :, :], w_br[l, i].rearrange("(ct p) d -> p ct d", p=128), [], [wbr])
        S.dma("pool", wo[:], w_out[l].rearrange("(kc p) e -> p kc e", p=128), [], [wo])
        ybl = [Rot([A.sb([128, 4, 512], BF16, "ybl%d" % i) for _ in range(2)]) for i in range(3)]
        gbl = Rot([A.sb([128, 512], BF16, "gbl") for _ in range(6)])
        mT = Rot([A.sb([128, 8, 512], BF16, "mT") for _ in range(2)])
        macc = Rot([A.sb([128, 512], F32, "macc") for _ in range(3)])
        mtmp = Rot([A.sb([128, 512], F32, "mtmp") for _ in range(3)])
        xin = Rot([A.sb([128, 8, 512], F32, "mxin") for _ in range(2)])
        xo = Rot([A.sb([128, 512], F32, "mxo") for _ in range(3)])
        srcs = [yaT, ybT, ycT]
        xv = xcur.rearrange("(kc p) t -> p kc t", p=128)
        for bi, (t0, n) in enumerate(BLOCKS):
            if (last and bi == 0) or upto < 6:
                continue
            j = 1 if bi == 0 else 0
            yb3 = []
            for i in range(3):
                y_ = ybl[i].get()
                S.dma("sp", y_[:, :, :n], srcs[i].rearrange("(ct p) t -> p ct t", p=128)[:, :, t0:t0 + n], [], [y_])
                yb3.append(y_)
            x_ = xin.get()
            S.dma("sp", x_[:, :, :n], xv[:, :, t0:t0 + n], [], [x_])
            m_ = mT.get()
            for dt_ in range(8):
                ma = macc.get()
                for i in range(3):
                    g_ = gbl.get()
                    r0 = i * D + dt_ * 128
                    S.dma("sp", g_[:, :n], gT[r0:r0 + 128, t0:t0 + n], [], [g_])
                    pb = psum()
                    for ct in range(4):
                        mm(pb[:, :n], wbr[:, i, ct, dt_ * 128:(dt_ + 1) * 128], yb3[i][:, ct, :n], [wbr, yb3[i]], [pb],
                           start=(ct == 0), stop=(ct == 3))
                    if i == 0:
                        tt("dve", ma[:, :n], pb[:, :n], g_[:, :n], ALU.mult, [pb, g_], [ma])
                    else:
                        tp = mtmp.get()
                        tt("dve", tp[:, :n], pb[:, :n], g_[:, :n], ALU.mult, [pb, g_], [tp])
                        if i == 1:
                            tt("pool", ma[:, :n], ma[:, :n], tp[:, :n], ALU.add, [ma, tp], [ma])
                        else:
                            tt("pool", m_[:, dt_, :n], ma[:, :n], tp[:, :n], ALU.add, [ma, tp], [m_])
            for et in range(8):
                pb = psum()
                for dt_ in range(8):
                    mm(pb[:, :n], wo[:, dt_, et * 128:(et + 1) * 128], m_[:, dt_, :n], [wo, m_], [pb],
                       start=(dt_ == 0), stop=(dt_ == 7))
                o = xo.get()
                stt(o[:, :n], pb[:, :n], mod[:, 16 + et, j:j + 1], x_[:, et, :n], ALU.mult, ALU.add, [pb, mod, x_], [o])
                if last:
                    S.dma("pool", yT[et * 128:(et + 1) * 128, t0 - 256:t0 - 256 + n], o[:, :n], [o], [])
                else:
                    S.dma("pool", xnext[et * 128:(et + 1) * 128, t0:t0 + n], o[:, :n], [o], [])
        A.pop()
        A.pop()

    S.barrier()
    if dbg:
        A.push()
        for nm, (src, dst) in dbg_out.items():
            if len(src.shape) == 3:
                for a in range(src.shape[0]):
                    S.dma("sp", dst[a], src[a], [], [])
            else:
                S.dma("sp", dst, src, [], [])
        A.pop()
    if NL < DEPTH and upto >= 6:
        S.dma("sp", yT[:, :], xTs[(NL - 1) % 2][:, 256:TALL], [], [])
    A.pop()
    S.barrier()
    return nc, S


def _consts():
    cf = np.zeros((128, NCF), np.float32)
    cf[:, 0:128] = np.eye(128)
    blk = np.zeros((128, 128), np.float32)
    blk[:64, :64] = 1
    blk[64:, 64:] = 1
    cf[:, 128:256] = blk
    cf[:64, 256] = 1
    cf[64:, 257] = 1
    idx = np.arange(128)
    fw_strict = (idx[:, None] < idx[None, :]).astype(np.float32)
    bw_strict = (idx[:, None] > idx[None, :]).astype(np.float32)
    eye = np.eye(128, dtype=np.float32)
    cf[:, 260:388] = fw_strict
    cf[:, 388:516] = fw_strict + eye
    cf[:, 516:644] = bw_strict
    cf[:, 644:772] = bw_strict + eye
    cf[:, 772:900] = fw_strict.T
    cf[:, 900:1028] = bw_strict.T
    cb = np.zeros((128, NCB), np.float32)
    cb[:, 0:128] = np.eye(128)
    cb[:, 128:256] = 1
    cb[:, 256:384] = blk
    rotT = np.zeros((128, 128), np.float32)
    for m in range(128):
        f = m % 64
        base = m - f
        blk32 = (f // 32) * 32
        g = f % 32
        partner = base + blk32 + (g + 16 if g < 16 else g - 16)
        rotT[partner, m] = 1
    cb[:, 384:512] = rotT
    cb[:64, 512] = 1
    cb[64:, 513] = 1
    for d in range(2):
        for lev in range(7):
            b = 1 << lev
            blk = idx // b
            blk2 = idx // (2 * b)
            same2 = blk2[:, None] == blk2[None, :]
            diff = blk[:, None] != blk[None, :]
            before = (idx[:, None] < idx[None, :]) if d == 0 else (idx[:, None] > idx[None, :])
            m = (same2 & diff & before).astype(np.float32)
            c0 = 516 + (d * 7 + lev) * 128
            cb[:, c0:c0 + 128] = m
    nf = 16
    inv = (10000.0 ** (-np.arange(nf, dtype=np.float32) / nf)).astype(np.float32)
    tpos = np.arange(TLAT)
    row = (tpos // 64).astype(np.float32)
    col = (tpos % 64).astype(np.float32)
    C = np.ones((128, TALL), np.float32)
    Sg = np.zeros((128, TALL), np.float32)
    for p in range(128):
        f = p % 64
        pos = row if f < 32 else col
        g = f % 32
        i = g % 16
        ang = pos * inv[i]
        C[p, 256:] = np.cos(ang)
        Sg[p, 256:] = (-np.sin(ang)) if g < 16 else np.sin(ang)
    return cf, cb, C, Sg


def _host_inputs(inputs):
    f = lambda a: np.ascontiguousarray(np.asarray(a, dtype=np.float32))
    L = DEPTH
    pp = np.zeros((L, 128, NPP), np.float32)
    bc = np.zeros((L, 128, NBC), np.float32)
    b_mod = f(inputs["b_mod"])
    norm_g = f(inputs["norm_g"])
    r_conv = f(inputs["r_conv"])
    for l in range(L):
        pp[l, :, 0:24] = b_mod[l].reshape(24, 128).T
        pp[l, :, 24:32] = norm_g[l].reshape(8, 128).T
        pp[l, :, 32] = np.tile(f(inputs["d_qnorm"])[l], 2)
        pp[l, :, 33] = np.tile(f(inputs["d_knorm"])[l], 2)
        for which in range(3):
            for tap in range(3):
                pp[l, :, 34 + which * 12 + tap * 4:34 + which * 12 + tap * 4 + 4] = r_conv[l, which, tap].reshape(4, 128).T
        for d_ in range(2):
            pp[l, :, 70 + d_ * 4:74 + d_ * 4] = f(inputs["r_w0"])[l, d_].reshape(4, 128).T
            pp[l, :, 78 + d_ * 4:82 + d_ * 4] = f(inputs["r_a0"])[l, d_].reshape(4, 128).T
        pp[l, :, 86:90] = f(inputs["r_kk"])[l].reshape(4, 128).T
        pp[l, :, 90:94] = f(inputs["r_ka"])[l].reshape(4, 128).T
        pp[l, :, 98:102] = f(inputs["r_rk"])[l].reshape(4, 128).T
        row = np.concatenate([f(inputs["a_ln_g"])[l], f(inputs["a_ln_b"])[l], f(inputs["a_bs"])[l].reshape(-1),
                              f(inputs["d_subln_g"])[l], f(inputs["r_ln_g"])[l], f(inputs["r_ln_b"])[l],
                              f(inputs["d_lam"])[l].reshape(-1)])
        bc[l] = np.broadcast_to(row[None, :], (128, NBC))
    cf, cb, C, Sg = _consts()
    shared = dict(pp=pp, bc=bc, cf=cf, cbf=cb, ropeC=C, ropeS=Sg,
                  w_mod=f(inputs["w_mod"]), w_in=f(inputs["w_in"]),
                  a_wsT=np.ascontiguousarray(np.transpose(f(inputs["a_ws"]), (0, 1, 3, 2))),
                  r_w2=f(inputs["r_w2"]), r_a2=f(inputs["r_a2"]), w_br=f(inputs["w_br"]), w_out=f(inputs["w_out"]))
    x = f(inputs["x"])
    c = f(inputs["c"])
    ctx = f(inputs["ctx"])
    c_ctx = f(inputs["c_ctx"])
    maps = []
    for b in range(4):
        xT0 = np.ascontiguousarray(np.concatenate([ctx[b], x[b]], axis=0).T)
        cc = np.stack([c[b].reshape(8, 128).T, c_ctx.reshape(8, 128).T], axis=-1)
        m = dict(shared)
        m["xT0"] = xT0
        m["cc"] = np.ascontiguousarray(cc)
        maps.append(m)
    return maps


_CACHE = {}


def kernel(**inputs):
    maps = _host_inputs(inputs)
    if "nc" not in _CACHE:
        _CACHE["nc"] = build()[0]
    nc = _CACHE["nc"]
    in_maps = [maps[i % 4] for i in range(8)]
    res = run_bass_kernel_spmd(nc, in_maps, core_ids=list(range(8)))
    out = np.stack([np.ascontiguousarray(res.results[b]["yT"].T) for b in range(4)], axis=0)
    return out.astype(np.float32)
```

```python
import numpy as np
import concourse.bass as bass
import concourse.mybir as mybir
from concourse.bass_utils import run_bass_kernel_spmd

F32 = mybir.dt.float32
BF16 = mybir.dt.bfloat16
AF = mybir.ActivationFunctionType
ALU = mybir.AluOpType
AX = mybir.AxisListType

D = 1024
NCTX = 256
TLAT = 4096
TALL = NCTX + TLAT
NTILE = TALL // 128
DEPTH = 4
INC = 8960
C0 = 0.6065306597126334
BLOCKS = [(0, 256)] + [(256 + 512 * i, 512) for i in range(8)]
NPP = 104
NBC = 2944
NCF = 1092
NCB = 2308
NDS = 32


class Buf:
    __slots__ = ("w", "r", "excl")

    def __init__(self):
        self.w = None
        self.r = {}
        self.excl = False


class TileT:
    def __init__(self, h, nb=1):
        self.h = h
        self.b = Buf()

    def __getitem__(self, k):
        return self.h[k]


class Sched:
    def __init__(self, nc):
        self.nc = nc
        self.sems = []
        self.eng = {}
        for name, h in (("pe", nc.tensor), ("act", nc.scalar), ("dve", nc.vector), ("pool", nc.gpsimd), ("sp", nc.sync)):
            sid = len(self.sems)
            self.sems.append(nc.alloc_semaphore("s_" + name))
            self.eng[name] = dict(h=h, sid=sid, n=0, waited={})
        self.dq = {}
        for q in ("sp", "pool", "act"):
            ids = []
            for i in range(NDS):
                ids.append(len(self.sems))
                self.sems.append(nc.alloc_semaphore("d%s%d" % (q, i)))
            self.dq[q] = dict(ids=ids, nxt=0)
        self.dcnt = {}
        self.nins = 0
        import os as _os2
        self.maxops = int(_os2.environ.get('MAXOPS', '1000000000'))
        self.k = 0

    def _wait(self, e, toks):
        E = self.eng[e]
        need = {}
        for t in toks:
            if t is None:
                continue
            sid, val = t
            if sid == E["sid"] and e == "pe":
                continue
            if need.get(sid, 0) < val:
                need[sid] = val
        for sid, val in need.items():
            if E["waited"].get(sid, 0) < val:
                E["h"].wait_ge(self.sems[sid], val)
                E["waited"][sid] = val
                self.nins += 1

    def _deps(self, reads, writes):
        toks = []
        for b in reads:
            toks.append(b.w)
        for b in writes:
            toks.append(b.w)
            toks.extend(b.r.items())
        return toks

    def _mark(self, tok, reads, writes):
        for b in reads:
            if b.r.get(tok[0], 0) < tok[1]:
                b.r[tok[0]] = tok[1]
        for b in writes:
            b.w = tok
            b.r = {}

    def op(self, e, emit, reads=(), writes=()):
        self.k += 1
        if self.k > self.maxops:
            return
        reads = [x.b if isinstance(x, TileT) else x for x in reads]
        writes = [x.b if isinstance(x, TileT) else x for x in writes]
        writes = writes + [b for b in reads if b.excl]
        reads = [b for b in reads if not b.excl]
        self._wait(e, self._deps(reads, writes))
        E = self.eng[e]
        E["n"] += 1
        ins = emit(E["h"])
        ins.then_inc(self.sems[E["sid"]], 1)
        self.nins += 1
        self._mark((E["sid"], E["n"]), reads, writes)

    def dma(self, q, out, in_, reads=(), writes=()):
        self.k += 1
        if self.k > self.maxops:
            return
        reads = [x.b if isinstance(x, TileT) else x for x in reads]
        writes = [x.b if isinstance(x, TileT) else x for x in writes]
        self._wait(q, self._deps(reads, writes))
        Q = self.dq[q]
        sid = Q["ids"][Q["nxt"] % NDS]
        Q["nxt"] += 1
        prev = self.dcnt.get(sid, 0)
        if prev > 0:
            self._wait(q, [(sid, 16 * prev)])
        self.dcnt[sid] = prev + 1
        self.eng[q]["h"].dma_start(out=out, in_=in_).then_inc(self.sems[sid], 16)
        self.nins += 1
        self._mark((sid, 16 * (prev + 1)), reads, writes)

    def barrier(self):
        toks = [(E["sid"], E["n"]) for E in self.eng.values() if E["n"] > 0]
        toks += [(sid, 16 * c) for sid, c in self.dcnt.items()]
        for e in self.eng:
            E = self.eng[e]
            for sid, val in toks:
                if sid == E["sid"]:
                    continue
                if E["waited"].get(sid, 0) < val:
                    E["h"].wait_ge(self.sems[sid], val)
                    E["waited"][sid] = val


class Alloc:
    def __init__(self, nc, S):
        self.nc = nc
        self.S = S
        self.stack = []
        self.cnt = 0

    def push(self):
        self.stack.append([])

    def pop(self):
        self.S.barrier()
        for cm in reversed(self.stack.pop()):
            cm.__exit__(None, None, None)

    def sb(self, shape, dt, name=None):
        self.cnt += 1
        cm = self.nc.sbuf_tensor("%s_%d" % (name or "t", self.cnt), list(shape), dt)
        h = cm.__enter__()
        self.stack[-1].append(cm)
        return TileT(h)


def build(NL=DEPTH, dbg=None, upto=9):
    import os as _os
    lo = int(_os.environ.get('SKIP_TO', '0'))
    RWCUT = float(_os.environ.get('RW_CUT', '99'))
    nc = bass.Bass("TRN2", target_bir_lowering=False)
    S = Sched(nc)
    A = Alloc(nc, S)

    def din(name, shape, dt=F32):
        return nc.dram_tensor(name, list(shape), dt, kind="ExternalInput").ap()

    def dscr(name, shape, dt=F32):
        return nc.dram_tensor(name, list(shape), dt, kind="Internal").ap()

    xT0 = din("xT0", [D, TALL])
    cc_d = din("cc", [128, 8, 2])
    pp_d = din("pp", [DEPTH, 128, NPP])
    bc_d = din("bc", [DEPTH, 128, NBC])
    cf_d = din("cf", [128, NCF])
    cb_d = din("cbf", [128, NCB])
    ropeC_d = din("ropeC", [128, TALL])
    ropeS_d = din("ropeS", [128, TALL])
    w_mod = din("w_mod", [DEPTH, D, 3 * D])
    w_in = din("w_in", [DEPTH, D, INC])
    a_wsT = din("a_wsT", [DEPTH, 4, 128, 128])
    r_w2 = din("r_w2", [DEPTH, 2, 64, 512])
    r_a2 = din("r_a2", [DEPTH, 2, 64, 512])
    w_br = din("w_br", [DEPTH, 3, 512, D])
    w_out = din("w_out", [DEPTH, D, D])
    yT = nc.dram_tensor("yT", [D, TLAT], F32, kind="ExternalOutput").ap()

    xTs = [dscr("xTa", [D, TALL]), dscr("xTb", [D, TALL])]
    qT = dscr("qT", [4, 128, TALL], BF16)
    kT = dscr("kT", [4, 128, TALL], BF16)
    Vd = dscr("Vd", [TALL, 512], BF16)
    krT = dscr("krT", [512, TALL])
    vrT = dscr("vrT", [512, TALL])
    rrT = dscr("rrT", [512, TALL])
    wlT = dscr("wlT", [128, TALL])
    alT = dscr("alT", [128, TALL])
    uzT = dscr("uzT", [512, TALL], BF16)
    vln = dscr("vln", [TALL, 512], BF16)
    zbd = dscr("zbd", [TALL, 512], BF16)
    zcd = dscr("zcd", [TALL, 512], BF16)
    gT = dscr("gT", [3 * D, TALL], BF16)
    yaT = dscr("yaT", [512, TALL], BF16)
    ybT = dscr("ybT", [512, TALL], BF16)
    ycT = dscr("ycT", [512, TALL], BF16)
    yfw = dscr("yfw", [TALL, 512])
    dbg_out = {}
    if dbg:
        for nm in dbg:
            src = dict(qT=qT, kT=kT, Vd=Vd, krT=krT, vrT=vrT, rrT=rrT, wlT=wlT, alT=alT, uzT=uzT, vln=vln, zbd=zbd,
                       zcd=zcd, gT=gT, yaT=yaT, ybT=ybT, ycT=ycT, yfw=yfw, xTa=xTs[0], xTb=xTs[1])[nm]
            dbg_out[nm] = (src, nc.dram_tensor("dbg_" + nm, list(src.shape), src.dtype, kind="ExternalOutput").ap())

    banks = [TileT(nc.alloc_psum_tensor("ps%d" % i, [128, 512], F32)) for i in range(8)]
    for b_k in banks:
        b_k.b.excl = True
    pstate = dict(i=0)

    def psum():
        b = banks[pstate["i"] % 6]
        pstate["i"] += 1
        return b

    def mm(out, lhsT, rhs, R, W, start=True, stop=True):
        S.op("pe", lambda h: h.matmul(out, lhsT, rhs, start=start, stop=stop), R, W)

    def tr(out, in_, ident, R, W):
        S.op("pe", lambda h: h.transpose(out, in_, ident), R, W)

    def act(out, in_, func, R, W, bias=None, scale=None, accum=None, eng="act"):
        kw = {}
        if bias is not None:
            kw["bias"] = bias
        if scale is not None:
            kw["scale"] = scale
        if accum is not None:
            kw["accum_out"] = accum
        S.op("act", lambda h: h.activation(out, in_, func, **kw), R, W)

    def tt(e, out, in0, in1, op, R, W):
        S.op(e, lambda h: h.tensor_tensor(out, in0, in1, op), R, W)

    def ts(e, out, in0, s1, s2, op0, op1, R, W):
        if s2 is None:
            S.op(e, lambda h: h.tensor_scalar(out, in0, s1, None, op0), R, W)
        else:
            S.op(e, lambda h: h.tensor_scalar(out, in0, s1, s2, op0, op1), R, W)

    def stt(out, in0, sc, in1, op0, op1, R, W):
        S.op("dve", lambda h: h.scalar_tensor_tensor(out, in0, sc, in1, op0, op1), R, W)

    def cp(e, out, in_, R, W):
        if e == "act":
            S.op("act", lambda h: h.copy(out, in_), R, W)
        else:
            S.op(e, lambda h: h.tensor_copy(out, in_), R, W)

    def recip(out, in_, R, W):
        S.op("dve", lambda h: h.reciprocal(out, in_), R, W)

    def memset(e, ap, val, W):
        S.op(e, lambda h: h.memset(ap, val), [], W)

    class Rot:
        def __init__(self, tiles):
            self.t = tiles
            self.i = 0

        def get(self):
            t = self.t[self.i % len(self.t)]
            self.i += 1
            return t

    A.push()
    cf = A.sb([128, NCF], F32, "cf")
    cb = A.sb([128, NCB], BF16, "cb")
    cc = A.sb([128, 8, 2], F32, "cc")
    cact = A.sb([128, 8, 2], F32, "cact")
    epsT = A.sb([128, 4], F32, "eps")
    onesf = A.sb([128, 128], F32, "onesf")
    S.dma("sp", cf[:], cf_d[:, :], [], [cf])
    S.dma("pool", cb[:], cb_d[:, :], [], [cb])
    S.dma("sp", cc[:], cc_d[:, :, :], [], [cc])
    act(cact[:], cc[:], AF.Silu, [cc], [cact])
    memset("dve", epsT[:, 0:1], 1e-6, [epsT])
    memset("dve", epsT[:, 1:2], 1e-5, [epsT])
    memset("dve", epsT[:, 2:3], 64e-5, [epsT])
    memset("dve", epsT[:, 3:4], 0.0, [epsT])
    memset("dve", onesf[:], 1.0, [onesf])
    ident_f = lambda ps, cs: cf.h[ps, cs]
    blk64_f = cf.h[:, 128:256]
    hsel_f = cf.h[:, 256:258]
    identfold = cf.h[:, 1028:1092]
    maskT = [cf.h[:, 260:516], cf.h[:, 516:772]]
    maskL = [cf.h[:, 772:900], cf.h[:, 900:1028]]
    ident_b = cb.h[:, 0:128]
    ones_b = cb.h[:, 128:256]
    blk64_b = cb.h[:, 256:384]
    rotT_b = cb.h[:, 384:512]
    hsel_b = cb.h[:, 512:514]

    def mLT(d, lev):
        c0 = 516 + (d * 7 + lev) * 128
        return cb.h[:, c0:c0 + 128]

    for l in range(NL):
        lam_init = 0.8 - 0.6 * float(np.exp(-0.3 * l))
        xcur = xT0 if l == 0 else xTs[(l - 1) % 2]
        xnext = xTs[l % 2]
        last = l == DEPTH - 1
        A.push()
        pp = A.sb([128, NPP], F32, "pp")
        bcs = A.sb([128, NBC], F32, "bc")
        mod = A.sb([128, 24, 2], F32, "mod")
        gs = A.sb([128, 8, 2], F32, "gs")
        lamt = A.sb([128, 8], F32, "lam")
        sublng = A.sb([128, 128], F32, "sublng")
        omka = A.sb([128, 4], F32, "omka")
        S.dma("sp", pp[:], pp_d[l, :, :], [], [pp])
        S.dma("sp", bcs[:], bc_d[l, :, :], [], [bcs])

        A.push()
        wm = Rot([A.sb([128, 8, 512], F32, "wm") for _ in range(2)])
        pb = psum()
        wmv = w_mod[l].rearrange("(kc p) n -> p kc n", p=128)
        for g6 in range(6):
            w = wm.get()
            S.dma("sp", w[:], wmv[:, :, g6 * 512:(g6 + 1) * 512], [], [w])
            for nt4 in range(4):
                nt = g6 * 4 + nt4
                for kc in range(8):
                    mm(pb[:, nt * 2:nt * 2 + 2], w[:, kc, nt4 * 128:(nt4 + 1) * 128], cact[:, kc, :], [w, cact], [pb],
                       start=(kc == 0), stop=(kc == 7))
        tt("dve", mod[:], pb[:, 0:48].rearrange("p (a b) -> p a b", b=2),
           pp[:, 0:24].unsqueeze(2).to_broadcast([128, 24, 2]), ALU.add, [pb, pp], [mod])
        ts("dve", gs[:], mod[:, 8:16, :], 1.0, None, ALU.add, None, [mod], [gs])
        tt("dve", gs[:], gs[:], pp[:, 24:32].unsqueeze(2).to_broadcast([128, 8, 2]), ALU.mult, [gs, pp], [gs])
        lamtmp = A.sb([128, 2, 64], F32, "lamtmp")
        dl = bcs[:, 2688:2944].rearrange("p (a b) -> p a b", b=64)
        tt("dve", lamtmp[:, 0, :], dl[:, 0, :], dl[:, 1, :], ALU.mult, [bcs], [lamtmp])
        tt("dve", lamtmp[:, 1, :], dl[:, 2, :], dl[:, 3, :], ALU.mult, [bcs], [lamtmp])
        S.op("dve", lambda h: h.tensor_reduce(lamt[:, 0:2], lamtmp[:], AX.X, ALU.add), [lamtmp], [lamt])
        act(lamt[:, 2:4], lamt[:, 0:2], AF.Exp, [lamt], [lamt])
        tt("dve", lamt[:, 4:5], lamt[:, 2:3], lamt[:, 3:4], ALU.subtract, [lamt], [lamt])
        ts("dve", lamt[:, 5:6], lamt[:, 4:5], lam_init, -1.0, ALU.add, ALU.mult, [lamt], [lamt])
        ts("dve", sublng[:], bcs[:, 1536:1664], 1.0 - lam_init, None, ALU.mult, None, [bcs], [sublng])
        ts("dve", omka[:], pp[:, 90:94], -1.0, 1.0, ALU.mult, ALU.add, [pp], [omka])
        neg_lam = lamt.h[:, 5:6]
        A.pop()

        A.push()
        hT = A.sb([128, 8, TALL], BF16, "hT")
        hTb = [Buf() for _ in BLOCKS]
        A.push()
        xin = Rot([A.sb([128, 8, 512], F32, "xin") for _ in range(2)])
        sqs = Rot([A.sb([128, 8, 512], BF16, "sq") for _ in range(2)])
        rstds = Rot([A.sb([128, 512], F32, "rstd") for _ in range(2)])
        tmps = Rot([A.sb([128, 512], F32, "ntmp") for _ in range(3)])
        xv = xcur.rearrange("(kc p) t -> p kc t", p=128)
        for bi, (t0, n) in enumerate(BLOCKS if lo <= 1 <= upto else []):
            j = 1 if bi == 0 else 0
            x_ = xin.get()
            S.dma("sp", x_[:, :, :n], xv[:, :, t0:t0 + n], [], [x_])
            sq = sqs.get()
            act(sq[:, :, :n], x_[:, :, :n], AF.Square, [x_], [sq])
            pb = psum()
            for kc in range(8):
                mm(pb[:, :n], ones_b, sq[:, kc, :n], [cb, sq], [pb], start=(kc == 0), stop=(kc == 7))
            rs = rstds.get()
            act(rs[:, :n], pb[:, :n], AF.Sqrt, [pb, epsT], [rs], bias=epsT[:, 0:1], scale=1.0 / D)
            recip(rs[:, :n], rs[:, :n], [rs], [rs])
            for kc in range(8):
                tm = tmps.get()
                stt(tm[:, :n], x_[:, kc, :n], gs[:, kc, j:j + 1], rs[:, :n], ALU.mult, ALU.mult, [x_, gs, rs], [tm])
                act(hT[:, kc, t0:t0 + n], tm[:, :n], AF.Identity, [tm, mod], [hTb[bi]], bias=mod[:, kc, j:j + 1])
        A.pop()

        A.push()
        wbufs = Rot([A.sb([128, 8, 512], BF16, "wbuf") for _ in range(4)])
        ostf = Rot([A.sb([128, 512], F32, "ostf") for _ in range(4)])
        ostb = Rot([A.sb([128, 512], BF16, "ostb") for _ in range(4)])
        ptmp = Rot([A.sb([128, 512], F32, "ptmp") for _ in range(6)])
        ptmpb = Rot([A.sb([128, 512], BF16, "ptmpb") for _ in range(4)])
        ropeCs = Rot([A.sb([128, 512], F32, "rC") for _ in range(2)])
        ropeSs = Rot([A.sb([128, 512], F32, "rS") for _ in range(2)])
        stat = Rot([A.sb([128, 16], F32, "stat") for _ in range(4)])
        winv = w_in[l].rearrange("(kc p) n -> p kc n", p=128)

        def loadw(c0, ncols=512):
            w = wbufs.get()
            S.dma("pool", w[:, :, :ncols], winv[:, :, c0:c0 + ncols], [], [w])
            return w

        def fm_mm(w, ct, bi, t0, n):
            pb = psum()
            for kc in range(8):
                mm(pb[:, :n], w[:, kc, ct * 128:(ct + 1) * 128], hT[:, kc, t0:t0 + n], [w, hTb[bi]], [pb],
                   start=(kc == 0), stop=(kc == 7))
            return pb

        def tm_mm(w, t0):
            pb = psum()
            bi = 0 if t0 < 256 else 1 + (t0 - 256) // 512
            for kc in range(8):
                mm(pb[:, :], hT[:, kc, t0:t0 + 128], w[:, kc, :], [w, hTb[bi]], [pb], start=(kc == 0), stop=(kc == 7))
            return pb

        def fm_raw(c0, ncols, dst, alt=[0]):
            w = loadw(c0, ncols)
            for bi, (t0, n) in enumerate(BLOCKS):
                for ct in range(ncols // 128):
                    pb = fm_mm(w, ct, bi, t0, n)
                    o = ostf.get()
                    alt[0] ^= 1
                    cp("act" if alt[0] else "dve", o[:, :n], pb[:, :n], [pb], [o])
                    S.dma("pool", dst[ct * 128:(ct + 1) * 128, t0:t0 + n], o[:, :n], [o], [])

        def fm_qk(c0, dst, gcol):
            w = loadw(c0)
            for bi, (t0, n) in enumerate(BLOCKS):
                rC = ropeCs.get()
                rS = ropeSs.get()
                S.dma("sp", rC[:, :n], ropeC_d[:, t0:t0 + n], [], [rC])
                S.dma("sp", rS[:, :n], ropeS_d[:, t0:t0 + n], [], [rS])
                for ct in range(4):
                    pb = fm_mm(w, ct, bi, t0, n)
                    sqb = ptmpb.get()
                    act(sqb[:, :n], pb[:, :n], AF.Square, [pb], [sqb])
                    p2 = psum()
                    mm(p2[:, :n], blk64_b, sqb[:, :n], [cb, sqb], [p2])
                    rs = ptmp.get()
                    act(rs[:, :n], p2[:, :n], AF.Sqrt, [p2, epsT], [rs], bias=epsT[:, 0:1], scale=1.0 / 64)
                    recip(rs[:, :n], rs[:, :n], [rs], [rs])
                    xn = ptmpb.get()
                    stt(xn[:, :n], pb[:, :n], pp[:, gcol:gcol + 1], rs[:, :n], ALU.mult, ALU.mult, [pb, pp, rs], [xn])
                    p3 = psum()
                    mm(p3[:, :n], rotT_b, xn[:, :n], [cb, xn], [p3])
                    t1 = ptmp.get()
                    tt("pool", t1[:, :n], xn[:, :n], rC[:, :n], ALU.mult, [xn, rC], [t1])
                    t2 = ptmp.get()
                    tt("dve", t2[:, :n], p3[:, :n], rS[:, :n], ALU.mult, [p3, rS], [t2])
                    o = ostb.get()
                    tt("pool", o[:, :n], t1[:, :n], t2[:, :n], ALU.add, [t1, t2], [o])
                    S.dma("pool", dst[ct, :, t0:t0 + n], o[:, :n], [o], [])

        def tm_group(c0, kind, dst):
            w = loadw(c0)
            for ti in range(NTILE):
                t0 = ti * 128
                pb = tm_mm(w, t0)
                o = ostb.get()
                if kind == "copy":
                    cp("act" if ti % 2 else "dve", o[:], pb[:], [pb], [o])
                elif kind == "silu":
                    act(o[:], pb[:], AF.Silu, [pb], [o])
                else:
                    tA = ptmp.get()
                    tB = ptmp.get()
                    g = ptmp.get()
                    act(tA[:], pb[:], AF.Square, [pb], [tA])
                    ts("dve", tA[:], tA[:], 0.044715, 1.0, ALU.mult, ALU.add, [tA], [tA])
                    tt("dve", tA[:], tA[:], pb[:], ALU.mult, [tA, pb], [tA])
                    act(tB[:], tA[:], AF.Sigmoid, [tA], [tB], scale=1.5957691216057308)
                    tt("dve", g[:], tB[:], pb[:], ALU.mult, [tB, pb], [g])
                    st = stat.get()
                    S.op("dve", lambda h: h.bn_stats(st[:, 0:6], g[:]), [g], [st])
                    S.op("dve", lambda h: h.bn_aggr(st[:, 8:10], st[:, 0:6]), [st], [st])
                    act(st[:, 10:11], st[:, 9:10], AF.Sqrt, [st, epsT], [st], bias=epsT[:, 1:2], scale=1.0)
                    recip(st[:, 11:12], st[:, 10:11], [st], [st])
                    ts("dve", g[:], g[:], st[:, 8:9], st[:, 11:12], ALU.subtract, ALU.mult, [g, st], [g])
                    tt("pool", g[:], g[:], bcs[:, 0:512], ALU.mult, [g, bcs], [g])
                    tt("pool", o[:], g[:], bcs[:, 512:1024], ALU.add, [g, bcs], [o])
                S.dma("pool", dst[t0:t0 + 128, :], o[:], [o], [])

        def fm_uz():
            wu = loadw(3328)
            wz = loadw(4352)
            for bi, (t0, n) in enumerate(BLOCKS):
                for ct in range(4):
                    pu = fm_mm(wu, ct, bi, t0, n)
                    pz = fm_mm(wz, ct, bi, t0, n)
                    tA = ptmp.get()
                    tB = ptmp.get()
                    g = ptmp.get()
                    act(tA[:, :n], pu[:, :n], AF.Square, [pu], [tA])
                    ts("dve", tA[:, :n], tA[:, :n], 0.044715, 1.0, ALU.mult, ALU.add, [tA], [tA])
                    tt("dve", tA[:, :n], tA[:, :n], pu[:, :n], ALU.mult, [tA, pu], [tA])
                    act(tB[:, :n], tA[:, :n], AF.Sigmoid, [tA], [tB], scale=1.5957691216057308)
                    tt("dve", g[:, :n], tB[:, :n], pu[:, :n], ALU.mult, [tB, pu], [g])
                    sz = ptmp.get()
                    act(sz[:, :n], pz[:, :n], AF.Silu, [pz], [sz])
                    o = ostb.get()
                    tt("pool", o[:, :n], g[:, :n], sz[:, :n], ALU.mult, [g, sz], [o])
                    S.dma("pool", uzT[ct * 128:(ct + 1) * 128, t0:t0 + n], o[:, :n], [o], [])

        def fm_gl(c0, gi):
            w = loadw(c0)
            for bi, (t0, n) in enumerate(BLOCKS):
                for ct in range(4):
                    pb = fm_mm(w, ct, bi, t0, n)
                    o = ostb.get()
                    act(o[:, :n], pb[:, :n], AF.Sigmoid, [pb], [o])
                    r0 = gi * 512 + ct * 128
                    S.dma("pool", gT[r0:r0 + 128, t0:t0 + n], o[:, :n], [o], [])

        if lo <= 2 <= upto:
            fm_qk(0, kT, 33)
            tm_group(512, "copy", Vd)
            fm_raw(1024, 512, krT)
            fm_raw(1536, 512, vrT)
            fm_raw(2048, 128, wlT)
            fm_raw(2176, 128, alT)
            fm_qk(2304, qT, 32)
            fm_raw(2816, 512, rrT)
            fm_uz()
            tm_group(3840, "va", vln)
            tm_group(4864, "silu", zbd)
            tm_group(5376, "silu", zcd)
            for gi in range(6):
                fm_gl(5888 + gi * 512, gi)
        A.pop()
        A.pop()

        A.push()
        wsT = A.sb([128, 4, 128], BF16, "wsT")
        S.dma("pool", wsT[:], a_wsT[l].rearrange("g q p -> q g p"), [], [wsT])
        vlb = Rot([A.sb([128, 4, 512], BF16, "vlb") for _ in range(2)])
        uzb = Rot([A.sb([128, 4, 512], BF16, "uzb") for _ in range(2)])
        gtmp = Rot([A.sb([128, 512], F32, "gtmp") for _ in range(3)])
        gost = Rot([A.sb([128, 512], BF16, "gost") for _ in range(3)])
        vlnv = vln.rearrange("(t p) c -> p t c", p=128)
        uzv = uzT.rearrange("(ct p) t -> p ct t", p=128)
        for bi, (t0, n) in enumerate(BLOCKS if lo <= 3 <= upto else []):
            nt = n // 128
            vb_ = vlb.get()
            ub_ = uzb.get()
            S.dma("sp", vb_[:, :nt, :], vlnv[:, t0 // 128:t0 // 128 + nt, :], [], [vb_])
            S.dma("sp", ub_[:, :, :n], uzv[:, :, t0:t0 + n], [], [ub_])
            for g in range(4):
                pb = psum()
                for ti in range(nt):
                    mm(pb[:, ti * 128:(ti + 1) * 128], vb_[:, ti, g * 128:(g + 1) * 128], wsT[:, g, :], [vb_, wsT], [pb])
                tg = gtmp.get()
                tt("dve", tg[:, :n].rearrange("p (a b) -> p a b", b=128), pb[:, :n].rearrange("p (a b) -> p a b", b=128),
                   bcs[:, 1024 + g * 128:1024 + (g + 1) * 128].unsqueeze(1).to_broadcast([128, nt, 128]), ALU.add,
                   [pb, bcs], [tg])
                o = gost.get()
                tt("pool", o[:, :n], tg[:, :n], ub_[:, g, :n], ALU.mult, [tg, ub_], [o])
                S.dma("pool", yaT[g * 128:(g + 1) * 128, t0:t0 + n], o[:, :n], [o], [])
        A.pop()

        A.push()
        kTh = Rot([A.sb([128, TALL], BF16, "kTh") for _ in range(2)])
        Vh = Rot([A.sb([128, NTILE, 132], BF16, "Vh") for _ in range(2)])
        for v_ in Vh.t:
            memset("pool", v_[:, :, 128:132], 1.0, [v_])
        PTs = Rot([A.sb([128, NTILE, 512], BF16, "PT") for _ in range(3)])
        qTb = Rot([A.sb([128, 512], BF16, "qTb") for _ in range(2)])
        zbt = Rot([A.sb([128, 4, 128], BF16, "zbt") for _ in range(2)])
        ao = Rot([A.sb([128, 128], F32, "ao") for _ in range(6)])
        ast = Rot([A.sb([128, 8], F32, "ast") for _ in range(4)])
        ayb = Rot([A.sb([128, 128], BF16, "ayb") for _ in range(3)])
        aost = Rot([A.sb([128, 512], BF16, "aost") for _ in range(2)])
        Vv = Vd.rearrange("(t p) c -> p t c", p=128)
        zbv = zbd.rearrange("(t p) c -> p t c", p=128)
        for h_ in range(4 if lo <= 4 <= upto else 0):
            kh = kTh.get()
            vh = Vh.get()
            S.dma("sp", kh[:], kT[h_, :, :], [], [kh])
            for q0 in range(0, NTILE, 9):
                q1 = min(NTILE, q0 + 9)
                S.dma("sp", vh[:, q0:q1, 0:128], Vv[:, q0:q1, h_ * 128:(h_ + 1) * 128], [], [vh])
            for bi, (t0, n) in enumerate(BLOCKS):
                nt = n // 128
                kts = [0, 1] if bi == 0 else list(range(NTILE))
                qb = qTb.get()
                S.dma("sp", qb[:, :n], qT[h_, :, t0:t0 + n], [], [qb])
                zt = zbt.get()
                S.dma("sp", zt[:, :nt, :], zbv[:, t0 // 128:t0 // 128 + nt, h_ * 128:(h_ + 1) * 128], [], [zt])
                PT = [PTs.get(), PTs.get()]
                for j in range(2):
                    js = slice(j * 64, (j + 1) * 64)
                    for kt in kts:
                        pb = psum()
                        mm(pb[:, :n], kh[js, kt * 128:(kt + 1) * 128], qb[js, :n], [kh, qb], [pb])
                        act(PT[j][:, kt, :n], pb[:, :n], AF.Exp, [pb], [PT[j]], scale=0.125)
                ob = aost.get()
                for ti in range(nt):
                    oj = []
                    for j in range(2):
                        pb = psum()
                        for ki, kt in enumerate(kts):
                            mm(pb[:, 0:129], PT[j][:, kt, ti * 128:(ti + 1) * 128], vh[:, kt, 0:129], [PT[j], vh], [pb],
                               start=(ki == 0), stop=(ki == len(kts) - 1))
                        st = ast.get()
                        recip(st[:, 0:1], pb[:, 128:129], [pb], [st])
                        o_ = ao.get()
                        ts("dve", o_[:], pb[:, 0:128], st[:, 0:1], None, ALU.mult, None, [pb, st], [o_])
                        oj.append(o_)
                    od = ao.get()
                    stt(od[:], oj[1][:], neg_lam, oj[0][:], ALU.mult, ALU.add, [oj[1], oj[0], lamt], [od])
                    st = ast.get()
                    junk = ao.get()
                    act(junk[:], od[:], AF.Square, [od], [junk, st], accum=st[:, 0:1])
                    act(st[:, 1:2], st[:, 0:1], AF.Sqrt, [st, epsT], [st], bias=epsT[:, 0:1], scale=1.0 / 128)
                    recip(st[:, 2:3], st[:, 1:2], [st], [st])
                    stt(od[:], od[:], st[:, 2:3], sublng[:], ALU.mult, ALU.mult, [od, st, sublng], [od])
                    yb_ = ayb.get()
                    tt("dve", yb_[:], od[:], zt[:, ti, :], ALU.mult, [od, zt], [yb_])
                    pb = psum()
                    pbb = pb.h[:].bitcast(BF16)
                    tr(pbb[:, 0:128], yb_[:], ident_b, [yb_, cb], [pb])
                    cp("act", ob[:, ti * 128:(ti + 1) * 128], pbb[:, 0:128], [pb], [ob])
                S.dma("pool", ybT[h_ * 128:(h_ + 1) * 128, t0:t0 + n], ob[:, :n], [ob], [])
        A.pop()

        A.push()
        w2a2 = [A.sb([128, 512], BF16, "w2a2") for _ in range(2)]
        for d_ in range(2):
            S.dma("pool", w2a2[d_][0:64, :], r_w2[l, d_, :, :], [], [w2a2[d_]])
            S.dma("pool", w2a2[d_][64:128, :], r_a2[l, d_, :, :], [], [w2a2[d_]])
        STb = [A.sb([128, 4, 64], BF16, "STb") for _ in range(2)]
        STB = [[[Buf() for _ in range(2)] for _ in range(4)] for _ in range(2)]
        NW = 2

        def wsf(name, shape=(128, 128), dt=F32, k=NW):
            return Rot([A.sb(list(shape), dt, name) for _ in range(k)])

        W_wla = wsf("wla")
        W_lab = wsf("lab", dt=BF16)
        PT = [dict(krc=A.sb([128, 130], F32, "krc"), vrc=A.sb([128, 130], F32, "vrc"), rrc=A.sb([128, 130], F32, "rrc"),
                   **{nm: A.sb([128, 128], F32, nm) for nm in ("k", "v", "r", "sg", "a", "Gs", "G2", "Gx", "eG", "enG", "eGx",
                                                              "eGC", "kk", "sq", "b", "kd")},
                   **{nm: A.sb([128, 128], BF16, nm) for nm in ("sqb", "BhT", "KhT", "vb", "rkk")}) for ct in range(4)]
        PS = [[dict(AR=A.sb([128, 2, 128], BF16, "AR"), Bt=A.sb([128, 128], BF16, "Bt"), Kt=A.sb([128, 128], BF16, "Kt"),
                    TK=A.sb([128, 3, 128], BF16, "TK"), Rtf=A.sb([128, 128], F32, "Rtf"), sm=A.sb([128, 8], F32, "sm"))
               for ct in range(4)] for par in range(2)]
        VTs = [A.sb([128, 4, 128], BF16, "VT") for par in range(2)]
        betas = [A.sb([128, 8], F32, "beta") for par in range(2)]
        SL = [dict(M1=A.sb([128, 2, 256], BF16, "M1"), M2=A.sb([128, 2, 256], BF16, "M2"), X=A.sb([128, 2, 128], BF16, "X"),
                   T=[A.sb([128, 2, 128], BF16, "T") for _ in range(2)], TT=[A.sb([128, 2, 128], BF16, "TT") for _ in range(2)],
                   Lm=[A.sb([128, 2, 128], BF16, "Lm") for _ in range(8)], W=[A.sb([128, 2, 128], BF16, "W") for _ in range(2)],
                   UA=A.sb([128, 2, 128], BF16, "UA"), UP=A.sb([128, 2, 128], BF16, "UP"), Phi=A.sb([128, 64], BF16, "Phi"),
                   Psi=A.sb([128, 64], F32, "Psi"), QT=A.sb([128, 128], BF16, "QT")) for _ in range(4)]
        W_yo = wsf("yo", (128, 512), k=2)
        W_y = wsf("y", (128, 512), k=2)
        W_ysq = wsf("ysq", (128, 512), k=2)
        W_st = wsf("rst", (128, 48), k=2)
        W_zc = wsf("zc", (128, 512), BF16, 2)
        W_ycz = wsf("ycz", (128, 512), BF16, 2)
        W_yco = wsf("yco", (128, 4, 128), BF16, 2)
        zcv = zcd.rearrange("(t p) c -> p t c", p=128)
        ycTv = ycT.rearrange("(ct p) t -> p ct t", p=128)

        def conv(dst, src, which, ct):
            base = 34 + which * 12 + ct
            ts("dve", dst[:], src[:, 0:128], pp[:, base:base + 1], None, ALU.mult, None, [src, pp], [dst])
            stt(dst[:], src[:, 1:129], pp[:, base + 4:base + 5], dst[:], ALU.mult, ALU.add, [src, pp, dst], [dst])
            stt(dst[:], src[:, 2:130], pp[:, base + 8:base + 9], dst[:], ALU.mult, ALU.add, [src, pp, dst], [dst])

        def load_halo(tile, src, ct, t0):
            left = not (t0 == 0 or t0 == 256)
            right = not (t0 + 128 == 256 or t0 + 128 == TALL)
            lo = t0 - (1 if left else 0)
            hi = t0 + 128 + (1 if right else 0)
            c_lo = 1 - (t0 - lo)
            if not left:
                memset("pool", tile[:, 0:1], 0.0, [tile])
            if not right:
                memset("pool", tile[:, 129:130], 0.0, [tile])
            S.dma("sp", tile[:, c_lo:c_lo + (hi - lo)], src[ct * 128:(ct + 1) * 128, lo:hi], [], [tile])

        def prep_chunk(d_, c_):
            t0 = c_ * 128
            wla = W_wla.get()
            S.dma("sp", wla[0:64, :], wlT[d_ * 64:(d_ + 1) * 64, t0:t0 + 128], [], [wla])
            S.dma("sp", wla[64:128, :], alT[d_ * 64:(d_ + 1) * 64, t0:t0 + 128], [], [wla])
            lab = W_lab.get()
            act(lab[0:64, :], wla[0:64, :], AF.Tanh, [wla], [lab])
            cp("dve", lab[64:128, :], wla[64:128, :], [wla], [lab])
            return lab

        def prep_gen(d_, c_, par, ct, lab):
            t0 = c_ * 128
            VT = VTs[par]
            beta = betas[par]
            P = PS[par][ct]
            AR, Bt, Kt, TK, Rtf, sm = P["AR"], P["Bt"], P["Kt"], P["TK"], P["Rtf"], P["sm"]
            W = PT[ct]
            krc, vrc, rrc = W["krc"], W["vrc"], W["rrc"]
            load_halo(krc, krT, ct, t0)
            load_halo(vrc, vrT, ct, t0)
            load_halo(rrc, rrT, ct, t0)
            pw = psum()
            mm(pw[:, 0:128], w2a2[d_][0:64, ct * 128:(ct + 1) * 128], lab[0:64, :], [w2a2[d_], lab], [pw])
            sg = W["sg"]
            a_ = W["a"]
            act(sg[:], pw[:, 0:128], AF.Sigmoid, [pw, pp], [sg], bias=pp[:, 70 + d_ * 4 + ct:71 + d_ * 4 + ct])
            yield
            pw2 = psum()
            mm(pw2[:, 128:256], w2a2[d_][64:128, ct * 128:(ct + 1) * 128], lab[64:128, :], [w2a2[d_], lab], [pw2])
            act(a_[:], pw2[:, 128:256], AF.Sigmoid, [pw2, pp], [a_], bias=pp[:, 78 + d_ * 4 + ct:79 + d_ * 4 + ct])
            yield
            k_, v_, r_ = W["k"], W["v"], W["r"]
            for (dst, src, which) in ((r_, rrc, 0), (k_, krc, 1), (v_, vrc, 2)):
                base = 34 + which * 12 + ct
                ts("dve", dst[:], src[:, 0:128], pp[:, base:base + 1], None, ALU.mult, None, [src, pp], [dst])
                yield
                stt(dst[:], src[:, 1:129], pp[:, base + 4:base + 5], dst[:], ALU.mult, ALU.add, [src, pp, dst], [dst])
                yield
                stt(dst[:], src[:, 2:130], pp[:, base + 8:base + 9], dst[:], ALU.mult, ALU.add, [src, pp, dst], [dst])
                yield
            Gs = W["Gs"]
            S.op("dve", lambda h: h.tensor_tensor_scan(Gs[:], onesf[:], sg[:], 0.0, ALU.mult, ALU.add), [onesf, sg], [Gs])
            yield
            if d_ == 0:
                G = Gs
            else:
                G = W["G2"]
                stt(G[:], Gs[:], -1.0, sg[:], ALU.mult, ALU.add, [Gs, sg], [G])
                yield
                ts("dve", G[:], G[:], Gs[:, 127:128], None, ALU.add, None, [G, Gs], [G])
            ts("dve", sm[:, 0:1], Gs[:, 127:128], -C0, None, ALU.mult, None, [Gs], [sm])
            yield
            Gx = W["Gx"]
            tt("pool", Gx[:], G[:], sg[:], ALU.subtract, [G, sg], [Gx])
            eG, enG, eGx, eGC = W["eG"], W["enG"], W["eGx"], W["eGC"]
            act(eG[:], G[:], AF.Exp, [G], [eG], scale=-C0)
            yield
            act(enG[:], G[:], AF.Exp, [G], [enG], scale=C0)
            yield
            act(eGx[:], Gx[:], AF.Exp, [Gx], [eGx], scale=-C0)
            yield
            act(eGC[:], G[:], AF.Exp, [G, sm], [eGC], scale=C0, bias=sm[:, 0:1])
            act(sm[:, 1:2], sm[:, 0:1], AF.Exp, [sm], [sm])
            yield
            kk, sq, sqb = W["kk"], W["sq"], W["sqb"]
            ts("dve", kk[:], k_[:], pp[:, 86 + ct:87 + ct], None, ALU.mult, None, [k_, pp], [kk])
            yield
            act(sqb[:], kk[:], AF.Square, [kk], [sqb])
            yield
            pn = psum()
            mm(pn[:, 0:128], blk64_b, sqb[:], [cb, sqb], [pn])
            act(sq[:], pn[:, 0:128], AF.Sqrt, [pn], [sq])
            yield
            ts("dve", sq[:], sq[:], 1e-12, None, ALU.max, None, [sq], [sq])
            yield
            recip(sq[:], sq[:], [sq], [sq])
            yield
            tt("dve", kk[:], kk[:], sq[:], ALU.mult, [kk, sq], [kk])
            yield
            b_, kd = W["b"], W["kd"]
            tt("pool", b_[:], kk[:], a_[:], ALU.mult, [kk, a_], [b_])
            ts("dve", kd[:], a_[:], pp[:, 90 + ct:91 + ct], omka[:, ct:ct + 1], ALU.mult, ALU.add, [a_, pp, omka], [kd])
            yield
            tt("pool", kd[:], kd[:], k_[:], ALU.mult, [kd, k_], [kd])
            stt(AR[:, 0, :], kk[:], -1.0, eGx[:], ALU.mult, ALU.mult, [kk, eGx], [AR])
            yield
            tt("dve", Rtf[:], r_[:], eG[:], ALU.mult, [r_, eG], [Rtf])
            yield
            cp("act", AR[:, 1, :], Rtf[:], [Rtf], [AR])
            BhT, KhT, vb = W["BhT"], W["KhT"], W["vb"]
            tt("pool", Bt[:], b_[:], enG[:], ALU.mult, [b_, enG], [Bt])
            tt("dve", Kt[:], kd[:], enG[:], ALU.mult, [kd, enG], [Kt])
            yield
            tt("pool", BhT[:], b_[:], eGC[:], ALU.mult, [b_, eGC], [BhT])
            tt("dve", KhT[:], kd[:], eGC[:], ALU.mult, [kd, eGC], [KhT])
            cp("act", vb[:], v_[:], [v_], [vb])
            yield
            ptr = psum()
            ptb = ptr.h[:].bitcast(BF16)
            tr(ptb[:, 0:128], AR[:, 0, :], ident_b, [AR, cb], [ptr])
            tr(ptb[:, 128:256], BhT[:], ident_b, [BhT, cb], [ptr])
            tr(ptb[:, 256:384], KhT[:], ident_b, [KhT, cb], [ptr])
            tr(ptb[:, 384:512], vb[:], ident_b, [vb, cb], [ptr])
            cp("dve", TK[:].rearrange("p a b -> p (a b)"), ptb[:, 0:384], [ptr], [TK])
            cp("dve", VT[:, ct, :], ptb[:, 384:512], [ptr], [VT])
            yield
            if d_ == 1:
                rkk = W["rkk"]
                stt(rkk[:], r_[:], pp[:, 98 + ct:99 + ct], k_[:], ALU.mult, ALU.mult, [r_, pp, k_], [rkk])
                pbt = psum()
                mm(pbt[:, 0:2], rkk[:], hsel_b, [rkk, cb], [pbt])
                cp("act", beta[:, ct * 2:ct * 2 + 2], pbt[:, 0:2], [pbt], [beta])
                yield

        def pair_gen(d_, par, ct, ybk):
            P = PS[par][ct]
            AR, Bt, Kt, TK, Rtf, sm = P["AR"], P["Bt"], P["Kt"], P["TK"], P["Rtf"], P["sm"]
            VT = VTs[par]
            Q = SL[ct]
            M1, M2, X = Q["M1"], Q["M2"], Q["X"]
            ARf = AR[:].rearrange("p a b -> p (a b)")
            HP = [slice(0, 64), slice(64, 128)]
            v3 = lambda ap, e: ap.rearrange("p (h e) -> p h e", e=e)
            bc2 = lambda ap, e: ap.unsqueeze(1).to_broadcast([128, 2, e])
            for hh in range(2):
                hp = HP[hh]
                p1 = psum()
                mm(p1[:, 0:256], Bt[hp, :], ARf[hp, :], [Bt, AR], [p1])
                tt("dve", M1[:, hh, :], p1[:, 0:256], maskT[d_], ALU.mult, [p1, cf], [M1])
                yield
            for hh in range(2):
                hp = HP[hh]
                p2 = psum()
                mm(p2[:, 0:256], Kt[hp, :], ARf[hp, :], [Kt, AR], [p2])
                tt("dve", M2[:, hh, :], p2[:, 0:256], maskT[d_], ALU.mult, [p2, cf], [M2])
                yield
            for hh in range(2):
                hp = HP[hh]
                p3 = psum()
                mm(p3[:, 0:128], AR[hp, 0, :], Bt[hp, :], [AR, Bt], [p3])
                tt("dve", X[:, hh, :], p3[:, 0:128], maskL[d_], ALU.mult, [p3, cf], [X])
                yield
            LTap = M1[:, :, 0:128]
            T_ = Q["T"][0]
            TT = Q["TT"][0]
            Lm = Q["Lm"]
            tt("pool", Lm[6][:], LTap, bc2(mLT(d_, 0), 128), ALU.mult, [M1, cb], [Lm[6]])
            tt("pool", TT[:], Lm[6][:], bc2(ident_b, 128), ALU.add, [Lm[6], cb], [TT])
            tt("pool", Lm[7][:], X[:], bc2(mLT(1 - d_, 0), 128), ALU.mult, [X, cb], [Lm[7]])
            tt("pool", T_[:], Lm[7][:], bc2(ident_b, 128), ALU.add, [Lm[7], cb], [T_])
            yield
            for lev in range(1, 7):
                tt("pool", Lm[lev - 1][:], LTap, bc2(mLT(d_, lev), 128), ALU.mult, [M1, cb], [Lm[lev - 1]])
                if lev % 2 == 0:
                    yield
            for lev in range(1, 7):
                LmT = Lm[lev - 1]
                pa = psum()
                for hh in range(2):
                    mm(pa[:, hh * 128:(hh + 1) * 128], LmT[:, hh, :], T_[:, hh, :], [LmT, T_], [pa])
                Wt = Q["W"][lev % 2]
                tt("dve", Wt[:], v3(pa[:, 0:256], 128), bc2(ident_b, 128), ALU.add, [pa, cb], [Wt])
                yield
                if lev < 6:
                    pT = psum()
                    for hh in range(2):
                        mm(pT[:, hh * 128:(hh + 1) * 128], TT[:, hh, :], Wt[:, hh, :], [TT, Wt], [pT])
                    Tn = Q["T"][lev % 2]
                    cp("act", Tn[:].rearrange("p h e -> p (h e)"), pT[:, 0:256], [pT], [Tn])
                pTT = psum()
                for hh in range(2):
                    mm(pTT[:, hh * 128:(hh + 1) * 128], Wt[:, hh, :], TT[:, hh, :], [Wt, TT], [pTT])
                TTn = Q["TT"][lev % 2]
                if lev < 6:
                    cp("dve", TTn[:].rearrange("p h e -> p (h e)"), pTT[:, 0:256], [pTT], [TTn])
                else:
                    cp("act", TTn[:].rearrange("p h e -> p (h e)"), pTT[:, 0:256], [pTT], [TTn])
                TT = TTn
                if lev < 6:
                    T_ = Tn
                yield
            UA, UP, Phi, Psi, QT = Q["UA"], Q["UP"], Q["Phi"], Q["Psi"], Q["QT"]
            pwq = psum()
            for hh in range(2):
                mm(pwq[:, hh * 64:(hh + 1) * 64], M2[:, hh, 0:128], VT[:, ct, HP[hh]], [M2, VT], [pwq])
            cp("act", UA[:, :, 0:64], v3(pwq[:, 0:128], 64), [pwq], [UA])
            cp("pool", UA[:, :, 64:128], v3(TK[:, 0, :], 64), [TK], [UA])
            yield
            pu = psum()
            for hh in range(2):
                mm(pu[:, hh * 128:(hh + 1) * 128], TT[:, hh, :], UA[:, hh, :], [TT, UA], [pu])
            cp("act", UP[:].rearrange("p h e -> p (h e)"), pu[:, 0:256], [pu], [UP])
            yield
            pf = psum()
            for hh in range(2):
                hp = HP[hh]
                mm(pf[hp, 0:64], UP[:, hh, 64:128], TK[:, 1, hp], [UP, TK], [pf])
                mm(pf[hp, 64:128], TK[:, 1, hp], UP[:, hh, 0:64], [TK, UP], [pf], start=True, stop=False)
                mm(pf[hp, 64:128], TK[:, 2, hp], VT[:, ct, hp], [TK, VT], [pf], start=False, stop=True)
                mm(pf[hp, 128:256], UP[:, hh, 64:128], M1[:, hh, 128:256], [UP, M1], [pf])
            stt(Phi[:], identfold, sm[:, 1:2], pf[:, 0:64], ALU.mult, ALU.add, [cf, sm, pf], [Phi])
            cp("dve", Psi[:], pf[:, 64:128], [pf], [Psi])
            tt("dve", QT[:], pf[:, 128:256], Rtf[:], ALU.add, [pf, Rtf], [QT])
            yield
            for hh in range(2):
                hp = HP[hh]
                yc0 = ct * 128 + hh * 64
                mm(ybk[:, yc0:yc0 + 64], M1[:, hh, 128:256], UP[:, hh, 0:64], [M1, UP], [ybk], start=True, stop=False)
                mm(ybk[:, yc0:yc0 + 64], M2[:, hh, 128:256], VT[:, ct, hp], [M2, VT], [ybk], start=False, stop=False)
                sbuf_ = STB[d_][ct][hh]
                mm(ybk[:, yc0:yc0 + 64], QT[hp, :], STb[d_][hp, ct, :], [QT, sbuf_], [ybk], start=False, stop=True)
                pS = psum()
                mm(pS[hp, 0:64], Phi[hp, :], STb[d_][hp, ct, :], [Phi, sbuf_], [pS])
                tt("dve", STb[d_][hp, ct, :], pS[hp, 0:64], Psi[hp, :], ALU.add, [pS, Psi], [sbuf_])
                yield

        def readout(d_, c_, par, ybk):
            t0 = c_ * 128
            VT = VTs[par]
            beta = betas[par]
            if d_ == 0:
                yo = W_yo.get()
                cp("act", yo[:], ybk[:], [ybk], [yo])
                S.dma("pool", yfw[t0:t0 + 128, :], yo[:], [yo], [])
                return
            yo = W_yo.get()
            S.dma("sp", yo[:], yfw[t0:t0 + 128, :], [], [yo])
            zc_ = W_zc.get()
            S.dma("sp", zc_[:], zcv[:, c_, :], [], [zc_])
            y = W_y.get()
            tt("dve", y[:], ybk[:], yo[:], ALU.add, [ybk, yo], [y])
            y3 = y[:].rearrange("p (a b) -> p a b", b=64)
            st = W_st.get()
            S.op("dve", lambda h: h.tensor_reduce(st[:, 0:8], y3, AX.X, ALU.add), [y], [st])
            ysq = W_ysq.get()
            act(ysq[:], y[:], AF.Square, [y], [ysq])
            S.op("dve", lambda h: h.tensor_reduce(st[:, 8:16], ysq[:].rearrange("p (a b) -> p a b", b=64), AX.X, ALU.add),
                 [ysq], [st])
            ts("dve", st[:, 0:16], st[:, 0:16], 1.0 / 64, None, ALU.mult, None, [st], [st])
            tt("dve", st[:, 16:24], st[:, 0:8], st[:, 0:8], ALU.mult, [st], [st])
            tt("dve", st[:, 24:32], st[:, 8:16], st[:, 16:24], ALU.subtract, [st], [st])
            act(st[:, 32:40], st[:, 24:32], AF.Sqrt, [st, epsT], [st], bias=epsT[:, 2:3], scale=1.0)
            recip(st[:, 40:48], st[:, 32:40], [st], [st])
            tt("dve", y3, y3, st[:, 0:8].unsqueeze(2).to_broadcast([128, 8, 64]), ALU.subtract, [y, st], [y])
            tt("dve", y3, y3, st[:, 40:48].unsqueeze(2).to_broadcast([128, 8, 64]), ALU.mult, [y, st], [y])
            tt("pool", y[:], y[:], bcs[:, 1664:2176], ALU.mult, [y, bcs], [y])
            tt("pool", y[:], y[:], bcs[:, 2176:2688], ALU.add, [y, bcs], [y])
            bon = ysq
            tt("dve", bon[:].rearrange("p (a b) -> p a b", b=64), VT[:].rearrange("p c (h e) -> p (c h) e", e=64),
               beta[:, 0:8].unsqueeze(2).to_broadcast([128, 8, 64]), ALU.mult, [VT, beta], [bon])
            tt("pool", y[:], y[:], bon[:], ALU.add, [y, bon], [y])
            ycz = W_ycz.get()
            tt("dve", ycz[:], y[:], zc_[:], ALU.mult, [y, zc_], [ycz])
            ptr = psum()
            ptb = ptr.h[:].bitcast(BF16)
            for ct in range(4):
                tr(ptb[:, ct * 128:(ct + 1) * 128], ycz[:, ct * 128:(ct + 1) * 128], ident_b, [ycz, cb], [ptr])
            yco = W_yco.get()
            cp("act", yco[:].rearrange("p a b -> p (a b)"), ptb[:, 0:512], [ptr], [yco])
            S.dma("pool", ycTv[:, :, t0:t0 + 128], yco[:], [yco], [])

        def lockstep(gens):
            while gens:
                nxt = []
                for g in gens:
                    try:
                        next(g)
                        nxt.append(g)
                    except StopIteration:
                        pass
                gens = nxt

        def preps(d_, c_, par):
            lab = prep_chunk(d_, c_)
            return [prep_gen(d_, c_, par, ct, lab) for ct in range(4)]

        ybank_i = 0
        for d_ in range(2 if upto >= 5 else 0):
            memset("dve", STb[d_][:], 0.0, [STb[d_]] + [STB[d_][c4][h2] for c4 in range(4) for h2 in range(2)])
            order = [0, 1] + list(range(2, NTILE)) if d_ == 0 else [1, 0] + list(range(NTILE - 1, 1, -1))
            order = order[:int(_os.environ.get('RW_CHUNKS', '99'))]
            lockstep(preps(d_, order[0], 0))
            for i, c_ in enumerate(order):
                par = i % 2
                ybk = banks[6 + (ybank_i % 2)]
                ybank_i += 1
                gens = [pair_gen(d_, par, ct, ybk) for ct in range(4)]
                if i + 1 < len(order):
                    gens = gens + preps(d_, order[i + 1], 1 - par)
                lockstep(gens)
                readout(d_, c_, par, ybk)
            S.barrier()
        A.pop()

        A.push()
        wbr = A.sb([128, 3, 4, D], BF16, "wbr")
        wo = A.sb([128, 8, D], BF16, "wo")
        for i in range(3):
            S.dma("pool", wbr[:, i, :, :], w_br[l, i].rearrange("(ct p) d -> p ct d", p=128), [], [wbr])
        S.dma("pool", wo[:], w_out[l].rearrange("(kc p) e -> p kc e", p=128), [], [wo])
        ybl = [Rot([A.sb([128, 4, 512], BF16, "ybl%d" % i) for _ in range(2)]) for i in range(3)]
        gbl = Rot([A.sb([128, 512], BF16, "gbl") for _ in range(6)])
        mT = Rot([A.sb([128, 8, 512], BF16, "mT") for _ in range(2)])
        macc = Rot([A.sb([128, 512], F32, "macc") for _ in range(3)])
        mtmp = Rot([A.sb([128, 512], F32, "mtmp") for _ in range(3)])
        xin = Rot([A.sb([128, 8, 512], F32, "mxin") for _ in range(2)])
        xo = Rot([A.sb([128, 512], F32, "mxo") for _ in range(3)])
        srcs = [yaT, ybT, ycT]
        xv = xcur.rearrange("(kc p) t -> p kc t", p=128)
        for bi, (t0, n) in enumerate(BLOCKS):
            if (last and bi == 0) or upto < 6:
                continue
            j = 1 if bi == 0 else 0
            yb3 = []
            for i in range(3):
                y_ = ybl[i].get()
                S.dma("sp", y_[:, :, :n], srcs[i].rearrange("(ct p) t -> p ct t", p=128)[:, :, t0:t0 + n], [], [y_])
                yb3.append(y_)
            x_ = xin.get()
            S.dma("sp", x_[:, :, :n], xv[:, :, t0:t0 + n], [], [x_])
            m_ = mT.get()
            for dt_ in range(8):
                ma = macc.get()
                for i in range(3):
                    g_ = gbl.get()
                    r0 = i * D + dt_ * 128
                    S.dma("sp", g_[:, :n], gT[r0:r0 + 128, t0:t0 + n], [], [g_])
                    pb = psum()
                    for ct in range(4):
                        mm(pb[:, :n], wbr[:, i, ct, dt_ * 128:(dt_ + 1) * 128], yb3[i][:, ct, :n], [wbr, yb3[i]], [pb],
                           start=(ct == 0), stop=(ct == 3))
                    if i == 0:
                        tt("dve", ma[:, :n], pb[:, :n], g_[:, :n], ALU.mult, [pb, g_], [ma])
                    else:
                        tp = mtmp.get()
                        tt("dve", tp[:, :n], pb[:, :n], g_[:, :n], ALU.mult, [pb, g_], [tp])
                        if i == 1:
                            tt("pool", ma[:, :n], ma[:, :n], tp[:, :n], ALU.add, [ma, tp], [ma])
                        else:
                            tt("pool", m_[:, dt_, :n], ma[:, :n], tp[:, :n], ALU.add, [ma, tp], [m_])
            for et in range(8):
                pb = psum()
                for dt_ in range(8):
                    mm(pb[:, :n], wo[:, dt_, et * 128:(et + 1) * 128], m_[:, dt_, :n], [wo, m_], [pb],
                       start=(dt_ == 0), stop=(dt_ == 7))
                o = xo.get()
                stt(o[:, :n], pb[:, :n], mod[:, 16 + et, j:j + 1], x_[:, et, :n], ALU.mult, ALU.add, [pb, mod, x_], [o])
                if last:
                    S.dma("pool", yT[et * 128:(et + 1) * 128, t0 - 256:t0 - 256 + n], o[:, :n], [o], [])
                else:
                    S.dma("pool", xnext[et * 128:(et + 1) * 128, t0:t0 + n], o[:, :n], [o], [])
        A.pop()
        A.pop()

    S.barrier()
    if dbg:
        A.push()
        for nm, (src, dst) in dbg_out.items():
            if len(src.shape) == 3:
                for a in range(src.shape[0]):
                    S.dma("sp", dst[a], src[a], [], [])
            else:
                S.dma("sp", dst, src, [], [])
        A.pop()
    if NL < DEPTH and upto >= 6:
        S.dma("sp", yT[:, :], xTs[(NL - 1) % 2][:, 256:TALL], [], [])
    A.pop()
    S.barrier()
    return nc, S


def _consts():
    cf = np.zeros((128, NCF), np.float32)
    cf[:, 0:128] = np.eye(128)
    blk = np.zeros((128, 128), np.float32)
    blk[:64, :64] = 1
    blk[64:, 64:] = 1
    cf[:, 128:256] = blk
    cf[:64, 256] = 1
    cf[64:, 257] = 1
    idx = np.arange(128)
    fw_strict = (idx[:, None] < idx[None, :]).astype(np.float32)
    bw_strict = (idx[:, None] > idx[None, :]).astype(np.float32)
    eye = np.eye(128, dtype=np.float32)
    cf[:, 260:388] = fw_strict
    cf[:, 388:516] = fw_strict + eye
    cf[:, 516:644] = bw_strict
    cf[:, 644:772] = bw_strict + eye
    cf[:, 772:900] = fw_strict.T
    cf[:, 900:1028] = bw_strict.T
    cf[:, 1028:1092] = eye[:, 0:64] + eye[:, 64:128]
    cb = np.zeros((128, NCB), np.float32)
    cb[:, 0:128] = np.eye(128)
    cb[:, 128:256] = 1
    cb[:, 256:384] = blk
    rotT = np.zeros((128, 128), np.float32)
    for m in range(128):
        f = m % 64
        base = m - f
        blk32 = (f // 32) * 32
        g = f % 32
        partner = base + blk32 + (g + 16 if g < 16 else g - 16)
        rotT[partner, m] = 1
    cb[:, 384:512] = rotT
    cb[:64, 512] = 1
    cb[64:, 513] = 1
    for d in range(2):
        for lev in range(7):
            b = 1 << lev
            blk = idx // b
            blk2 = idx // (2 * b)
            same2 = blk2[:, None] == blk2[None, :]
            diff = blk[:, None] != blk[None, :]
            before = (idx[:, None] < idx[None, :]) if d == 0 else (idx[:, None] > idx[None, :])
            m = (same2 & diff & before).astype(np.float32)
            c0 = 516 + (d * 7 + lev) * 128
            cb[:, c0:c0 + 128] = m
    nf = 16
    inv = (10000.0 ** (-np.arange(nf, dtype=np.float32) / nf)).astype(np.float32)
    tpos = np.arange(TLAT)
    row = (tpos // 64).astype(np.float32)
    col = (tpos % 64).astype(np.float32)
    C = np.ones((128, TALL), np.float32)
    Sg = np.zeros((128, TALL), np.float32)
    for p in range(128):
        f = p % 64
        pos = row if f < 32 else col
        g = f % 32
        i = g % 16
        ang = pos * inv[i]
        C[p, 256:] = np.cos(ang)
        Sg[p, 256:] = (-np.sin(ang)) if g < 16 else np.sin(ang)
    return cf, cb, C, Sg


def _host_inputs(inputs):
    f = lambda a: np.ascontiguousarray(np.asarray(a, dtype=np.float32))
    L = DEPTH
    pp = np.zeros((L, 128, NPP), np.float32)
    bc = np.zeros((L, 128, NBC), np.float32)
    b_mod = f(inputs["b_mod"])
    norm_g = f(inputs["norm_g"])
    r_conv = f(inputs["r_conv"])
    for l in range(L):
        pp[l, :, 0:24] = b_mod[l].reshape(24, 128).T
        pp[l, :, 24:32] = norm_g[l].reshape(8, 128).T
        pp[l, :, 32] = np.tile(f(inputs["d_qnorm"])[l], 2)
        pp[l, :, 33] = np.tile(f(inputs["d_knorm"])[l], 2)
        for which in range(3):
            for tap in range(3):
                pp[l, :, 34 + which * 12 + tap * 4:34 + which * 12 + tap * 4 + 4] = r_conv[l, which, tap].reshape(4, 128).T
        for d_ in range(2):
            pp[l, :, 70 + d_ * 4:74 + d_ * 4] = f(inputs["r_w0"])[l, d_].reshape(4, 128).T
            pp[l, :, 78 + d_ * 4:82 + d_ * 4] = f(inputs["r_a0"])[l, d_].reshape(4, 128).T
        pp[l, :, 86:90] = f(inputs["r_kk"])[l].reshape(4, 128).T
        pp[l, :, 90:94] = f(inputs["r_ka"])[l].reshape(4, 128).T
        pp[l, :, 98:102] = f(inputs["r_rk"])[l].reshape(4, 128).T
        row = np.concatenate([f(inputs["a_ln_g"])[l], f(inputs["a_ln_b"])[l], f(inputs["a_bs"])[l].reshape(-1),
                              f(inputs["d_subln_g"])[l], f(inputs["r_ln_g"])[l], f(inputs["r_ln_b"])[l],
                              f(inputs["d_lam"])[l].reshape(-1)])
        bc[l] = np.broadcast_to(row[None, :], (128, NBC))
    cf, cb, C, Sg = _consts()
    shared = dict(pp=pp, bc=bc, cf=cf, cbf=cb, ropeC=C, ropeS=Sg,
                  w_mod=f(inputs["w_mod"]), w_in=f(inputs["w_in"]),
                  a_wsT=np.ascontiguousarray(np.transpose(f(inputs["a_ws"]), (0, 1, 3, 2))),
                  r_w2=f(inputs["r_w2"]), r_a2=f(inputs["r_a2"]), w_br=f(inputs["w_br"]), w_out=f(inputs["w_out"]))
    x = f(inputs["x"])
    c = f(inputs["c"])
    ctx = f(inputs["ctx"])
    c_ctx = f(inputs["c_ctx"])
    maps = []
    for b in range(4):
        xT0 = np.ascontiguousarray(np.concatenate([ctx[b], x[b]], axis=0).T)
        cc = np.stack([c[b].reshape(8, 128).T, c_ctx.reshape(8, 128).T], axis=-1)
        m = dict(shared)
        m["xT0"] = xT0
        m["cc"] = np.ascontiguousarray(cc)
        maps.append(m)
    return maps


_CACHE = {}


def kernel(**inputs):
    maps = _host_inputs(inputs)
    if "nc" not in _CACHE:
        _CACHE["nc"] = build()[0]
    nc = _CACHE["nc"]
    in_maps = [maps[i % 4] for i in range(8)]
    res = run_bass_kernel_spmd(nc, in_maps, core_ids=list(range(8)))
    out = np.stack([np.ascontiguousarray(res.results[b]["yT"].T) for b in range(4)], axis=0)
    return out.astype(np.float32)
```

```python
import numpy as np
import concourse.bass as bass
import concourse.mybir as mybir
from concourse.bass_utils import run_bass_kernel_spmd

F32 = mybir.dt.float32
BF16 = mybir.dt.bfloat16
AF = mybir.ActivationFunctionType
ALU = mybir.AluOpType
AX = mybir.AxisListType

D = 1024
NCTX = 256
TLAT = 4096
TALL = NCTX + TLAT
NTILE = TALL // 128
DEPTH = 4
INC = 8960
C0 = 0.6065306597126334
BLOCKS = [(0, 256)] + [(256 + 512 * i, 512) for i in range(8)]
NPP = 104
NBC = 2944
NCF = 1092
NCB = 2308
NDS = 32


class Buf:
    __slots__ = ("w", "r", "excl")

    def __init__(self):
        self.w = None
        self.r = {}
        self.excl = False


class TileT:
    def __init__(self, h, nb=1):
        self.h = h
        self.b = Buf()

    def __getitem__(self, k):
        return self.h[k]


class Sched:
    def __init__(self, nc):
        self.nc = nc
        self.sems = []
        self.eng = {}
        for name, h in (("pe", nc.tensor), ("act", nc.scalar), ("dve", nc.vector), ("pool", nc.gpsimd), ("sp", nc.sync)):
            sid = len(self.sems)
            self.sems.append(nc.alloc_semaphore("s_" + name))
            self.eng[name] = dict(h=h, sid=sid, n=0, waited={})
        self.dq = {}
        for q in ("sp", "pool", "act"):
            ids = []
            for i in range(NDS):
                ids.append(len(self.sems))
                self.sems.append(nc.alloc_semaphore("d%s%d" % (q, i)))
            self.dq[q] = dict(ids=ids, nxt=0)
        self.dcnt = {}
        self.nins = 0
        import os as _os2
        self.maxops = int(_os2.environ.get('MAXOPS', '1000000000'))
        self.k = 0

    def _wait(self, e, toks):
        E = self.eng[e]
        need = {}
        for t in toks:
            if t is None:
                continue
            sid, val = t
            if sid == E["sid"] and e == "pe":
                continue
            if need.get(sid, 0) < val:
                need[sid] = val
        for sid, val in need.items():
            if E["waited"].get(sid, 0) < val:
                E["h"].wait_ge(self.sems[sid], val)
                E["waited"][sid] = val
                self.nins += 1

    def _deps(self, reads, writes):
        toks = []
        for b in reads:
            toks.append(b.w)
        for b in writes:
            toks.append(b.w)
            toks.extend(b.r.items())
        return toks

    def _mark(self, tok, reads, writes):
        for b in reads:
            if b.r.get(tok[0], 0) < tok[1]:
                b.r[tok[0]] = tok[1]
        for b in writes:
            b.w = tok
            b.r = {}

    def op(self, e, emit, reads=(), writes=()):
        self.k += 1
        if self.k > self.maxops:
            return
        reads = [x.b if isinstance(x, TileT) else x for x in reads]
        writes = [x.b if isinstance(x, TileT) else x for x in writes]
        writes = writes + [b for b in reads if b.excl]
        reads = [b for b in reads if not b.excl]
        self._wait(e, self._deps(reads, writes))
        E = self.eng[e]
        E["n"] += 1
        ins = emit(E["h"])
        ins.then_inc(self.sems[E["sid"]], 1)
        self.nins += 1
        self._mark((E["sid"], E["n"]), reads, writes)

    def dma(self, q, out, in_, reads=(), writes=()):
        self.k += 1
        if self.k > self.maxops:
            return
        reads = [x.b if isinstance(x, TileT) else x for x in reads]
        writes = [x.b if isinstance(x, TileT) else x for x in writes]
        self._wait(q, self._deps(reads, writes))
        Q = self.dq[q]
        sid = Q["ids"][Q["nxt"] % NDS]
        Q["nxt"] += 1
        prev = self.dcnt.get(sid, 0)
        if prev > 0:
            self._wait(q, [(sid, 16 * prev)])
        self.dcnt[sid] = prev + 1
        self.eng[q]["h"].dma_start(out=out, in_=in_).then_inc(self.sems[sid], 16)
        self.nins += 1
        self._mark((sid, 16 * (prev + 1)), reads, writes)

    def barrier(self):
        toks = [(E["sid"], E["n"]) for E in self.eng.values() if E["n"] > 0]
        toks += [(sid, 16 * c) for sid, c in self.dcnt.items()]
        for e in self.eng:
            E = self.eng[e]
            for sid, val in toks:
                if sid == E["sid"]:
                    continue
                if E["waited"].get(sid, 0) < val:
                    E["h"].wait_ge(self.sems[sid], val)
                    E["waited"][sid] = val


class Alloc:
    def __init__(self, nc, S):
        self.nc = nc
        self.S = S
        self.stack = []
        self.cnt = 0

    def push(self):
        self.stack.append([])

    def pop(self):
        self.S.barrier()
        for cm in reversed(self.stack.pop()):
            cm.__exit__(None, None, None)

    def sb(self, shape, dt, name=None):
        self.cnt += 1
        cm = self.nc.sbuf_tensor("%s_%d" % (name or "t", self.cnt), list(shape), dt)
        h = cm.__enter__()
        self.stack[-1].append(cm)
        return TileT(h)


def build(NL=DEPTH, dbg=None, upto=9):
    import os as _os
    lo = int(_os.environ.get('SKIP_TO', '0'))
    RWCUT = float(_os.environ.get('RW_CUT', '99'))
    nc = bass.Bass("TRN2", target_bir_lowering=False)
    S = Sched(nc)
    A = Alloc(nc, S)

    def din(name, shape, dt=F32):
        return nc.dram_tensor(name, list(shape), dt, kind="ExternalInput").ap()

    def dscr(name, shape, dt=F32):
        return nc.dram_tensor(name, list(shape), dt, kind="Internal").ap()

    xT0 = din("xT0", [D, TALL])
    cc_d = din("cc", [128, 8, 2])
    pp_d = din("pp", [DEPTH, 128, NPP])
    bc_d = din("bc", [DEPTH, 128, NBC])
    cf_d = din("cf", [128, NCF])
    cb_d = din("cbf", [128, NCB])
    ropeC_d = din("ropeC", [128, TALL])
    ropeS_d = din("ropeS", [128, TALL])
    w_mod = din("w_mod", [DEPTH, D, 3 * D])
    w_in = din("w_in", [DEPTH, D, INC])
    a_wsT = din("a_wsT", [DEPTH, 4, 128, 128])
    r_w2 = din("r_w2", [DEPTH, 2, 64, 512])
    r_a2 = din("r_a2", [DEPTH, 2, 64, 512])
    w_br = din("w_br", [DEPTH, 3, 512, D])
    w_out = din("w_out", [DEPTH, D, D])
    yT = nc.dram_tensor("yT", [D, TLAT], F32, kind="ExternalOutput").ap()

    xTs = [dscr("xTa", [D, TALL]), dscr("xTb", [D, TALL])]
    qT = dscr("qT", [4, 128, TALL], BF16)
    kT = dscr("kT", [4, 128, TALL], BF16)
    Vd = dscr("Vd", [TALL, 512], BF16)
    krT = dscr("krT", [512, TALL])
    vrT = dscr("vrT", [512, TALL])
    rrT = dscr("rrT", [512, TALL])
    wlT = dscr("wlT", [128, TALL])
    alT = dscr("alT", [128, TALL])
    uzT = dscr("uzT", [512, TALL], BF16)
    vln = dscr("vln", [TALL, 512], BF16)
    zbd = dscr("zbd", [TALL, 512], BF16)
    zcd = dscr("zcd", [TALL, 512], BF16)
    gT = dscr("gT", [3 * D, TALL], BF16)
    yaT = dscr("yaT", [512, TALL], BF16)
    ybT = dscr("ybT", [512, TALL], BF16)
    ycT = dscr("ycT", [512, TALL], BF16)
    yfw = dscr("yfw", [TALL, 512])
    dbg_out = {}
    if dbg:
        for nm in dbg:
            src = dict(qT=qT, kT=kT, Vd=Vd, krT=krT, vrT=vrT, rrT=rrT, wlT=wlT, alT=alT, uzT=uzT, vln=vln, zbd=zbd,
                       zcd=zcd, gT=gT, yaT=yaT, ybT=ybT, ycT=ycT, yfw=yfw, xTa=xTs[0], xTb=xTs[1])[nm]
            dbg_out[nm] = (src, nc.dram_tensor("dbg_" + nm, list(src.shape), src.dtype, kind="ExternalOutput").ap())

    banks = [TileT(nc.alloc_psum_tensor("ps%d" % i, [128, 512], F32)) for i in range(8)]
    for b_k in banks:
        b_k.b.excl = True
    pstate = dict(i=0, nb=8)

    def psum():
        b = banks[pstate["i"] % pstate["nb"]]
        pstate["i"] += 1
        return b

    def mm(out, lhsT, rhs, R, W, start=True, stop=True):
        S.op("pe", lambda h: h.matmul(out, lhsT, rhs, start=start, stop=stop), R, W)

    def tr(out, in_, ident, R, W):
        S.op("pe", lambda h: h.transpose(out, in_, ident), R, W)

    def act(out, in_, func, R, W, bias=None, scale=None, accum=None, eng="act"):
        kw = {}
        if bias is not None:
            kw["bias"] = bias
        if scale is not None:
            kw["scale"] = scale
        if accum is not None:
            kw["accum_out"] = accum
        S.op("act", lambda h: h.activation(out, in_, func, **kw), R, W)

    def tt(e, out, in0, in1, op, R, W):
        S.op(e, lambda h: h.tensor_tensor(out, in0, in1, op), R, W)

    def ts(e, out, in0, s1, s2, op0, op1, R, W):
        if s2 is None:
            S.op(e, lambda h: h.tensor_scalar(out, in0, s1, None, op0), R, W)
        else:
            S.op(e, lambda h: h.tensor_scalar(out, in0, s1, s2, op0, op1), R, W)

    def stt(out, in0, sc, in1, op0, op1, R, W):
        S.op("dve", lambda h: h.scalar_tensor_tensor(out, in0, sc, in1, op0, op1), R, W)

    def cp(e, out, in_, R, W):
        if e == "act":
            S.op("act", lambda h: h.copy(out, in_), R, W)
        else:
            S.op(e, lambda h: h.tensor_copy(out, in_), R, W)

    def recip(out, in_, R, W):
        S.op("dve", lambda h: h.reciprocal(out, in_), R, W)

    def memset(e, ap, val, W):
        S.op(e, lambda h: h.memset(ap, val), [], W)

    class Rot:
        def __init__(self, tiles):
            self.t = tiles
            self.i = 0

        def get(self):
            t = self.t[self.i % len(self.t)]
            self.i += 1
            return t

    A.push()
    cf = A.sb([128, NCF], F32, "cf")
    cb = A.sb([128, NCB], BF16, "cb")
    cc = A.sb([128, 8, 2], F32, "cc")
    cact = A.sb([128, 8, 2], F32, "cact")
    epsT = A.sb([128, 4], F32, "eps")
    onesf = A.sb([128, 128], F32, "onesf")
    S.dma("sp", cf[:], cf_d[:, :], [], [cf])
    S.dma("pool", cb[:], cb_d[:, :], [], [cb])
    S.dma("sp", cc[:], cc_d[:, :, :], [], [cc])
    act(cact[:], cc[:], AF.Silu, [cc], [cact])
    memset("dve", epsT[:, 0:1], 1e-6, [epsT])
    memset("dve", epsT[:, 1:2], 1e-5, [epsT])
    memset("dve", epsT[:, 2:3], 64e-5, [epsT])
    memset("dve", epsT[:, 3:4], 0.0, [epsT])
    memset("dve", onesf[:], 1.0, [onesf])
    ident_f = lambda ps, cs: cf.h[ps, cs]
    blk64_f = cf.h[:, 128:256]
    hsel_f = cf.h[:, 256:258]
    identfold = cf.h[:, 1028:1092]
    maskT = [cf.h[:, 260:516], cf.h[:, 516:772]]
    maskL = [cf.h[:, 772:900], cf.h[:, 900:1028]]
    ident_b = cb.h[:, 0:128]
    ones_b = cb.h[:, 128:256]
    blk64_b = cb.h[:, 256:384]
    rotT_b = cb.h[:, 384:512]
    hsel_b = cb.h[:, 512:514]

    def mLT(d, lev):
        c0 = 516 + (d * 7 + lev) * 128
        return cb.h[:, c0:c0 + 128]

    for l in range(NL):
        lam_init = 0.8 - 0.6 * float(np.exp(-0.3 * l))
        xcur = xT0 if l == 0 else xTs[(l - 1) % 2]
        xnext = xTs[l % 2]
        last = l == DEPTH - 1
        A.push()
        pp = A.sb([128, NPP], F32, "pp")
        bcs = A.sb([128, NBC], F32, "bc")
        mod = A.sb([128, 24, 2], F32, "mod")
        gs = A.sb([128, 8, 2], F32, "gs")
        lamt = A.sb([128, 8], F32, "lam")
        sublng = A.sb([128, 128], F32, "sublng")
        omka = A.sb([128, 4], F32, "omka")
        S.dma("sp", pp[:], pp_d[l, :, :], [], [pp])
        S.dma("sp", bcs[:], bc_d[l, :, :], [], [bcs])

        A.push()
        wm = Rot([A.sb([128, 8, 512], F32, "wm") for _ in range(2)])
        pb = psum()
        wmv = w_mod[l].rearrange("(kc p) n -> p kc n", p=128)
        for g6 in range(6):
            w = wm.get()
            S.dma("sp", w[:], wmv[:, :, g6 * 512:(g6 + 1) * 512], [], [w])
            for nt4 in range(4):
                nt = g6 * 4 + nt4
                for kc in range(8):
                    mm(pb[:, nt * 2:nt * 2 + 2], w[:, kc, nt4 * 128:(nt4 + 1) * 128], cact[:, kc, :], [w, cact], [pb],
                       start=(kc == 0), stop=(kc == 7))
        tt("dve", mod[:], pb[:, 0:48].rearrange("p (a b) -> p a b", b=2),
           pp[:, 0:24].unsqueeze(2).to_broadcast([128, 24, 2]), ALU.add, [pb, pp], [mod])
        ts("dve", gs[:], mod[:, 8:16, :], 1.0, None, ALU.add, None, [mod], [gs])
        tt("dve", gs[:], gs[:], pp[:, 24:32].unsqueeze(2).to_broadcast([128, 8, 2]), ALU.mult, [gs, pp], [gs])
        lamtmp = A.sb([128, 2, 64], F32, "lamtmp")
        dl = bcs[:, 2688:2944].rearrange("p (a b) -> p a b", b=64)
        tt("dve", lamtmp[:, 0, :], dl[:, 0, :], dl[:, 1, :], ALU.mult, [bcs], [lamtmp])
        tt("dve", lamtmp[:, 1, :], dl[:, 2, :], dl[:, 3, :], ALU.mult, [bcs], [lamtmp])
        S.op("dve", lambda h: h.tensor_reduce(lamt[:, 0:2], lamtmp[:], AX.X, ALU.add), [lamtmp], [lamt])
        act(lamt[:, 2:4], lamt[:, 0:2], AF.Exp, [lamt], [lamt])
        tt("dve", lamt[:, 4:5], lamt[:, 2:3], lamt[:, 3:4], ALU.subtract, [lamt], [lamt])
        ts("dve", lamt[:, 5:6], lamt[:, 4:5], lam_init, -1.0, ALU.add, ALU.mult, [lamt], [lamt])
        ts("dve", sublng[:], bcs[:, 1536:1664], 1.0 - lam_init, None, ALU.mult, None, [bcs], [sublng])
        ts("dve", omka[:], pp[:, 90:94], -1.0, 1.0, ALU.mult, ALU.add, [pp], [omka])
        neg_lam = lamt.h[:, 5:6]
        A.pop()

        A.push()
        hT = A.sb([128, 8, TALL], BF16, "hT")
        hTb = [Buf() for _ in BLOCKS]
        A.push()
        xin = Rot([A.sb([128, 8, 512], F32, "xin") for _ in range(2)])
        sqs = Rot([A.sb([128, 8, 512], BF16, "sq") for _ in range(2)])
        rstds = Rot([A.sb([128, 512], F32, "rstd") for _ in range(2)])
        tmps = Rot([A.sb([128, 512], F32, "ntmp") for _ in range(3)])
        xv = xcur.rearrange("(kc p) t -> p kc t", p=128)
        for bi, (t0, n) in enumerate(BLOCKS if lo <= 1 <= upto else []):
            j = 1 if bi == 0 else 0
            x_ = xin.get()
            S.dma("sp", x_[:, :, :n], xv[:, :, t0:t0 + n], [], [x_])
            sq = sqs.get()
            act(sq[:, :, :n], x_[:, :, :n], AF.Square, [x_], [sq])
            pb = psum()
            for kc in range(8):
                mm(pb[:, :n], ones_b, sq[:, kc, :n], [cb, sq], [pb], start=(kc == 0), stop=(kc == 7))
            rs = rstds.get()
            act(rs[:, :n], pb[:, :n], AF.Sqrt, [pb, epsT], [rs], bias=epsT[:, 0:1], scale=1.0 / D)
            recip(rs[:, :n], rs[:, :n], [rs], [rs])
            for kc in range(8):
                tm = tmps.get()
                stt(tm[:, :n], x_[:, kc, :n], gs[:, kc, j:j + 1], rs[:, :n], ALU.mult, ALU.mult, [x_, gs, rs], [tm])
                act(hT[:, kc, t0:t0 + n], tm[:, :n], AF.Identity, [tm, mod], [hTb[bi]], bias=mod[:, kc, j:j + 1])
        A.pop()

        A.push()
        wbufs = Rot([A.sb([128, 8, 512], BF16, "wbuf") for _ in range(4)])
        ostf = Rot([A.sb([128, 512], F32, "ostf") for _ in range(4)])
        ostb = Rot([A.sb([128, 512], BF16, "ostb") for _ in range(4)])
        ptmp = Rot([A.sb([128, 512], F32, "ptmp") for _ in range(6)])
        ptmpb = Rot([A.sb([128, 512], BF16, "ptmpb") for _ in range(4)])
        ropeCs = Rot([A.sb([128, 512], F32, "rC") for _ in range(2)])
        ropeSs = Rot([A.sb([128, 512], F32, "rS") for _ in range(2)])
        stat = Rot([A.sb([128, 16], F32, "stat") for _ in range(4)])
        winv = w_in[l].rearrange("(kc p) n -> p kc n", p=128)

        def loadw(c0, ncols=512):
            w = wbufs.get()
            S.dma("pool", w[:, :, :ncols], winv[:, :, c0:c0 + ncols], [], [w])
            return w

        def fm_mm(w, ct, bi, t0, n):
            pb = psum()
            for kc in range(8):
                mm(pb[:, :n], w[:, kc, ct * 128:(ct + 1) * 128], hT[:, kc, t0:t0 + n], [w, hTb[bi]], [pb],
                   start=(kc == 0), stop=(kc == 7))
            return pb

        def tm_mm(w, t0):
            pb = psum()
            bi = 0 if t0 < 256 else 1 + (t0 - 256) // 512
            for kc in range(8):
                mm(pb[:, :], hT[:, kc, t0:t0 + 128], w[:, kc, :], [w, hTb[bi]], [pb], start=(kc == 0), stop=(kc == 7))
            return pb

        def fm_raw(c0, ncols, dst, alt=[0]):
            w = loadw(c0, ncols)
            for bi, (t0, n) in enumerate(BLOCKS):
                for ct in range(ncols // 128):
                    pb = fm_mm(w, ct, bi, t0, n)
                    o = ostf.get()
                    alt[0] ^= 1
                    cp("act" if alt[0] else "dve", o[:, :n], pb[:, :n], [pb], [o])
                    S.dma("pool", dst[ct * 128:(ct + 1) * 128, t0:t0 + n], o[:, :n], [o], [])

        def fm_qk(c0, dst, gcol):
            w = loadw(c0)
            for bi, (t0, n) in enumerate(BLOCKS):
                rC = ropeCs.get()
                rS = ropeSs.get()
                S.dma("sp", rC[:, :n], ropeC_d[:, t0:t0 + n], [], [rC])
                S.dma("sp", rS[:, :n], ropeS_d[:, t0:t0 + n], [], [rS])
                for ct in range(4):
                    pb = fm_mm(w, ct, bi, t0, n)
                    sqb = ptmpb.get()
                    act(sqb[:, :n], pb[:, :n], AF.Square, [pb], [sqb])
                    p2 = psum()
                    mm(p2[:, :n], blk64_b, sqb[:, :n], [cb, sqb], [p2])
                    rs = ptmp.get()
                    act(rs[:, :n], p2[:, :n], AF.Sqrt, [p2, epsT], [rs], bias=epsT[:, 0:1], scale=1.0 / 64)
                    recip(rs[:, :n], rs[:, :n], [rs], [rs])
                    xn = ptmpb.get()
                    stt(xn[:, :n], pb[:, :n], pp[:, gcol:gcol + 1], rs[:, :n], ALU.mult, ALU.mult, [pb, pp, rs], [xn])
                    p3 = psum()
                    mm(p3[:, :n], rotT_b, xn[:, :n], [cb, xn], [p3])
                    t1 = ptmp.get()
                    tt("pool", t1[:, :n], xn[:, :n], rC[:, :n], ALU.mult, [xn, rC], [t1])
                    t2 = ptmp.get()
                    tt("dve", t2[:, :n], p3[:, :n], rS[:, :n], ALU.mult, [p3, rS], [t2])
                    o = ostb.get()
                    tt("pool", o[:, :n], t1[:, :n], t2[:, :n], ALU.add, [t1, t2], [o])
                    S.dma("pool", dst[ct, :, t0:t0 + n], o[:, :n], [o], [])

        def tm_group(c0, kind, dst):
            w = loadw(c0)
            for ti in range(NTILE):
                t0 = ti * 128
                pb = tm_mm(w, t0)
                o = ostb.get()
                if kind == "copy":
                    cp("act" if ti % 2 else "dve", o[:], pb[:], [pb], [o])
                elif kind == "silu":
                    act(o[:], pb[:], AF.Silu, [pb], [o])
                else:
                    tA = ptmp.get()
                    tB = ptmp.get()
                    g = ptmp.get()
                    act(tA[:], pb[:], AF.Square, [pb], [tA])
                    ts("dve", tA[:], tA[:], 0.044715, 1.0, ALU.mult, ALU.add, [tA], [tA])
                    tt("dve", tA[:], tA[:], pb[:], ALU.mult, [tA, pb], [tA])
                    act(tB[:], tA[:], AF.Sigmoid, [tA], [tB], scale=1.5957691216057308)
                    tt("dve", g[:], tB[:], pb[:], ALU.mult, [tB, pb], [g])
                    st = stat.get()
                    S.op("dve", lambda h: h.bn_stats(st[:, 0:6], g[:]), [g], [st])
                    S.op("dve", lambda h: h.bn_aggr(st[:, 8:10], st[:, 0:6]), [st], [st])
                    act(st[:, 10:11], st[:, 9:10], AF.Sqrt, [st, epsT], [st], bias=epsT[:, 1:2], scale=1.0)
                    recip(st[:, 11:12], st[:, 10:11], [st], [st])
                    ts("dve", g[:], g[:], st[:, 8:9], st[:, 11:12], ALU.subtract, ALU.mult, [g, st], [g])
                    tt("pool", g[:], g[:], bcs[:, 0:512], ALU.mult, [g, bcs], [g])
                    tt("pool", o[:], g[:], bcs[:, 512:1024], ALU.add, [g, bcs], [o])
                S.dma("pool", dst[t0:t0 + 128, :], o[:], [o], [])

        def fm_uz():
            wu = loadw(3328)
            wz = loadw(4352)
            for bi, (t0, n) in enumerate(BLOCKS):
                for ct in range(4):
                    pu = fm_mm(wu, ct, bi, t0, n)
                    pz = fm_mm(wz, ct, bi, t0, n)
                    tA = ptmp.get()
                    tB = ptmp.get()
                    g = ptmp.get()
                    act(tA[:, :n], pu[:, :n], AF.Square, [pu], [tA])
                    ts("dve", tA[:, :n], tA[:, :n], 0.044715, 1.0, ALU.mult, ALU.add, [tA], [tA])
                    tt("dve", tA[:, :n], tA[:, :n], pu[:, :n], ALU.mult, [tA, pu], [tA])
                    act(tB[:, :n], tA[:, :n], AF.Sigmoid, [tA], [tB], scale=1.5957691216057308)
                    tt("dve", g[:, :n], tB[:, :n], pu[:, :n], ALU.mult, [tB, pu], [g])
                    sz = ptmp.get()
                    act(sz[:, :n], pz[:, :n], AF.Silu, [pz], [sz])
                    o = ostb.get()
                    tt("pool", o[:, :n], g[:, :n], sz[:, :n], ALU.mult, [g, sz], [o])
                    S.dma("pool", uzT[ct * 128:(ct + 1) * 128, t0:t0 + n], o[:, :n], [o], [])

        def fm_gl(c0, gi):
            w = loadw(c0)
            for bi, (t0, n) in enumerate(BLOCKS):
                for ct in range(4):
                    pb = fm_mm(w, ct, bi, t0, n)
                    o = ostb.get()
                    act(o[:, :n], pb[:, :n], AF.Sigmoid, [pb], [o])
                    r0 = gi * 512 + ct * 128
                    S.dma("pool", gT[r0:r0 + 128, t0:t0 + n], o[:, :n], [o], [])

        if lo <= 2 <= upto:
            fm_qk(0, kT, 33)
            tm_group(512, "copy", Vd)
            fm_raw(1024, 512, krT)
            fm_raw(1536, 512, vrT)
            fm_raw(2048, 128, wlT)
            fm_raw(2176, 128, alT)
            fm_qk(2304, qT, 32)
            fm_raw(2816, 512, rrT)
            fm_uz()
            tm_group(3840, "va", vln)
            tm_group(4864, "silu", zbd)
            tm_group(5376, "silu", zcd)
            for gi in range(6):
                fm_gl(5888 + gi * 512, gi)
        A.pop()
        A.pop()

        A.push()
        wsT = A.sb([128, 4, 128], BF16, "wsT")
        S.dma("pool", wsT[:], a_wsT[l].rearrange("g q p -> q g p"), [], [wsT])
        vlb = Rot([A.sb([128, 4, 512], BF16, "vlb") for _ in range(2)])
        uzb = Rot([A.sb([128, 4, 512], BF16, "uzb") for _ in range(2)])
        gtmp = Rot([A.sb([128, 512], F32, "gtmp") for _ in range(3)])
        gost = Rot([A.sb([128, 512], BF16, "gost") for _ in range(3)])
        vlnv = vln.rearrange("(t p) c -> p t c", p=128)
        uzv = uzT.rearrange("(ct p) t -> p ct t", p=128)
        for bi, (t0, n) in enumerate(BLOCKS if lo <= 3 <= upto else []):
            nt = n // 128
            vb_ = vlb.get()
            ub_ = uzb.get()
            S.dma("sp", vb_[:, :nt, :], vlnv[:, t0 // 128:t0 // 128 + nt, :], [], [vb_])
            S.dma("sp", ub_[:, :, :n], uzv[:, :, t0:t0 + n], [], [ub_])
            for g in range(4):
                pb = psum()
                for ti in range(nt):
                    mm(pb[:, ti * 128:(ti + 1) * 128], vb_[:, ti, g * 128:(g + 1) * 128], wsT[:, g, :], [vb_, wsT], [pb])
                tg = gtmp.get()
                tt("dve", tg[:, :n].rearrange("p (a b) -> p a b", b=128), pb[:, :n].rearrange("p (a b) -> p a b", b=128),
                   bcs[:, 1024 + g * 128:1024 + (g + 1) * 128].unsqueeze(1).to_broadcast([128, nt, 128]), ALU.add,
                   [pb, bcs], [tg])
                o = gost.get()
                tt("pool", o[:, :n], tg[:, :n], ub_[:, g, :n], ALU.mult, [tg, ub_], [o])
                S.dma("pool", yaT[g * 128:(g + 1) * 128, t0:t0 + n], o[:, :n], [o], [])
        A.pop()

        A.push()
        pstate["nb"] = 6
        kTh = Rot([A.sb([128, TALL], BF16, "kTh") for _ in range(2)])
        Vh = Rot([A.sb([128, NTILE, 132], BF16, "Vh") for _ in range(2)])
        for v_ in Vh.t:
            memset("pool", v_[:, :, 128:132], 1.0, [v_])
        PTs = Rot([A.sb([128, NTILE, 512], BF16, "PT") for _ in range(3)])
        qTb = Rot([A.sb([128, 512], BF16, "qTb") for _ in range(2)])
        zbt = Rot([A.sb([128, 4, 128], BF16, "zbt") for _ in range(2)])
        ao = Rot([A.sb([128, 128], F32, "ao") for _ in range(6)])
        ast = Rot([A.sb([128, 8], F32, "ast") for _ in range(4)])
        ayb = Rot([A.sb([128, 128], BF16, "ayb") for _ in range(3)])
        aost = Rot([A.sb([128, 512], BF16, "aost") for _ in range(2)])
        Vv = Vd.rearrange("(t p) c -> p t c", p=128)
        zbv = zbd.rearrange("(t p) c -> p t c", p=128)
        ao0 = [A.sb([128, 128], F32, "ao0") for _ in range(4)]
        pvbank_i = [0]

        def qke_gen(kh, qb, n, kts, j, PTj):
            js = slice(j * 64, (j + 1) * 64)
            for kt in kts:
                pb = psum()
                mm(pb[:, :n], kh[js, kt * 128:(kt + 1) * 128], qb[js, :n], [kh, qb], [pb])
                act(PTj[:, kt, :n], pb[:, :n], AF.Exp, [pb], [PTj], scale=0.125)
                yield

        def pv_gen(vh, PTj, j, kts, nt, zt, ob, h_, t0, n):
            for ti in range(nt):
                pb = banks[6 + (pvbank_i[0] % 2)]
                pvbank_i[0] += 1
                for ki, kt in enumerate(kts):
                    mm(pb[:, 0:129], PTj[:, kt, ti * 128:(ti + 1) * 128], vh[:, kt, 0:129], [PTj, vh], [pb],
                       start=(ki == 0), stop=(ki == len(kts) - 1))
                    if ki % 6 == 5:
                        yield
                st = ast.get()
                recip(st[:, 0:1], pb[:, 128:129], [pb], [st])
                if j == 0:
                    ts("dve", ao0[ti][:], pb[:, 0:128], st[:, 0:1], None, ALU.mult, None, [pb, st], [ao0[ti]])
                    yield
                    continue
                o1 = ao.get()
                ts("dve", o1[:], pb[:, 0:128], st[:, 0:1], None, ALU.mult, None, [pb, st], [o1])
                od = ao.get()
                stt(od[:], o1[:], neg_lam, ao0[ti][:], ALU.mult, ALU.add, [o1, ao0[ti], lamt], [od])
                st2 = ast.get()
                junk = ao.get()
                act(junk[:], od[:], AF.Square, [od], [junk, st2], accum=st2[:, 0:1])
                yield
                act(st2[:, 1:2], st2[:, 0:1], AF.Sqrt, [st2, epsT], [st2], bias=epsT[:, 0:1], scale=1.0 / 128)
                recip(st2[:, 2:3], st2[:, 1:2], [st2], [st2])
                stt(od[:], od[:], st2[:, 2:3], sublng[:], ALU.mult, ALU.mult, [od, st2, sublng], [od])
                yb_ = ayb.get()
                tt("dve", yb_[:], od[:], zt[:, ti, :], ALU.mult, [od, zt], [yb_])
                pbt_ = psum()
                pbb = pbt_.h[:].bitcast(BF16)
                tr(pbb[:, 0:128], yb_[:], ident_b, [yb_, cb], [pbt_])
                cp("act", ob[:, ti * 128:(ti + 1) * 128], pbb[:, 0:128], [pbt_], [ob])
                yield
            if j == 1:
                S.dma("pool", ybT[h_ * 128:(h_ + 1) * 128, t0:t0 + n], ob[:, :n], [ob], [])

        def run2(g_main, g_side, ratio=2):
            while g_main is not None or g_side is not None:
                if g_main is not None:
                    try:
                        next(g_main)
                    except StopIteration:
                        g_main = None
                if g_side is not None:
                    for _ in range(ratio):
                        try:
                            next(g_side)
                        except StopIteration:
                            g_side = None
                            break

        for h_ in range(4 if lo <= 4 <= upto else 0):
            kh = kTh.get()
            vh = Vh.get()
            S.dma("sp", kh[:], kT[h_, :, :], [], [kh])
            for q0 in range(0, NTILE, 9):
                q1 = min(NTILE, q0 + 9)
                S.dma("sp", vh[:, q0:q1, 0:128], Vv[:, q0:q1, h_ * 128:(h_ + 1) * 128], [], [vh])

            def blk_setup(bi):
                t0, n = BLOCKS[bi]
                nt = n // 128
                kts = [0, 1] if bi == 0 else list(range(NTILE))
                qb = qTb.get()
                S.dma("sp", qb[:, :n], qT[h_, :, t0:t0 + n], [], [qb])
                zt = zbt.get()
                S.dma("sp", zt[:, :nt, :], zbv[:, t0 // 128:t0 // 128 + nt, h_ * 128:(h_ + 1) * 128], [], [zt])
                return dict(t0=t0, n=n, nt=nt, kts=kts, qb=qb, zt=zt, PT=[None, None])

            cur = blk_setup(0)
            for j in range(2):
                cur["PT"][j] = PTs.get()
                run2(qke_gen(kh, cur["qb"], cur["n"], cur["kts"], j, cur["PT"][j]), None)
            for bi in range(len(BLOCKS)):
                nxt = blk_setup(bi + 1) if bi + 1 < len(BLOCKS) else None
                ob = aost.get()
                for j in range(2):
                    side = None
                    if nxt is not None:
                        nxt["PT"][j] = PTs.get()
                        side = qke_gen(kh, nxt["qb"], nxt["n"], nxt["kts"], j, nxt["PT"][j])
                    run2(pv_gen(vh, cur["PT"][j], j, cur["kts"], cur["nt"], cur["zt"], ob, h_, cur["t0"], cur["n"]), side)
                cur = nxt
        A.pop()

        A.push()
        w2a2 = [A.sb([128, 512], BF16, "w2a2") for _ in range(2)]
        for d_ in range(2):
            S.dma("pool", w2a2[d_][0:64, :], r_w2[l, d_, :, :], [], [w2a2[d_]])
            S.dma("pool", w2a2[d_][64:128, :], r_a2[l, d_, :, :], [], [w2a2[d_]])
        STb = [A.sb([128, 4, 64], BF16, "STb") for _ in range(2)]
        STB = [[[Buf() for _ in range(2)] for _ in range(4)] for _ in range(2)]
        NW = 2

        def wsf(name, shape=(128, 128), dt=F32, k=NW):
            return Rot([A.sb(list(shape), dt, name) for _ in range(k)])

        W_wla = wsf("wla")
        W_lab = wsf("lab", dt=BF16)
        PT = [dict(krc=A.sb([128, 130], F32, "krc"), vrc=A.sb([128, 130], F32, "vrc"), rrc=A.sb([128, 130], F32, "rrc"),
                   **{nm: A.sb([128, 128], F32, nm) for nm in ("k", "v", "r", "sg", "a", "Gs", "G2", "Gx", "eG", "enG", "eGx",
                                                              "eGC", "kk", "sq", "b", "kd")},
                   **{nm: A.sb([128, 128], BF16, nm) for nm in ("sqb", "BhT", "KhT", "vb", "rkk")}) for ct in range(4)]
        PS = [[dict(AR=A.sb([128, 2, 128], BF16, "AR"), Bt=A.sb([128, 128], BF16, "Bt"), Kt=A.sb([128, 128], BF16, "Kt"),
                    TK=A.sb([128, 3, 128], BF16, "TK"), Rtf=A.sb([128, 128], F32, "Rtf"), sm=A.sb([128, 8], F32, "sm"))
               for ct in range(4)] for par in range(2)]
        VTs = [A.sb([128, 4, 128], BF16, "VT") for par in range(2)]
        betas = [A.sb([128, 8], F32, "beta") for par in range(2)]
        SL = [dict(M1=A.sb([128, 2, 256], BF16, "M1"), M2=A.sb([128, 2, 256], BF16, "M2"), X=A.sb([128, 2, 128], BF16, "X"),
                   T=[A.sb([128, 2, 128], BF16, "T") for _ in range(2)], TT=[A.sb([128, 2, 128], BF16, "TT") for _ in range(2)],
                   Lm=[A.sb([128, 2, 128], BF16, "Lm") for _ in range(8)], W=[A.sb([128, 2, 128], BF16, "W") for _ in range(2)],
                   UA=A.sb([128, 2, 128], BF16, "UA"), UP=A.sb([128, 2, 128], BF16, "UP"), Phi=A.sb([128, 64], BF16, "Phi"),
                   Psi=A.sb([128, 64], BF16, "Psi"), QT=A.sb([128, 128], BF16, "QT")) for _ in range(4)]
        W_yo = wsf("yo", (128, 512), k=2)
        W_y = wsf("y", (128, 512), k=2)
        W_ysq = wsf("ysq", (128, 512), k=2)
        W_st = wsf("rst", (128, 48), k=2)
        W_zc = wsf("zc", (128, 512), BF16, 2)
        W_ycz = wsf("ycz", (128, 512), BF16, 2)
        W_yco = wsf("yco", (128, 4, 128), BF16, 2)
        zcv = zcd.rearrange("(t p) c -> p t c", p=128)
        ycTv = ycT.rearrange("(ct p) t -> p ct t", p=128)

        def conv(dst, src, which, ct):
            base = 34 + which * 12 + ct
            ts("dve", dst[:], src[:, 0:128], pp[:, base:base + 1], None, ALU.mult, None, [src, pp], [dst])
            stt(dst[:], src[:, 1:129], pp[:, base + 4:base + 5], dst[:], ALU.mult, ALU.add, [src, pp, dst], [dst])
            stt(dst[:], src[:, 2:130], pp[:, base + 8:base + 9], dst[:], ALU.mult, ALU.add, [src, pp, dst], [dst])

        def load_halo(tile, src, ct, t0):
            left = not (t0 == 0 or t0 == 256)
            right = not (t0 + 128 == 256 or t0 + 128 == TALL)
            lo = t0 - (1 if left else 0)
            hi = t0 + 128 + (1 if right else 0)
            c_lo = 1 - (t0 - lo)
            if not left:
                memset("pool", tile[:, 0:1], 0.0, [tile])
            if not right:
                memset("pool", tile[:, 129:130], 0.0, [tile])
            S.dma("sp", tile[:, c_lo:c_lo + (hi - lo)], src[ct * 128:(ct + 1) * 128, lo:hi], [], [tile])

        def prep_chunk(d_, c_):
            t0 = c_ * 128
            wla = W_wla.get()
            S.dma("sp", wla[0:64, :], wlT[d_ * 64:(d_ + 1) * 64, t0:t0 + 128], [], [wla])
            S.dma("sp", wla[64:128, :], alT[d_ * 64:(d_ + 1) * 64, t0:t0 + 128], [], [wla])
            lab = W_lab.get()
            act(lab[0:64, :], wla[0:64, :], AF.Tanh, [wla], [lab])
            cp("dve", lab[64:128, :], wla[64:128, :], [wla], [lab])
            return lab

        def prep_gen(d_, c_, par, ct, lab):
            t0 = c_ * 128
            VT = VTs[par]
            beta = betas[par]
            P = PS[par][ct]
            AR, Bt, Kt, TK, Rtf, sm = P["AR"], P["Bt"], P["Kt"], P["TK"], P["Rtf"], P["sm"]
            W = PT[ct]
            krc, vrc, rrc = W["krc"], W["vrc"], W["rrc"]
            load_halo(krc, krT, ct, t0)
            load_halo(vrc, vrT, ct, t0)
            load_halo(rrc, rrT, ct, t0)
            pw = psum()
            mm(pw[:, 0:128], w2a2[d_][0:64, ct * 128:(ct + 1) * 128], lab[0:64, :], [w2a2[d_], lab], [pw])
            sg = W["sg"]
            a_ = W["a"]
            act(sg[:], pw[:, 0:128], AF.Sigmoid, [pw, pp], [sg], bias=pp[:, 70 + d_ * 4 + ct:71 + d_ * 4 + ct])
            yield
            pw2 = psum()
            mm(pw2[:, 128:256], w2a2[d_][64:128, ct * 128:(ct + 1) * 128], lab[64:128, :], [w2a2[d_], lab], [pw2])
            act(a_[:], pw2[:, 128:256], AF.Sigmoid, [pw2, pp], [a_], bias=pp[:, 78 + d_ * 4 + ct:79 + d_ * 4 + ct])
            yield
            k_, v_, r_ = W["k"], W["v"], W["r"]
            for (dst, src, which) in ((r_, rrc, 0), (k_, krc, 1), (v_, vrc, 2)):
                base = 34 + which * 12 + ct
                ts("dve", dst[:], src[:, 0:128], pp[:, base:base + 1], None, ALU.mult, None, [src, pp], [dst])
                yield
                stt(dst[:], src[:, 1:129], pp[:, base + 4:base + 5], dst[:], ALU.mult, ALU.add, [src, pp, dst], [dst])
                yield
                stt(dst[:], src[:, 2:130], pp[:, base + 8:base + 9], dst[:], ALU.mult, ALU.add, [src, pp, dst], [dst])
                yield
            Gs = W["Gs"]
            S.op("dve", lambda h: h.tensor_tensor_scan(Gs[:], onesf[:], sg[:], 0.0, ALU.mult, ALU.add), [onesf, sg], [Gs])
            yield
            if d_ == 0:
                G = Gs
            else:
                G = W["G2"]
                stt(G[:], Gs[:], -1.0, sg[:], ALU.mult, ALU.add, [Gs, sg], [G])
                yield
                ts("dve", G[:], G[:], Gs[:, 127:128], None, ALU.add, None, [G, Gs], [G])
            ts("dve", sm[:, 0:1], Gs[:, 127:128], -C0, None, ALU.mult, None, [Gs], [sm])
            yield
            Gx = W["Gx"]
            tt("pool", Gx[:], G[:], sg[:], ALU.subtract, [G, sg], [Gx])
            eG, enG, eGx, eGC = W["eG"], W["enG"], W["eGx"], W["eGC"]
            act(eG[:], G[:], AF.Exp, [G], [eG], scale=-C0)
            yield
            act(enG[:], G[:], AF.Exp, [G], [enG], scale=C0)
            yield
            act(eGx[:], Gx[:], AF.Exp, [Gx], [eGx], scale=-C0)
            yield
            act(eGC[:], G[:], AF.Exp, [G, sm], [eGC], scale=C0, bias=sm[:, 0:1])
            act(sm[:, 1:2], sm[:, 0:1], AF.Exp, [sm], [sm])
            yield
            kk, sq, sqb = W["kk"], W["sq"], W["sqb"]
            ts("dve", kk[:], k_[:], pp[:, 86 + ct:87 + ct], None, ALU.mult, None, [k_, pp], [kk])
            yield
            act(sqb[:], kk[:], AF.Square, [kk], [sqb])
            yield
            pn = psum()
            mm(pn[:, 0:128], blk64_b, sqb[:], [cb, sqb], [pn])
            act(sq[:], pn[:, 0:128], AF.Sqrt, [pn], [sq])
            yield
            ts("dve", sq[:], sq[:], 1e-12, None, ALU.max, None, [sq], [sq])
            yield
            recip(sq[:], sq[:], [sq], [sq])
            yield
            tt("dve", kk[:], kk[:], sq[:], ALU.mult, [kk, sq], [kk])
            yield
            b_, kd = W["b"], W["kd"]
            tt("pool", b_[:], kk[:], a_[:], ALU.mult, [kk, a_], [b_])
            ts("dve", kd[:], a_[:], pp[:, 90 + ct:91 + ct], omka[:, ct:ct + 1], ALU.mult, ALU.add, [a_, pp, omka], [kd])
            yield
            tt("pool", kd[:], kd[:], k_[:], ALU.mult, [kd, k_], [kd])
            stt(AR[:, 0, :], kk[:], -1.0, eGx[:], ALU.mult, ALU.mult, [kk, eGx], [AR])
            yield
            tt("dve", Rtf[:], r_[:], eG[:], ALU.mult, [r_, eG], [Rtf])
            yield
            cp("act", AR[:, 1, :], Rtf[:], [Rtf], [AR])
            BhT, KhT, vb = W["BhT"], W["KhT"], W["vb"]
            tt("pool", Bt[:], b_[:], enG[:], ALU.mult, [b_, enG], [Bt])
            tt("dve", Kt[:], kd[:], enG[:], ALU.mult, [kd, enG], [Kt])
            yield
            tt("pool", BhT[:], b_[:], eGC[:], ALU.mult, [b_, eGC], [BhT])
            tt("dve", KhT[:], kd[:], eGC[:], ALU.mult, [kd, eGC], [KhT])
            cp("act", vb[:], v_[:], [v_], [vb])
            yield
            ptr = psum()
            ptb = ptr.h[:].bitcast(BF16)
            tr(ptb[:, 0:128], AR[:, 0, :], ident_b, [AR, cb], [ptr])
            tr(ptb[:, 128:256], BhT[:], ident_b, [BhT, cb], [ptr])
            tr(ptb[:, 256:384], KhT[:], ident_b, [KhT, cb], [ptr])
            tr(ptb[:, 384:512], vb[:], ident_b, [vb, cb], [ptr])
            cp("dve", TK[:].rearrange("p a b -> p (a b)"), ptb[:, 0:384], [ptr], [TK])
            cp("dve", VT[:, ct, :], ptb[:, 384:512], [ptr], [VT])
            yield
            if d_ == 1:
                rkk = W["rkk"]
                stt(rkk[:], r_[:], pp[:, 98 + ct:99 + ct], k_[:], ALU.mult, ALU.mult, [r_, pp, k_], [rkk])
                pbt = psum()
                mm(pbt[:, 0:2], rkk[:], hsel_b, [rkk, cb], [pbt])
                cp("act", beta[:, ct * 2:ct * 2 + 2], pbt[:, 0:2], [pbt], [beta])
                yield

        def pair_gen(d_, par, ct, ybk):
            P = PS[par][ct]
            AR, Bt, Kt, TK, Rtf, sm = P["AR"], P["Bt"], P["Kt"], P["TK"], P["Rtf"], P["sm"]
            VT = VTs[par]
            Q = SL[ct]
            M1, M2, X = Q["M1"], Q["M2"], Q["X"]
            ARf = AR[:].rearrange("p a b -> p (a b)")
            HP = [slice(0, 64), slice(64, 128)]
            v3 = lambda ap, e: ap.rearrange("p (h e) -> p h e", e=e)
            bc2 = lambda ap, e: ap.unsqueeze(1).to_broadcast([128, 2, e])
            for hh in range(2):
                hp = HP[hh]
                p1 = psum()
                mm(p1[:, 0:256], Bt[hp, :], ARf[hp, :], [Bt, AR], [p1])
                cp("act", M1[:, hh, :], p1[:, 0:256], [p1], [M1])
                yield
            for hh in range(2):
                hp = HP[hh]
                p2 = psum()
                mm(p2[:, 0:256], Kt[hp, :], ARf[hp, :], [Kt, AR], [p2])
                tt("dve", M2[:, hh, :], p2[:, 0:256], maskT[d_], ALU.mult, [p2, cf], [M2])
                yield
            for hh in range(2):
                hp = HP[hh]
                p3 = psum()
                mm(p3[:, 0:128], AR[hp, 0, :], Bt[hp, :], [AR, Bt], [p3])
                cp("act", X[:, hh, :], p3[:, 0:128], [p3], [X])
                yield
            tt("pool", M1[:, :, 128:256], M1[:, :, 128:256], bc2(maskT[d_][:, 128:256], 128), ALU.mult, [M1, cf], [M1])
            LTap = M1[:, :, 0:128]
            T_ = Q["T"][0]
            TT = Q["TT"][0]
            Lm = Q["Lm"]
            tt("pool", Lm[6][:], LTap, bc2(mLT(d_, 0), 128), ALU.mult, [M1, cb], [Lm[6]])
            tt("pool", TT[:], Lm[6][:], bc2(ident_b, 128), ALU.add, [Lm[6], cb], [TT])
            tt("pool", Lm[7][:], X[:], bc2(mLT(1 - d_, 0), 128), ALU.mult, [X, cb], [Lm[7]])
            tt("pool", T_[:], Lm[7][:], bc2(ident_b, 128), ALU.add, [Lm[7], cb], [T_])
            yield
            for lev in range(1, 7):
                tt("pool", Lm[lev - 1][:], LTap, bc2(mLT(d_, lev), 128), ALU.mult, [M1, cb], [Lm[lev - 1]])
                if lev % 2 == 0:
                    yield
            for lev in range(1, 7):
                LmT = Lm[lev - 1]
                pa = psum()
                for hh in range(2):
                    mm(pa[:, hh * 128:(hh + 1) * 128], LmT[:, hh, :], T_[:, hh, :], [LmT, T_], [pa], start=True, stop=False)
                    mm(pa[:, hh * 128:(hh + 1) * 128], ident_b, ident_b, [cb], [pa], start=False, stop=True)
                Wt = Q["W"][lev % 2]
                cp("act" if lev % 2 else "dve", Wt[:].rearrange("p h e -> p (h e)"), pa[:, 0:256], [pa], [Wt])
                yield
                if lev < 6:
                    pT = psum()
                    for hh in range(2):
                        mm(pT[:, hh * 128:(hh + 1) * 128], TT[:, hh, :], Wt[:, hh, :], [TT, Wt], [pT])
                    Tn = Q["T"][lev % 2]
                    cp("act", Tn[:].rearrange("p h e -> p (h e)"), pT[:, 0:256], [pT], [Tn])
                pTT = psum()
                for hh in range(2):
                    mm(pTT[:, hh * 128:(hh + 1) * 128], Wt[:, hh, :], TT[:, hh, :], [Wt, TT], [pTT])
                TTn = Q["TT"][lev % 2]
                if lev < 6:
                    cp("dve", TTn[:].rearrange("p h e -> p (h e)"), pTT[:, 0:256], [pTT], [TTn])
                else:
                    cp("act", TTn[:].rearrange("p h e -> p (h e)"), pTT[:, 0:256], [pTT], [TTn])
                TT = TTn
                if lev < 6:
                    T_ = Tn
                yield
            UA, UP, Phi, Psi, QT = Q["UA"], Q["UP"], Q["Phi"], Q["Psi"], Q["QT"]
            pwq = psum()
            for hh in range(2):
                mm(pwq[:, hh * 64:(hh + 1) * 64], M2[:, hh, 0:128], VT[:, ct, HP[hh]], [M2, VT], [pwq])
            cp("act", UA[:, :, 0:64], v3(pwq[:, 0:128], 64), [pwq], [UA])
            cp("pool", UA[:, :, 64:128], v3(TK[:, 0, :], 64), [TK], [UA])
            yield
            pu = psum()
            for hh in range(2):
                mm(pu[:, hh * 128:(hh + 1) * 128], TT[:, hh, :], UA[:, hh, :], [TT, UA], [pu])
            cp("act", UP[:].rearrange("p h e -> p (h e)"), pu[:, 0:256], [pu], [UP])
            yield
            pf = psum()
            for hh in range(2):
                hp = HP[hh]
                mm(pf[hp, 0:64], UP[:, hh, 64:128], TK[:, 1, hp], [UP, TK], [pf])
                mm(pf[hp, 64:128], TK[:, 1, hp], UP[:, hh, 0:64], [TK, UP], [pf], start=True, stop=False)
                mm(pf[hp, 64:128], TK[:, 2, hp], VT[:, ct, hp], [TK, VT], [pf], start=False, stop=True)
                mm(pf[hp, 128:256], UP[:, hh, 64:128], M1[:, hh, 128:256], [UP, M1], [pf], start=True, stop=False)
                mm(pf[hp, 128:256], ident_b[:, hp], AR[:, 1, :], [cb, AR], [pf], start=False, stop=True)
            stt(Phi[:], identfold, sm[:, 1:2], pf[:, 0:64], ALU.mult, ALU.add, [cf, sm, pf], [Phi])
            cp("act", Psi[:], pf[:, 64:128], [pf], [Psi])
            cp("act", QT[:], pf[:, 128:256], [pf], [QT])
            yield
            for hh in range(2):
                hp = HP[hh]
                yc0 = ct * 128 + hh * 64
                mm(ybk[:, yc0:yc0 + 64], M1[:, hh, 128:256], UP[:, hh, 0:64], [M1, UP], [ybk], start=True, stop=False)
                mm(ybk[:, yc0:yc0 + 64], M2[:, hh, 128:256], VT[:, ct, hp], [M2, VT], [ybk], start=False, stop=False)
                sbuf_ = STB[d_][ct][hh]
                mm(ybk[:, yc0:yc0 + 64], QT[hp, :], STb[d_][hp, ct, :], [QT, sbuf_], [ybk], start=False, stop=True)
                pS = psum()
                mm(pS[hp, 0:64], Phi[hp, :], STb[d_][hp, ct, :], [Phi, sbuf_], [pS], start=True, stop=False)
                mm(pS[hp, 0:64], ident_b[hp, hp], Psi[hp, :], [cb, Psi], [pS], start=False, stop=True)
                cp("act" if hh else "dve", STb[d_][hp, ct, :], pS[hp, 0:64], [pS], [sbuf_])
                yield

        def readout(d_, c_, par, ybk):
            t0 = c_ * 128
            VT = VTs[par]
            beta = betas[par]
            if d_ == 0:
                yo = W_yo.get()
                cp("act", yo[:], ybk[:], [ybk], [yo])
                S.dma("pool", yfw[t0:t0 + 128, :], yo[:], [yo], [])
                return
            yo = W_yo.get()
            S.dma("sp", yo[:], yfw[t0:t0 + 128, :], [], [yo])
            zc_ = W_zc.get()
            S.dma("sp", zc_[:], zcv[:, c_, :], [], [zc_])
            y = W_y.get()
            tt("dve", y[:], ybk[:], yo[:], ALU.add, [ybk, yo], [y])
            y3 = y[:].rearrange("p (a b) -> p a b", b=64)
            st = W_st.get()
            S.op("dve", lambda h: h.tensor_reduce(st[:, 0:8], y3, AX.X, ALU.add), [y], [st])
            ysq = W_ysq.get()
            act(ysq[:], y[:], AF.Square, [y], [ysq])
            S.op("dve", lambda h: h.tensor_reduce(st[:, 8:16], ysq[:].rearrange("p (a b) -> p a b", b=64), AX.X, ALU.add),
                 [ysq], [st])
            ts("dve", st[:, 0:16], st[:, 0:16], 1.0 / 64, None, ALU.mult, None, [st], [st])
            tt("dve", st[:, 16:24], st[:, 0:8], st[:, 0:8], ALU.mult, [st], [st])
            tt("dve", st[:, 24:32], st[:, 8:16], st[:, 16:24], ALU.subtract, [st], [st])
            act(st[:, 32:40], st[:, 24:32], AF.Sqrt, [st, epsT], [st], bias=epsT[:, 2:3], scale=1.0)
            recip(st[:, 40:48], st[:, 32:40], [st], [st])
            tt("dve", y3, y3, st[:, 0:8].unsqueeze(2).to_broadcast([128, 8, 64]), ALU.subtract, [y, st], [y])
            tt("dve", y3, y3, st[:, 40:48].unsqueeze(2).to_broadcast([128, 8, 64]), ALU.mult, [y, st], [y])
            tt("pool", y[:], y[:], bcs[:, 1664:2176], ALU.mult, [y, bcs], [y])
            tt("pool", y[:], y[:], bcs[:, 2176:2688], ALU.add, [y, bcs], [y])
            bon = ysq
            tt("dve", bon[:].rearrange("p (a b) -> p a b", b=64), VT[:].rearrange("p c (h e) -> p (c h) e", e=64),
               beta[:, 0:8].unsqueeze(2).to_broadcast([128, 8, 64]), ALU.mult, [VT, beta], [bon])
            tt("pool", y[:], y[:], bon[:], ALU.add, [y, bon], [y])
            ycz = W_ycz.get()
            tt("dve", ycz[:], y[:], zc_[:], ALU.mult, [y, zc_], [ycz])
            ptr = psum()
            ptb = ptr.h[:].bitcast(BF16)
            for ct in range(4):
                tr(ptb[:, ct * 128:(ct + 1) * 128], ycz[:, ct * 128:(ct + 1) * 128], ident_b, [ycz, cb], [ptr])
            yco = W_yco.get()
            cp("act", yco[:].rearrange("p a b -> p (a b)"), ptb[:, 0:512], [ptr], [yco])
            S.dma("pool", ycTv[:, :, t0:t0 + 128], yco[:], [yco], [])

        def lockstep(gens):
            while gens:
                nxt = []
                for g in gens:
                    try:
                        next(g)
                        nxt.append(g)
                    except StopIteration:
                        pass
                gens = nxt

        def preps(d_, c_, par):
            lab = prep_chunk(d_, c_)
            return [prep_gen(d_, c_, par, ct, lab) for ct in range(4)]

        ybank_i = 0
        for d_ in range(2 if upto >= 5 else 0):
            memset("dve", STb[d_][:], 0.0, [STb[d_]] + [STB[d_][c4][h2] for c4 in range(4) for h2 in range(2)])
            order = [0, 1] + list(range(2, NTILE)) if d_ == 0 else [1, 0] + list(range(NTILE - 1, 1, -1))
            order = order[:int(_os.environ.get('RW_CHUNKS', '99'))]
            lockstep(preps(d_, order[0], 0))
            for i, c_ in enumerate(order):
                par = i % 2
                ybk = banks[6 + (ybank_i % 2)]
                ybank_i += 1
                gens = [pair_gen(d_, par, ct, ybk) for ct in range(4)]
                if i + 1 < len(order):
                    gens = gens + preps(d_, order[i + 1], 1 - par)
                lockstep(gens)
                readout(d_, c_, par, ybk)
            S.barrier()
        A.pop()

        A.push()
        pstate["nb"] = 8
        wbr = A.sb([128, 3, 4, D], BF16, "wbr")
        wo = A.sb([128, 8, D], BF16, "wo")
        for i in range(3):
            S.dma("pool", wbr[:, i, :, :], w_br[l, i].rearrange("(ct p) d -> p ct d", p=128), [], [wbr])
        S.dma("pool", wo[:], w_out[l].rearrange("(kc p) e -> p kc e", p=128), [], [wo])
        ybl = [Rot([A.sb([128, 4, 512], BF16, "ybl%d" % i) for _ in range(2)]) for i in range(3)]
        gbl = Rot([A.sb([128, 512], BF16, "gbl") for _ in range(6)])
        mT = Rot([A.sb([128, 8, 512], BF16, "mT") for _ in range(2)])
        macc = Rot([A.sb([128, 512], F32, "macc") for _ in range(3)])
        mtmp = Rot([A.sb([128, 512], F32, "mtmp") for _ in range(3)])
        xin = Rot([A.sb([128, 8, 512], F32, "mxin") for _ in range(2)])
        xo = Rot([A.sb([128, 512], F32, "mxo") for _ in range(3)])
        srcs = [yaT, ybT, ycT]
        xv = xcur.rearrange("(kc p) t -> p kc t", p=128)
        for bi, (t0, n) in enumerate(BLOCKS):
            if (last and bi == 0) or upto < 6:
                continue
            j = 1 if bi == 0 else 0
            yb3 = []
            for i in range(3):
                y_ = ybl[i].get()
                S.dma("sp", y_[:, :, :n], srcs[i].rearrange("(ct p) t -> p ct t", p=128)[:, :, t0:t0 + n], [], [y_])
                yb3.append(y_)
            x_ = xin.get()
            S.dma("sp", x_[:, :, :n], xv[:, :, t0:t0 + n], [], [x_])
            m_ = mT.get()
            for dt_ in range(8):
                ma = macc.get()
                for i in range(3):
                    g_ = gbl.get()
                    r0 = i * D + dt_ * 128
                    S.dma("sp", g_[:, :n], gT[r0:r0 + 128, t0:t0 + n], [], [g_])
                    pb = psum()
                    for ct in range(4):
                        mm(pb[:, :n], wbr[:, i, ct, dt_ * 128:(dt_ + 1) * 128], yb3[i][:, ct, :n], [wbr, yb3[i]], [pb],
                           start=(ct == 0), stop=(ct == 3))
                    if i == 0:
                        tt("dve", ma[:, :n], pb[:, :n], g_[:, :n], ALU.mult, [pb, g_], [ma])
                    else:
                        tp = mtmp.get()
                        tt("dve", tp[:, :n], pb[:, :n], g_[:, :n], ALU.mult, [pb, g_], [tp])
                        if i == 1:
                            tt("pool", ma[:, :n], ma[:, :n], tp[:, :n], ALU.add, [ma, tp], [ma])
                        else:
                            tt("pool", m_[:, dt_, :n], ma[:, :n], tp[:, :n], ALU.add, [ma, tp], [m_])
            for et in range(8):
                pb = psum()
                for dt_ in range(8):
                    mm(pb[:, :n], wo[:, dt_, et * 128:(et + 1) * 128], m_[:, dt_, :n], [wo, m_], [pb],
                       start=(dt_ == 0), stop=(dt_ == 7))
                o = xo.get()
                stt(o[:, :n], pb[:, :n], mod[:, 16 + et, j:j + 1], x_[:, et, :n], ALU.mult, ALU.add, [pb, mod, x_], [o])
                if last:
                    S.dma("pool", yT[et * 128:(et + 1) * 128, t0 - 256:t0 - 256 + n], o[:, :n], [o], [])
                else:
                    S.dma("pool", xnext[et * 128:(et + 1) * 128, t0:t0 + n], o[:, :n], [o], [])
        A.pop()
        A.pop()

    S.barrier()
    if dbg:
        A.push()
        for nm, (src, dst) in dbg_out.items():
            if len(src.shape) == 3:
                for a in range(src.shape[0]):
                    S.dma("sp", dst[a], src[a], [], [])
            else:
                S.dma("sp", dst, src, [], [])
        A.pop()
    if NL < DEPTH and upto >= 6:
        S.dma("sp", yT[:, :], xTs[(NL - 1) % 2][:, 256:TALL], [], [])
    A.pop()
    S.barrier()
    return nc, S


def _consts():
    cf = np.zeros((128, NCF), np.float32)
    cf[:, 0:128] = np.eye(128)
    blk = np.zeros((128, 128), np.float32)
    blk[:64, :64] = 1
    blk[64:, 64:] = 1
    cf[:, 128:256] = blk
    cf[:64, 256] = 1
    cf[64:, 257] = 1
    idx = np.arange(128)
    fw_strict = (idx[:, None] < idx[None, :]).astype(np.float32)
    bw_strict = (idx[:, None] > idx[None, :]).astype(np.float32)
    eye = np.eye(128, dtype=np.float32)
    cf[:, 260:388] = fw_strict
    cf[:, 388:516] = fw_strict + eye
    cf[:, 516:644] = bw_strict
    cf[:, 644:772] = bw_strict + eye
    cf[:, 772:900] = fw_strict.T
    cf[:, 900:1028] = bw_strict.T
    cf[:, 1028:1092] = eye[:, 0:64] + eye[:, 64:128]
    cb = np.zeros((128, NCB), np.float32)
    cb[:, 0:128] = np.eye(128)
    cb[:, 128:256] = 1
    cb[:, 256:384] = blk
    rotT = np.zeros((128, 128), np.float32)
    for m in range(128):
        f = m % 64
        base = m - f
        blk32 = (f // 32) * 32
        g = f % 32
        partner = base + blk32 + (g + 16 if g < 16 else g - 16)
        rotT[partner, m] = 1
    cb[:, 384:512] = rotT
    cb[:64, 512] = 1
    cb[64:, 513] = 1
    for d in range(2):
        for lev in range(7):
            b = 1 << lev
            blk = idx // b
            blk2 = idx // (2 * b)
            same2 = blk2[:, None] == blk2[None, :]
            diff = blk[:, None] != blk[None, :]
            before = (idx[:, None] < idx[None, :]) if d == 0 else (idx[:, None] > idx[None, :])
            m = (same2 & diff & before).astype(np.float32)
            c0 = 516 + (d * 7 + lev) * 128
            cb[:, c0:c0 + 128] = m
    nf = 16
    inv = (10000.0 ** (-np.arange(nf, dtype=np.float32) / nf)).astype(np.float32)
    tpos = np.arange(TLAT)
    row = (tpos // 64).astype(np.float32)
    col = (tpos % 64).astype(np.float32)
    C = np.ones((128, TALL), np.float32)
    Sg = np.zeros((128, TALL), np.float32)
    for p in range(128):
        f = p % 64
        pos = row if f < 32 else col
        g = f % 32
        i = g % 16
        ang = pos * inv[i]
        C[p, 256:] = np.cos(ang)
        Sg[p, 256:] = (-np.sin(ang)) if g < 16 else np.sin(ang)
    return cf, cb, C, Sg


def _host_inputs(inputs):
    f = lambda a: np.ascontiguousarray(np.asarray(a, dtype=np.float32))
    L = DEPTH
    pp = np.zeros((L, 128, NPP), np.float32)
    bc = np.zeros((L, 128, NBC), np.float32)
    b_mod = f(inputs["b_mod"])
    norm_g = f(inputs["norm_g"])
    r_conv = f(inputs["r_conv"])
    for l in range(L):
        pp[l, :, 0:24] = b_mod[l].reshape(24, 128).T
        pp[l, :, 24:32] = norm_g[l].reshape(8, 128).T
        pp[l, :, 32] = np.tile(f(inputs["d_qnorm"])[l], 2)
        pp[l, :, 33] = np.tile(f(inputs["d_knorm"])[l], 2)
        for which in range(3):
            for tap in range(3):
                pp[l, :, 34 + which * 12 + tap * 4:34 + which * 12 + tap * 4 + 4] = r_conv[l, which, tap].reshape(4, 128).T
        for d_ in range(2):
            pp[l, :, 70 + d_ * 4:74 + d_ * 4] = f(inputs["r_w0"])[l, d_].reshape(4, 128).T
            pp[l, :, 78 + d_ * 4:82 + d_ * 4] = f(inputs["r_a0"])[l, d_].reshape(4, 128).T
        pp[l, :, 86:90] = f(inputs["r_kk"])[l].reshape(4, 128).T
        pp[l, :, 90:94] = f(inputs["r_ka"])[l].reshape(4, 128).T
        pp[l, :, 98:102] = f(inputs["r_rk"])[l].reshape(4, 128).T
        row = np.concatenate([f(inputs["a_ln_g"])[l], f(inputs["a_ln_b"])[l], f(inputs["a_bs"])[l].reshape(-1),
                              f(inputs["d_subln_g"])[l], f(inputs["r_ln_g"])[l], f(inputs["r_ln_b"])[l],
                              f(inputs["d_lam"])[l].reshape(-1)])
        bc[l] = np.broadcast_to(row[None, :], (128, NBC))
    cf, cb, C, Sg = _consts()
    shared = dict(pp=pp, bc=bc, cf=cf, cbf=cb, ropeC=C, ropeS=Sg,
                  w_mod=f(inputs["w_mod"]), w_in=f(inputs["w_in"]),
                  a_wsT=np.ascontiguousarray(np.transpose(f(inputs["a_ws"]), (0, 1, 3, 2))),
                  r_w2=f(inputs["r_w2"]), r_a2=f(inputs["r_a2"]), w_br=f(inputs["w_br"]), w_out=f(inputs["w_out"]))
    x = f(inputs["x"])
    c = f(inputs["c"])
    ctx = f(inputs["ctx"])
    c_ctx = f(inputs["c_ctx"])
    maps = []
    for b in range(4):
        xT0 = np.ascontiguousarray(np.concatenate([ctx[b], x[b]], axis=0).T)
        cc = np.stack([c[b].reshape(8, 128).T, c_ctx.reshape(8, 128).T], axis=-1)
        m = dict(shared)
        m["xT0"] = xT0
        m["cc"] = np.ascontiguousarray(cc)
        maps.append(m)
    return maps


_CACHE = {}


def kernel(**inputs):
    maps = _host_inputs(inputs)
    if "nc" not in _CACHE:
        _CACHE["nc"] = build()[0]
    nc = _CACHE["nc"]
    in_maps = [maps[i % 4] for i in range(8)]
    res = run_bass_kernel_spmd(nc, in_maps, core_ids=list(range(8)))
    out = np.stack([np.ascontiguousarray(res.results[b]["yT"].T) for b in range(4)], axis=0)
    return out.astype(np.float32)
```

```python
import numpy as np
import concourse.bass as bass
import concourse.mybir as mybir
from concourse.bass_utils import run_bass_kernel_spmd

F32 = mybir.dt.float32
BF16 = mybir.dt.bfloat16
AF = mybir.ActivationFunctionType
ALU = mybir.AluOpType
AX = mybir.AxisListType

D = 1024
NCTX = 256
TLAT = 4096
TALL = NCTX + TLAT
NTILE = TALL // 128
DEPTH = 4
INC = 8960
C0 = 0.6065306597126334
BLOCKS = [(0, 256)] + [(256 + 512 * i, 512) for i in range(8)]
NPP = 104
NBC = 2944
NCF = 1092
NCB = 2308
NDS = 32


class Buf:
    __slots__ = ("w", "r", "excl")

    def __init__(self):
        self.w = None
        self.r = {}
        self.excl = False


class TileT:
    def __init__(self, h, nb=1):
        self.h = h
        self.b = Buf()

    def __getitem__(self, k):
        return self.h[k]


class Sched:
    def __init__(self, nc):
        self.nc = nc
        self.sems = []
        self.eng = {}
        for name, h in (("pe", nc.tensor), ("act", nc.scalar), ("dve", nc.vector), ("pool", nc.gpsimd), ("sp", nc.sync)):
            sid = len(self.sems)
            self.sems.append(nc.alloc_semaphore("s_" + name))
            self.eng[name] = dict(h=h, sid=sid, n=0, waited={})
        self.dq = {}
        for q in ("sp", "pool", "act"):
            ids = []
            for i in range(NDS):
                ids.append(len(self.sems))
                self.sems.append(nc.alloc_semaphore("d%s%d" % (q, i)))
            self.dq[q] = dict(ids=ids, nxt=0)
        self.dcnt = {}
        self.nins = 0
        import os as _os2
        self.maxops = int(_os2.environ.get('MAXOPS', '1000000000'))
        self.k = 0

    def _wait(self, e, toks):
        E = self.eng[e]
        need = {}
        for t in toks:
            if t is None:
                continue
            sid, val = t
            if sid == E["sid"] and e == "pe":
                continue
            if need.get(sid, 0) < val:
                need[sid] = val
        for sid, val in need.items():
            if E["waited"].get(sid, 0) < val:
                E["h"].wait_ge(self.sems[sid], val)
                E["waited"][sid] = val
                self.nins += 1

    def _deps(self, reads, writes):
        toks = []
        for b in reads:
            toks.append(b.w)
        for b in writes:
            toks.append(b.w)
            toks.extend(b.r.items())
        return toks

    def _mark(self, tok, reads, writes):
        for b in reads:
            if b.r.get(tok[0], 0) < tok[1]:
                b.r[tok[0]] = tok[1]
        for b in writes:
            b.w = tok
            b.r = {}

    def op(self, e, emit, reads=(), writes=()):
        self.k += 1
        if self.k > self.maxops:
            return
        reads = [x.b if isinstance(x, TileT) else x for x in reads]
        writes = [x.b if isinstance(x, TileT) else x for x in writes]
        writes = writes + [b for b in reads if b.excl]
        reads = [b for b in reads if not b.excl]
        self._wait(e, self._deps(reads, writes))
        E = self.eng[e]
        E["n"] += 1
        ins = emit(E["h"])
        ins.then_inc(self.sems[E["sid"]], 1)
        self.nins += 1
        self._mark((E["sid"], E["n"]), reads, writes)

    def dma(self, q, out, in_, reads=(), writes=()):
        self.k += 1
        if self.k > self.maxops:
            return
        reads = [x.b if isinstance(x, TileT) else x for x in reads]
        writes = [x.b if isinstance(x, TileT) else x for x in writes]
        self._wait(q, self._deps(reads, writes))
        Q = self.dq[q]
        sid = Q["ids"][Q["nxt"] % NDS]
        Q["nxt"] += 1
        prev = self.dcnt.get(sid, 0)
        if prev > 0:
            self._wait(q, [(sid, 16 * prev)])
        self.dcnt[sid] = prev + 1
        self.eng[q]["h"].dma_start(out=out, in_=in_).then_inc(self.sems[sid], 16)
        self.nins += 1
        self._mark((sid, 16 * (prev + 1)), reads, writes)

    def barrier(self):
        toks = [(E["sid"], E["n"]) for E in self.eng.values() if E["n"] > 0]
        toks += [(sid, 16 * c) for sid, c in self.dcnt.items()]
        for e in self.eng:
            E = self.eng[e]
            for sid, val in toks:
                if sid == E["sid"]:
                    continue
                if E["waited"].get(sid, 0) < val:
                    E["h"].wait_ge(self.sems[sid], val)
                    E["waited"][sid] = val


class Alloc:
    def __init__(self, nc, S):
        self.nc = nc
        self.S = S
        self.stack = []
        self.cnt = 0

    def push(self):
        self.stack.append([])

    def pop(self):
        self.S.barrier()
        for cm in reversed(self.stack.pop()):
            cm.__exit__(None, None, None)

    def sb(self, shape, dt, name=None):
        self.cnt += 1
        cm = self.nc.sbuf_tensor("%s_%d" % (name or "t", self.cnt), list(shape), dt)
        h = cm.__enter__()
        self.stack[-1].append(cm)
        return TileT(h)


def build(NL=DEPTH, dbg=None, upto=9):
    import os as _os
    lo = int(_os.environ.get('SKIP_TO', '0'))
    RWCUT = float(_os.environ.get('RW_CUT', '99'))
    nc = bass.Bass("TRN2", target_bir_lowering=False)
    S = Sched(nc)
    A = Alloc(nc, S)

    def din(name, shape, dt=F32):
        return nc.dram_tensor(name, list(shape), dt, kind="ExternalInput").ap()

    def dscr(name, shape, dt=F32):
        return nc.dram_tensor(name, list(shape), dt, kind="Internal").ap()

    xT0 = din("xT0", [D, TALL])
    cc_d = din("cc", [128, 8, 2])
    pp_d = din("pp", [DEPTH, 128, NPP])
    bc_d = din("bc", [DEPTH, 128, NBC])
    cf_d = din("cf", [128, NCF])
    cb_d = din("cbf", [128, NCB])
    ropeC_d = din("ropeC", [128, TALL])
    ropeS_d = din("ropeS", [128, TALL])
    w_mod = din("w_mod", [DEPTH, D, 3 * D])
    w_in = din("w_in", [DEPTH, D, INC])
    a_wsT = din("a_wsT", [DEPTH, 4, 128, 128])
    r_w2 = din("r_w2", [DEPTH, 2, 64, 512])
    r_a2 = din("r_a2", [DEPTH, 2, 64, 512])
    w_br = din("w_br", [DEPTH, 3, 512, D])
    w_out = din("w_out", [DEPTH, D, D])
    yT = nc.dram_tensor("yT", [D, TLAT], F32, kind="ExternalOutput").ap()

    xTs = [dscr("xTa", [D, TALL]), dscr("xTb", [D, TALL])]
    qT = dscr("qT", [4, 128, TALL], BF16)
    kT = dscr("kT", [4, 128, TALL], BF16)
    Vd = dscr("Vd", [TALL, 512], BF16)
    krT = dscr("krT", [512, TALL])
    vrT = dscr("vrT", [512, TALL])
    rrT = dscr("rrT", [512, TALL])
    wlT = dscr("wlT", [128, TALL])
    alT = dscr("alT", [128, TALL])
    uzT = dscr("uzT", [512, TALL], BF16)
    vln = dscr("vln", [TALL, 512], BF16)
    zbd = dscr("zbd", [TALL, 512], BF16)
    zcd = dscr("zcd", [TALL, 512], BF16)
    gT = dscr("gT", [3 * D, TALL], BF16)
    yaT = dscr("yaT", [512, TALL], BF16)
    ybT = dscr("ybT", [512, TALL], BF16)
    ycT = dscr("ycT", [512, TALL], BF16)
    yfw = dscr("yfw", [TALL, 512])
    dbg_out = {}
    if dbg:
        for nm in dbg:
            src = dict(qT=qT, kT=kT, Vd=Vd, krT=krT, vrT=vrT, rrT=rrT, wlT=wlT, alT=alT, uzT=uzT, vln=vln, zbd=zbd,
                       zcd=zcd, gT=gT, yaT=yaT, ybT=ybT, ycT=ycT, yfw=yfw, xTa=xTs[0], xTb=xTs[1])[nm]
            dbg_out[nm] = (src, nc.dram_tensor("dbg_" + nm, list(src.shape), src.dtype, kind="ExternalOutput").ap())

    banks = [TileT(nc.alloc_psum_tensor("ps%d" % i, [128, 512], F32)) for i in range(8)]
    for b_k in banks:
        b_k.b.excl = True
    pstate = dict(i=0, nb=8)

    def psum():
        b = banks[pstate["i"] % pstate["nb"]]
        pstate["i"] += 1
        return b

    def mm(out, lhsT, rhs, R, W, start=True, stop=True):
        S.op("pe", lambda h: h.matmul(out, lhsT, rhs, start=start, stop=stop), R, W)

    def tr(out, in_, ident, R, W):
        S.op("pe", lambda h: h.transpose(out, in_, ident), R, W)

    def act(out, in_, func, R, W, bias=None, scale=None, accum=None, eng="act"):
        kw = {}
        if bias is not None:
            kw["bias"] = bias
        if scale is not None:
            kw["scale"] = scale
        if accum is not None:
            kw["accum_out"] = accum
        S.op("act", lambda h: h.activation(out, in_, func, **kw), R, W)

    def tt(e, out, in0, in1, op, R, W):
        S.op(e, lambda h: h.tensor_tensor(out, in0, in1, op), R, W)

    def ts(e, out, in0, s1, s2, op0, op1, R, W):
        if s2 is None:
            S.op(e, lambda h: h.tensor_scalar(out, in0, s1, None, op0), R, W)
        else:
            S.op(e, lambda h: h.tensor_scalar(out, in0, s1, s2, op0, op1), R, W)

    def stt(out, in0, sc, in1, op0, op1, R, W):
        S.op("dve", lambda h: h.scalar_tensor_tensor(out, in0, sc, in1, op0, op1), R, W)

    def cp(e, out, in_, R, W):
        if e == "act":
            S.op("act", lambda h: h.copy(out, in_), R, W)
        else:
            S.op(e, lambda h: h.tensor_copy(out, in_), R, W)

    def recip(out, in_, R, W):
        S.op("dve", lambda h: h.reciprocal(out, in_), R, W)

    def memset(e, ap, val, W):
        S.op(e, lambda h: h.memset(ap, val), [], W)

    class Rot:
        def __init__(self, tiles):
            self.t = tiles
            self.i = 0

        def get(self):
            t = self.t[self.i % len(self.t)]
            self.i += 1
            return t

    A.push()
    cf = A.sb([128, NCF], F32, "cf")
    cb = A.sb([128, NCB], BF16, "cb")
    cc = A.sb([128, 8, 2], F32, "cc")
    cact = A.sb([128, 8, 2], F32, "cact")
    epsT = A.sb([128, 4], F32, "eps")
    onesf = A.sb([128, 128], F32, "onesf")
    S.dma("sp", cf[:], cf_d[:, :], [], [cf])
    S.dma("pool", cb[:], cb_d[:, :], [], [cb])
    S.dma("sp", cc[:], cc_d[:, :, :], [], [cc])
    act(cact[:], cc[:], AF.Silu, [cc], [cact])
    memset("dve", epsT[:, 0:1], 1e-6, [epsT])
    memset("dve", epsT[:, 1:2], 1e-5, [epsT])
    memset("dve", epsT[:, 2:3], 64e-5, [epsT])
    memset("dve", epsT[:, 3:4], 0.0, [epsT])
    memset("dve", onesf[:], 1.0, [onesf])
    ident_f = lambda ps, cs: cf.h[ps, cs]
    blk64_f = cf.h[:, 128:256]
    hsel_f = cf.h[:, 256:258]
    identfold = cf.h[:, 1028:1092]
    maskT = [cf.h[:, 260:516], cf.h[:, 516:772]]
    maskL = [cf.h[:, 772:900], cf.h[:, 900:1028]]
    ident_b = cb.h[:, 0:128]
    ones_b = cb.h[:, 128:256]
    blk64_b = cb.h[:, 256:384]
    rotT_b = cb.h[:, 384:512]
    hsel_b = cb.h[:, 512:514]

    def mLT(d, lev):
        c0 = 516 + (d * 7 + lev) * 128
        return cb.h[:, c0:c0 + 128]

    for l in range(NL):
        lam_init = 0.8 - 0.6 * float(np.exp(-0.3 * l))
        xcur = xT0 if l == 0 else xTs[(l - 1) % 2]
        xnext = xTs[l % 2]
        last = l == DEPTH - 1
        A.push()
        pp = A.sb([128, NPP], F32, "pp")
        bcs = A.sb([128, NBC], F32, "bc")
        mod = A.sb([128, 24, 2], F32, "mod")
        gs = A.sb([128, 8, 2], F32, "gs")
        lamt = A.sb([128, 8], F32, "lam")
        sublng = A.sb([128, 128], F32, "sublng")
        omka = A.sb([128, 4], F32, "omka")
        S.dma("sp", pp[:], pp_d[l, :, :], [], [pp])
        S.dma("sp", bcs[:], bc_d[l, :, :], [], [bcs])

        A.push()
        wm = Rot([A.sb([128, 8, 512], F32, "wm") for _ in range(2)])
        pb = psum()
        wmv = w_mod[l].rearrange("(kc p) n -> p kc n", p=128)
        for g6 in range(6):
            w = wm.get()
            S.dma("sp", w[:], wmv[:, :, g6 * 512:(g6 + 1) * 512], [], [w])
            for nt4 in range(4):
                nt = g6 * 4 + nt4
                for kc in range(8):
                    mm(pb[:, nt * 2:nt * 2 + 2], w[:, kc, nt4 * 128:(nt4 + 1) * 128], cact[:, kc, :], [w, cact], [pb],
                       start=(kc == 0), stop=(kc == 7))
        tt("dve", mod[:], pb[:, 0:48].rearrange("p (a b) -> p a b", b=2),
           pp[:, 0:24].unsqueeze(2).to_broadcast([128, 24, 2]), ALU.add, [pb, pp], [mod])
        ts("dve", gs[:], mod[:, 8:16, :], 1.0, None, ALU.add, None, [mod], [gs])
        tt("dve", gs[:], gs[:], pp[:, 24:32].unsqueeze(2).to_broadcast([128, 8, 2]), ALU.mult, [gs, pp], [gs])
        lamtmp = A.sb([128, 2, 64], F32, "lamtmp")
        dl = bcs[:, 2688:2944].rearrange("p (a b) -> p a b", b=64)
        tt("dve", lamtmp[:, 0, :], dl[:, 0, :], dl[:, 1, :], ALU.mult, [bcs], [lamtmp])
        tt("dve", lamtmp[:, 1, :], dl[:, 2, :], dl[:, 3, :], ALU.mult, [bcs], [lamtmp])
        S.op("dve", lambda h: h.tensor_reduce(lamt[:, 0:2], lamtmp[:], AX.X, ALU.add), [lamtmp], [lamt])
        act(lamt[:, 2:4], lamt[:, 0:2], AF.Exp, [lamt], [lamt])
        tt("dve", lamt[:, 4:5], lamt[:, 2:3], lamt[:, 3:4], ALU.subtract, [lamt], [lamt])
        ts("dve", lamt[:, 5:6], lamt[:, 4:5], lam_init, -1.0, ALU.add, ALU.mult, [lamt], [lamt])
        ts("dve", sublng[:], bcs[:, 1536:1664], 1.0 - lam_init, None, ALU.mult, None, [bcs], [sublng])
        ts("dve", omka[:], pp[:, 90:94], -1.0, 1.0, ALU.mult, ALU.add, [pp], [omka])
        neg_lam = lamt.h[:, 5:6]
        A.pop()

        A.push()
        hT = A.sb([128, 8, TALL], BF16, "hT")
        hTb = [Buf() for _ in BLOCKS]
        A.push()
        xin = Rot([A.sb([128, 8, 512], F32, "xin") for _ in range(2)])
        sqs = Rot([A.sb([128, 8, 512], BF16, "sq") for _ in range(2)])
        rstds = Rot([A.sb([128, 512], F32, "rstd") for _ in range(2)])
        tmps = Rot([A.sb([128, 512], F32, "ntmp") for _ in range(3)])
        xv = xcur.rearrange("(kc p) t -> p kc t", p=128)
        for bi, (t0, n) in enumerate(BLOCKS if lo <= 1 <= upto else []):
            j = 1 if bi == 0 else 0
            x_ = xin.get()
            S.dma("sp", x_[:, :, :n], xv[:, :, t0:t0 + n], [], [x_])
            sq = sqs.get()
            act(sq[:, :, :n], x_[:, :, :n], AF.Square, [x_], [sq])
            pb = psum()
            for kc in range(8):
                mm(pb[:, :n], ones_b, sq[:, kc, :n], [cb, sq], [pb], start=(kc == 0), stop=(kc == 7))
            rs = rstds.get()
            act(rs[:, :n], pb[:, :n], AF.Sqrt, [pb, epsT], [rs], bias=epsT[:, 0:1], scale=1.0 / D)
            recip(rs[:, :n], rs[:, :n], [rs], [rs])
            for kc in range(8):
                tm = tmps.get()
                stt(tm[:, :n], x_[:, kc, :n], gs[:, kc, j:j + 1], rs[:, :n], ALU.mult, ALU.mult, [x_, gs, rs], [tm])
                act(hT[:, kc, t0:t0 + n], tm[:, :n], AF.Identity, [tm, mod], [hTb[bi]], bias=mod[:, kc, j:j + 1])
        A.pop()

        A.push()
        wbufs = Rot([A.sb([128, 8, 512], BF16, "wbuf") for _ in range(4)])
        ostf = Rot([A.sb([128, 512], F32, "ostf") for _ in range(4)])
        ostb = Rot([A.sb([128, 512], BF16, "ostb") for _ in range(4)])
        ptmp = Rot([A.sb([128, 512], F32, "ptmp") for _ in range(6)])
        ptmpb = Rot([A.sb([128, 512], BF16, "ptmpb") for _ in range(4)])
        ropeCs = Rot([A.sb([128, 512], F32, "rC") for _ in range(2)])
        ropeSs = Rot([A.sb([128, 512], F32, "rS") for _ in range(2)])
        stat = Rot([A.sb([128, 16], F32, "stat") for _ in range(4)])
        winv = w_in[l].rearrange("(kc p) n -> p kc n", p=128)

        preq = {}

        def loadw_raw(c0, ncols=512):
            w = wbufs.get()
            S.dma("pool", w[:, :, :ncols], winv[:, :, c0:c0 + ncols], [], [w])
            return w

        def loadw(c0, ncols=512):
            if c0 in preq:
                return preq.pop(c0)
            return loadw_raw(c0, ncols)

        def prefetch(c0, ncols=512):
            preq[c0] = loadw_raw(c0, ncols)

        def fm_mm(w, ct, bi, t0, n):
            pb = psum()
            for kc in range(8):
                mm(pb[:, :n], w[:, kc, ct * 128:(ct + 1) * 128], hT[:, kc, t0:t0 + n], [w, hTb[bi]], [pb],
                   start=(kc == 0), stop=(kc == 7))
            return pb

        def tm_mm(w, t0):
            pb = psum()
            bi = 0 if t0 < 256 else 1 + (t0 - 256) // 512
            for kc in range(8):
                mm(pb[:, :], hT[:, kc, t0:t0 + 128], w[:, kc, :], [w, hTb[bi]], [pb], start=(kc == 0), stop=(kc == 7))
            return pb

        def fm_raw(c0, ncols, dst, alt=[0]):
            w = loadw(c0, ncols)
            for bi, (t0, n) in enumerate(BLOCKS):
                for ct in range(ncols // 128):
                    pb = fm_mm(w, ct, bi, t0, n)
                    o = ostf.get()
                    alt[0] ^= 1
                    cp("act" if alt[0] else "dve", o[:, :n], pb[:, :n], [pb], [o])
                    S.dma("pool", dst[ct * 128:(ct + 1) * 128, t0:t0 + n], o[:, :n], [o], [])

        def fm_qk(c0, dst, gcol):
            w = loadw(c0)
            for bi, (t0, n) in enumerate(BLOCKS):
                rC = ropeCs.get()
                rS = ropeSs.get()
                S.dma("sp", rC[:, :n], ropeC_d[:, t0:t0 + n], [], [rC])
                S.dma("sp", rS[:, :n], ropeS_d[:, t0:t0 + n], [], [rS])
                for ct in range(4):
                    pb = fm_mm(w, ct, bi, t0, n)
                    sqb = ptmpb.get()
                    act(sqb[:, :n], pb[:, :n], AF.Square, [pb], [sqb])
                    p2 = psum()
                    mm(p2[:, :n], blk64_b, sqb[:, :n], [cb, sqb], [p2])
                    rs = ptmp.get()
                    act(rs[:, :n], p2[:, :n], AF.Sqrt, [p2, epsT], [rs], bias=epsT[:, 0:1], scale=1.0 / 64)
                    recip(rs[:, :n], rs[:, :n], [rs], [rs])
                    xn = ptmpb.get()
                    stt(xn[:, :n], pb[:, :n], pp[:, gcol:gcol + 1], rs[:, :n], ALU.mult, ALU.mult, [pb, pp, rs], [xn])
                    p3 = psum()
                    mm(p3[:, :n], rotT_b, xn[:, :n], [cb, xn], [p3])
                    t1 = ptmp.get()
                    tt("pool", t1[:, :n], xn[:, :n], rC[:, :n], ALU.mult, [xn, rC], [t1])
                    t2 = ptmp.get()
                    tt("dve", t2[:, :n], p3[:, :n], rS[:, :n], ALU.mult, [p3, rS], [t2])
                    o = ostb.get()
                    tt("pool", o[:, :n], t1[:, :n], t2[:, :n], ALU.add, [t1, t2], [o])
                    S.dma("pool", dst[ct, :, t0:t0 + n], o[:, :n], [o], [])

        def tm_group(c0, kind, dst):
            w = loadw(c0)
            for ti in range(NTILE):
                t0 = ti * 128
                pb = tm_mm(w, t0)
                o = ostb.get()
                if kind == "copy":
                    cp("act" if ti % 2 else "dve", o[:], pb[:], [pb], [o])
                elif kind == "silu":
                    act(o[:], pb[:], AF.Silu, [pb], [o])
                else:
                    tA = ptmp.get()
                    tB = ptmp.get()
                    g = ptmp.get()
                    act(tA[:], pb[:], AF.Square, [pb], [tA])
                    ts("dve", tA[:], tA[:], 0.044715, 1.0, ALU.mult, ALU.add, [tA], [tA])
                    tt("dve", tA[:], tA[:], pb[:], ALU.mult, [tA, pb], [tA])
                    act(tB[:], tA[:], AF.Sigmoid, [tA], [tB], scale=1.5957691216057308)
                    tt("dve", g[:], tB[:], pb[:], ALU.mult, [tB, pb], [g])
                    st = stat.get()
                    S.op("dve", lambda h: h.bn_stats(st[:, 0:6], g[:]), [g], [st])
                    S.op("dve", lambda h: h.bn_aggr(st[:, 8:10], st[:, 0:6]), [st], [st])
                    act(st[:, 10:11], st[:, 9:10], AF.Sqrt, [st, epsT], [st], bias=epsT[:, 1:2], scale=1.0)
                    recip(st[:, 11:12], st[:, 10:11], [st], [st])
                    ts("dve", g[:], g[:], st[:, 8:9], st[:, 11:12], ALU.subtract, ALU.mult, [g, st], [g])
                    tt("pool", g[:], g[:], bcs[:, 0:512], ALU.mult, [g, bcs], [g])
                    tt("pool", o[:], g[:], bcs[:, 512:1024], ALU.add, [g, bcs], [o])
                S.dma("pool", dst[t0:t0 + 128, :], o[:], [o], [])

        def fm_uz():
            wu = loadw(3328)
            wz = loadw(4352)
            for bi, (t0, n) in enumerate(BLOCKS):
                for ct in range(4):
                    pu = fm_mm(wu, ct, bi, t0, n)
                    pz = fm_mm(wz, ct, bi, t0, n)
                    tA = ptmp.get()
                    tB = ptmp.get()
                    g = ptmp.get()
                    act(tA[:, :n], pu[:, :n], AF.Square, [pu], [tA])
                    ts("dve", tA[:, :n], tA[:, :n], 0.044715, 1.0, ALU.mult, ALU.add, [tA], [tA])
                    tt("dve", tA[:, :n], tA[:, :n], pu[:, :n], ALU.mult, [tA, pu], [tA])
                    act(tB[:, :n], tA[:, :n], AF.Sigmoid, [tA], [tB], scale=1.5957691216057308)
                    tt("dve", g[:, :n], tB[:, :n], pu[:, :n], ALU.mult, [tB, pu], [g])
                    sz = ptmp.get()
                    act(sz[:, :n], pz[:, :n], AF.Silu, [pz], [sz])
                    o = ostb.get()
                    tt("pool", o[:, :n], g[:, :n], sz[:, :n], ALU.mult, [g, sz], [o])
                    S.dma("pool", uzT[ct * 128:(ct + 1) * 128, t0:t0 + n], o[:, :n], [o], [])

        def fm_gl(c0, gi):
            w = loadw(c0)
            for bi, (t0, n) in enumerate(BLOCKS):
                for ct in range(4):
                    pb = fm_mm(w, ct, bi, t0, n)
                    o = ostb.get()
                    act(o[:, :n], pb[:, :n], AF.Sigmoid, [pb], [o])
                    r0 = gi * 512 + ct * 128
                    S.dma("pool", gT[r0:r0 + 128, t0:t0 + n], o[:, :n], [o], [])

        if lo <= 2 <= upto:
            prefetch(0)
            prefetch(512)
            fm_qk(0, kT, 33)
            prefetch(1024)
            tm_group(512, "copy", Vd)
            prefetch(1536)
            fm_raw(1024, 512, krT)
            prefetch(2048, 128)
            fm_raw(1536, 512, vrT)
            prefetch(2176, 128)
            fm_raw(2048, 128, wlT)
            prefetch(2304)
            fm_raw(2176, 128, alT)
            prefetch(2816)
            fm_qk(2304, qT, 32)
            prefetch(3328)
            fm_raw(2816, 512, rrT)
            prefetch(4352)
            fm_uz()
            prefetch(3840)
            prefetch(4864)
            tm_group(3840, "va", vln)
            prefetch(5376)
            tm_group(4864, "silu", zbd)
            prefetch(5888)
            tm_group(5376, "silu", zcd)
            for gi in range(6):
                if gi + 1 < 6:
                    prefetch(5888 + (gi + 1) * 512)
                fm_gl(5888 + gi * 512, gi)
        A.pop()
        A.pop()

        A.push()
        wsT = A.sb([128, 4, 128], BF16, "wsT")
        S.dma("pool", wsT[:], a_wsT[l].rearrange("g q p -> q g p"), [], [wsT])
        vlb = Rot([A.sb([128, 4, 512], BF16, "vlb") for _ in range(2)])
        uzb = Rot([A.sb([128, 4, 512], BF16, "uzb") for _ in range(2)])
        gtmp = Rot([A.sb([128, 512], F32, "gtmp") for _ in range(3)])
        gost = Rot([A.sb([128, 512], BF16, "gost") for _ in range(3)])
        vlnv = vln.rearrange("(t p) c -> p t c", p=128)
        uzv = uzT.rearrange("(ct p) t -> p ct t", p=128)
        for bi, (t0, n) in enumerate(BLOCKS if lo <= 3 <= upto else []):
            nt = n // 128
            vb_ = vlb.get()
            ub_ = uzb.get()
            S.dma("sp", vb_[:, :nt, :], vlnv[:, t0 // 128:t0 // 128 + nt, :], [], [vb_])
            S.dma("sp", ub_[:, :, :n], uzv[:, :, t0:t0 + n], [], [ub_])
            for g in range(4):
                pb = psum()
                for ti in range(nt):
                    mm(pb[:, ti * 128:(ti + 1) * 128], vb_[:, ti, g * 128:(g + 1) * 128], wsT[:, g, :], [vb_, wsT], [pb])
                tg = gtmp.get()
                tt("dve", tg[:, :n].rearrange("p (a b) -> p a b", b=128), pb[:, :n].rearrange("p (a b) -> p a b", b=128),
                   bcs[:, 1024 + g * 128:1024 + (g + 1) * 128].unsqueeze(1).to_broadcast([128, nt, 128]), ALU.add,
                   [pb, bcs], [tg])
                o = gost.get()
                tt("pool", o[:, :n], tg[:, :n], ub_[:, g, :n], ALU.mult, [tg, ub_], [o])
                S.dma("pool", yaT[g * 128:(g + 1) * 128, t0:t0 + n], o[:, :n], [o], [])
        A.pop()

        A.push()
        pstate["nb"] = 6
        kTh = Rot([A.sb([128, TALL], BF16, "kTh") for _ in range(2)])
        Vh = Rot([A.sb([128, NTILE, 132], BF16, "Vh") for _ in range(2)])
        for v_ in Vh.t:
            memset("pool", v_[:, :, 128:132], 1.0, [v_])
        PTs = Rot([A.sb([128, NTILE, 512], BF16, "PT") for _ in range(3)])
        qTb = Rot([A.sb([128, 512], BF16, "qTb") for _ in range(2)])
        zbt = Rot([A.sb([128, 4, 128], BF16, "zbt") for _ in range(2)])
        ao = Rot([A.sb([128, 128], F32, "ao") for _ in range(6)])
        ast = Rot([A.sb([128, 8], F32, "ast") for _ in range(4)])
        ayb = Rot([A.sb([128, 128], BF16, "ayb") for _ in range(3)])
        aost = Rot([A.sb([128, 512], BF16, "aost") for _ in range(2)])
        Vv = Vd.rearrange("(t p) c -> p t c", p=128)
        zbv = zbd.rearrange("(t p) c -> p t c", p=128)
        ao0 = [A.sb([128, 128], F32, "ao0") for _ in range(4)]
        pvbank_i = [0]

        def qke_gen(kh, qb, n, kts, j, PTj):
            js = slice(j * 64, (j + 1) * 64)
            for kt in kts:
                pb = psum()
                mm(pb[:, :n], kh[js, kt * 128:(kt + 1) * 128], qb[js, :n], [kh, qb], [pb])
                act(PTj[:, kt, :n], pb[:, :n], AF.Exp, [pb], [PTj], scale=0.125)
                yield

        def pv_gen(vh, PTj, j, kts, nt, zt, ob, h_, t0, n):
            for ti in range(nt):
                pb = banks[6 + (pvbank_i[0] % 2)]
                pvbank_i[0] += 1
                for ki, kt in enumerate(kts):
                    mm(pb[:, 0:129], PTj[:, kt, ti * 128:(ti + 1) * 128], vh[:, kt, 0:129], [PTj, vh], [pb],
                       start=(ki == 0), stop=(ki == len(kts) - 1))
                    if ki % 6 == 5:
                        yield
                st = ast.get()
                recip(st[:, 0:1], pb[:, 128:129], [pb], [st])
                if j == 0:
                    ts("dve", ao0[ti][:], pb[:, 0:128], st[:, 0:1], None, ALU.mult, None, [pb, st], [ao0[ti]])
                    yield
                    continue
                o1 = ao.get()
                ts("dve", o1[:], pb[:, 0:128], st[:, 0:1], None, ALU.mult, None, [pb, st], [o1])
                od = ao.get()
                stt(od[:], o1[:], neg_lam, ao0[ti][:], ALU.mult, ALU.add, [o1, ao0[ti], lamt], [od])
                st2 = ast.get()
                junk = ao.get()
                act(junk[:], od[:], AF.Square, [od], [junk, st2], accum=st2[:, 0:1])
                yield
                act(st2[:, 1:2], st2[:, 0:1], AF.Sqrt, [st2, epsT], [st2], bias=epsT[:, 0:1], scale=1.0 / 128)
                recip(st2[:, 2:3], st2[:, 1:2], [st2], [st2])
                stt(od[:], od[:], st2[:, 2:3], sublng[:], ALU.mult, ALU.mult, [od, st2, sublng], [od])
                yb_ = ayb.get()
                tt("dve", yb_[:], od[:], zt[:, ti, :], ALU.mult, [od, zt], [yb_])
                pbt_ = psum()
                pbb = pbt_.h[:].bitcast(BF16)
                tr(pbb[:, 0:128], yb_[:], ident_b, [yb_, cb], [pbt_])
                cp("act", ob[:, ti * 128:(ti + 1) * 128], pbb[:, 0:128], [pbt_], [ob])
                yield
            if j == 1:
                S.dma("pool", ybT[h_ * 128:(h_ + 1) * 128, t0:t0 + n], ob[:, :n], [ob], [])

        def run2(g_main, g_side, ratio=2):
            while g_main is not None or g_side is not None:
                if g_main is not None:
                    try:
                        next(g_main)
                    except StopIteration:
                        g_main = None
                if g_side is not None:
                    for _ in range(ratio):
                        try:
                            next(g_side)
                        except StopIteration:
                            g_side = None
                            break

        for h_ in range(4 if lo <= 4 <= upto else 0):
            kh = kTh.get()
            vh = Vh.get()
            S.dma("sp", kh[:], kT[h_, :, :], [], [kh])
            for q0 in range(0, NTILE, 9):
                q1 = min(NTILE, q0 + 9)
                S.dma("sp", vh[:, q0:q1, 0:128], Vv[:, q0:q1, h_ * 128:(h_ + 1) * 128], [], [vh])

            def blk_setup(bi):
                t0, n = BLOCKS[bi]
                nt = n // 128
                kts = [0, 1] if bi == 0 else list(range(NTILE))
                qb = qTb.get()
                S.dma("sp", qb[:, :n], qT[h_, :, t0:t0 + n], [], [qb])
                zt = zbt.get()
                S.dma("sp", zt[:, :nt, :], zbv[:, t0 // 128:t0 // 128 + nt, h_ * 128:(h_ + 1) * 128], [], [zt])
                return dict(t0=t0, n=n, nt=nt, kts=kts, qb=qb, zt=zt, PT=[None, None])

            cur = blk_setup(0)
            for j in range(2):
                cur["PT"][j] = PTs.get()
                run2(qke_gen(kh, cur["qb"], cur["n"], cur["kts"], j, cur["PT"][j]), None)
            for bi in range(len(BLOCKS)):
                nxt = blk_setup(bi + 1) if bi + 1 < len(BLOCKS) else None
                ob = aost.get()
                for j in range(2):
                    side = None
                    if nxt is not None:
                        nxt["PT"][j] = PTs.get()
                        side = qke_gen(kh, nxt["qb"], nxt["n"], nxt["kts"], j, nxt["PT"][j])
                    run2(pv_gen(vh, cur["PT"][j], j, cur["kts"], cur["nt"], cur["zt"], ob, h_, cur["t0"], cur["n"]), side)
                cur = nxt
        A.pop()

        A.push()
        w2a2 = [A.sb([128, 512], BF16, "w2a2") for _ in range(2)]
        for d_ in range(2):
            S.dma("pool", w2a2[d_][0:64, :], r_w2[l, d_, :, :], [], [w2a2[d_]])
            S.dma("pool", w2a2[d_][64:128, :], r_a2[l, d_, :, :], [], [w2a2[d_]])
        STb = [A.sb([128, 4, 64], BF16, "STb") for _ in range(2)]
        STB = [[[Buf() for _ in range(2)] for _ in range(4)] for _ in range(2)]
        NW = 2

        def wsf(name, shape=(128, 128), dt=F32, k=NW):
            return Rot([A.sb(list(shape), dt, name) for _ in range(k)])

        W_wla = wsf("wla")
        W_lab = wsf("lab", dt=BF16)
        PT = [dict(krc=A.sb([128, 130], F32, "krc"), vrc=A.sb([128, 130], F32, "vrc"), rrc=A.sb([128, 130], F32, "rrc"),
                   **{nm: A.sb([128, 128], F32, nm) for nm in ("k", "v", "r", "sg", "a", "Gs", "G2", "Gx", "eG", "enG", "eGx",
                                                              "eGC", "kk", "sq", "b", "kd")},
                   **{nm: A.sb([128, 128], BF16, nm) for nm in ("sqb", "BhT", "KhT", "vb", "rkk")}) for ct in range(4)]
        PS = [[dict(AR=A.sb([128, 2, 128], BF16, "AR"), Bt=A.sb([128, 128], BF16, "Bt"), Kt=A.sb([128, 128], BF16, "Kt"),
                    TK=A.sb([128, 3, 128], BF16, "TK"), Rtf=A.sb([128, 128], F32, "Rtf"), sm=A.sb([128, 8], F32, "sm"))
               for ct in range(4)] for par in range(2)]
        VTs = [A.sb([128, 4, 128], BF16, "VT") for par in range(2)]
        betas = [A.sb([128, 8], F32, "beta") for par in range(2)]
        SL = [dict(M1=A.sb([128, 2, 256], BF16, "M1"), M2=A.sb([128, 2, 256], BF16, "M2"), X=A.sb([128, 2, 128], BF16, "X"),
                   T=[A.sb([128, 2, 128], BF16, "T") for _ in range(2)], TT=[A.sb([128, 2, 128], BF16, "TT") for _ in range(2)],
                   Lm=[A.sb([128, 2, 128], BF16, "Lm") for _ in range(8)], W=[A.sb([128, 2, 128], BF16, "W") for _ in range(2)],
                   UA=A.sb([128, 2, 128], BF16, "UA"), UP=A.sb([128, 2, 128], BF16, "UP"), Phi=A.sb([128, 64], BF16, "Phi"),
                   Psi=A.sb([128, 64], BF16, "Psi"), QT=A.sb([128, 128], BF16, "QT")) for _ in range(4)]
        W_yo = wsf("yo", (128, 512), k=2)
        W_y = wsf("y", (128, 512), k=2)
        W_ysq = wsf("ysq", (128, 512), k=2)
        W_st = wsf("rst", (128, 48), k=2)
        W_zc = wsf("zc", (128, 512), BF16, 2)
        W_ycz = wsf("ycz", (128, 512), BF16, 2)
        W_yco = wsf("yco", (128, 4, 128), BF16, 2)
        zcv = zcd.rearrange("(t p) c -> p t c", p=128)
        ycTv = ycT.rearrange("(ct p) t -> p ct t", p=128)

        def conv(dst, src, which, ct):
            base = 34 + which * 12 + ct
            ts("dve", dst[:], src[:, 0:128], pp[:, base:base + 1], None, ALU.mult, None, [src, pp], [dst])
            stt(dst[:], src[:, 1:129], pp[:, base + 4:base + 5], dst[:], ALU.mult, ALU.add, [src, pp, dst], [dst])
            stt(dst[:], src[:, 2:130], pp[:, base + 8:base + 9], dst[:], ALU.mult, ALU.add, [src, pp, dst], [dst])

        def load_halo(tile, src, ct, t0):
            left = not (t0 == 0 or t0 == 256)
            right = not (t0 + 128 == 256 or t0 + 128 == TALL)
            lo = t0 - (1 if left else 0)
            hi = t0 + 128 + (1 if right else 0)
            c_lo = 1 - (t0 - lo)
            if not left:
                memset("pool", tile[:, 0:1], 0.0, [tile])
            if not right:
                memset("pool", tile[:, 129:130], 0.0, [tile])
            S.dma("sp", tile[:, c_lo:c_lo + (hi - lo)], src[ct * 128:(ct + 1) * 128, lo:hi], [], [tile])

        def prep_chunk(d_, c_):
            t0 = c_ * 128
            wla = W_wla.get()
            S.dma("sp", wla[0:64, :], wlT[d_ * 64:(d_ + 1) * 64, t0:t0 + 128], [], [wla])
            S.dma("sp", wla[64:128, :], alT[d_ * 64:(d_ + 1) * 64, t0:t0 + 128], [], [wla])
            lab = W_lab.get()
            act(lab[0:64, :], wla[0:64, :], AF.Tanh, [wla], [lab])
            cp("dve", lab[64:128, :], wla[64:128, :], [wla], [lab])
            return lab

        def prep_gen(d_, c_, par, ct, lab):
            t0 = c_ * 128
            VT = VTs[par]
            beta = betas[par]
            P = PS[par][ct]
            AR, Bt, Kt, TK, Rtf, sm = P["AR"], P["Bt"], P["Kt"], P["TK"], P["Rtf"], P["sm"]
            W = PT[ct]
            krc, vrc, rrc = W["krc"], W["vrc"], W["rrc"]
            load_halo(krc, krT, ct, t0)
            load_halo(vrc, vrT, ct, t0)
            load_halo(rrc, rrT, ct, t0)
            pw = psum()
            mm(pw[:, 0:128], w2a2[d_][0:64, ct * 128:(ct + 1) * 128], lab[0:64, :], [w2a2[d_], lab], [pw])
            sg = W["sg"]
            a_ = W["a"]
            act(sg[:], pw[:, 0:128], AF.Sigmoid, [pw, pp], [sg], bias=pp[:, 70 + d_ * 4 + ct:71 + d_ * 4 + ct])
            yield
            pw2 = psum()
            mm(pw2[:, 128:256], w2a2[d_][64:128, ct * 128:(ct + 1) * 128], lab[64:128, :], [w2a2[d_], lab], [pw2])
            act(a_[:], pw2[:, 128:256], AF.Sigmoid, [pw2, pp], [a_], bias=pp[:, 78 + d_ * 4 + ct:79 + d_ * 4 + ct])
            yield
            k_, v_, r_ = W["k"], W["v"], W["r"]
            for (dst, src, which) in ((r_, rrc, 0), (k_, krc, 1), (v_, vrc, 2)):
                base = 34 + which * 12 + ct
                ts("dve", dst[:], src[:, 0:128], pp[:, base:base + 1], None, ALU.mult, None, [src, pp], [dst])
                yield
                stt(dst[:], src[:, 1:129], pp[:, base + 4:base + 5], dst[:], ALU.mult, ALU.add, [src, pp, dst], [dst])
                yield
                stt(dst[:], src[:, 2:130], pp[:, base + 8:base + 9], dst[:], ALU.mult, ALU.add, [src, pp, dst], [dst])
                yield
            Gs = W["Gs"]
            S.op("dve", lambda h: h.tensor_tensor_scan(Gs[:], onesf[:], sg[:], 0.0, ALU.mult, ALU.add), [onesf, sg], [Gs])
            yield
            if d_ == 0:
                G = Gs
            else:
                G = W["G2"]
                stt(G[:], Gs[:], -1.0, sg[:], ALU.mult, ALU.add, [Gs, sg], [G])
                yield
                ts("dve", G[:], G[:], Gs[:, 127:128], None, ALU.add, None, [G, Gs], [G])
            ts("dve", sm[:, 0:1], Gs[:, 127:128], -C0, None, ALU.mult, None, [Gs], [sm])
            yield
            Gx = W["Gx"]
            tt("pool", Gx[:], G[:], sg[:], ALU.subtract, [G, sg], [Gx])
            eG, enG, eGx, eGC = W["eG"], W["enG"], W["eGx"], W["eGC"]
            act(eG[:], G[:], AF.Exp, [G], [eG], scale=-C0)
            yield
            act(enG[:], G[:], AF.Exp, [G], [enG], scale=C0)
            yield
            act(eGx[:], Gx[:], AF.Exp, [Gx], [eGx], scale=-C0)
            yield
            act(eGC[:], G[:], AF.Exp, [G, sm], [eGC], scale=C0, bias=sm[:, 0:1])
            act(sm[:, 1:2], sm[:, 0:1], AF.Exp, [sm], [sm])
            yield
            kk, sq, sqb = W["kk"], W["sq"], W["sqb"]
            ts("dve", kk[:], k_[:], pp[:, 86 + ct:87 + ct], None, ALU.mult, None, [k_, pp], [kk])
            yield
            act(sqb[:], kk[:], AF.Square, [kk], [sqb])
            yield
            pn = psum()
            mm(pn[:, 0:128], blk64_b, sqb[:], [cb, sqb], [pn])
            act(sq[:], pn[:, 0:128], AF.Sqrt, [pn], [sq])
            yield
            ts("dve", sq[:], sq[:], 1e-12, None, ALU.max, None, [sq], [sq])
            yield
            recip(sq[:], sq[:], [sq], [sq])
            yield
            tt("dve", kk[:], kk[:], sq[:], ALU.mult, [kk, sq], [kk])
            yield
            b_, kd = W["b"], W["kd"]
            tt("pool", b_[:], kk[:], a_[:], ALU.mult, [kk, a_], [b_])
            ts("dve", kd[:], a_[:], pp[:, 90 + ct:91 + ct], omka[:, ct:ct + 1], ALU.mult, ALU.add, [a_, pp, omka], [kd])
            yield
            tt("pool", kd[:], kd[:], k_[:], ALU.mult, [kd, k_], [kd])
            stt(AR[:, 0, :], kk[:], -1.0, eGx[:], ALU.mult, ALU.mult, [kk, eGx], [AR])
            yield
            tt("dve", Rtf[:], r_[:], eG[:], ALU.mult, [r_, eG], [Rtf])
            yield
            cp("act", AR[:, 1, :], Rtf[:], [Rtf], [AR])
            BhT, KhT, vb = W["BhT"], W["KhT"], W["vb"]
            tt("pool", Bt[:], b_[:], enG[:], ALU.mult, [b_, enG], [Bt])
            tt("dve", Kt[:], kd[:], enG[:], ALU.mult, [kd, enG], [Kt])
            yield
            tt("pool", BhT[:], b_[:], eGC[:], ALU.mult, [b_, eGC], [BhT])
            tt("dve", KhT[:], kd[:], eGC[:], ALU.mult, [kd, eGC], [KhT])
            cp("act", vb[:], v_[:], [v_], [vb])
            yield
            ptr = psum()
            ptb = ptr.h[:].bitcast(BF16)
            tr(ptb[:, 0:128], AR[:, 0, :], ident_b, [AR, cb], [ptr])
            tr(ptb[:, 128:256], BhT[:], ident_b, [BhT, cb], [ptr])
            tr(ptb[:, 256:384], KhT[:], ident_b, [KhT, cb], [ptr])
            tr(ptb[:, 384:512], vb[:], ident_b, [vb, cb], [ptr])
            cp("dve", TK[:].rearrange("p a b -> p (a b)"), ptb[:, 0:384], [ptr], [TK])
            cp("dve", VT[:, ct, :], ptb[:, 384:512], [ptr], [VT])
            yield
            if d_ == 1:
                rkk = W["rkk"]
                stt(rkk[:], r_[:], pp[:, 98 + ct:99 + ct], k_[:], ALU.mult, ALU.mult, [r_, pp, k_], [rkk])
                pbt = psum()
                mm(pbt[:, 0:2], rkk[:], hsel_b, [rkk, cb], [pbt])
                cp("act", beta[:, ct * 2:ct * 2 + 2], pbt[:, 0:2], [pbt], [beta])
                yield

        def pair_gen(d_, par, ct, ybk):
            P = PS[par][ct]
            AR, Bt, Kt, TK, Rtf, sm = P["AR"], P["Bt"], P["Kt"], P["TK"], P["Rtf"], P["sm"]
            VT = VTs[par]
            Q = SL[ct]
            M1, M2, X = Q["M1"], Q["M2"], Q["X"]
            ARf = AR[:].rearrange("p a b -> p (a b)")
            HP = [slice(0, 64), slice(64, 128)]
            v3 = lambda ap, e: ap.rearrange("p (h e) -> p h e", e=e)
            bc2 = lambda ap, e: ap.unsqueeze(1).to_broadcast([128, 2, e])
            for hh in range(2):
                hp = HP[hh]
                p1 = psum()
                mm(p1[:, 0:256], Bt[hp, :], ARf[hp, :], [Bt, AR], [p1])
                cp("act", M1[:, hh, :], p1[:, 0:256], [p1], [M1])
                yield
            for hh in range(2):
                hp = HP[hh]
                p2 = psum()
                mm(p2[:, 0:256], Kt[hp, :], ARf[hp, :], [Kt, AR], [p2])
                tt("dve", M2[:, hh, :], p2[:, 0:256], maskT[d_], ALU.mult, [p2, cf], [M2])
                yield
            for hh in range(2):
                hp = HP[hh]
                p3 = psum()
                mm(p3[:, 0:128], AR[hp, 0, :], Bt[hp, :], [AR, Bt], [p3])
                cp("act", X[:, hh, :], p3[:, 0:128], [p3], [X])
                yield
            tt("pool", M1[:, :, 128:256], M1[:, :, 128:256], bc2(maskT[d_][:, 128:256], 128), ALU.mult, [M1, cf], [M1])
            LTap = M1[:, :, 0:128]
            T_ = Q["T"][0]
            TT = Q["TT"][0]
            Lm = Q["Lm"]
            tt("pool", Lm[6][:], LTap, bc2(mLT(d_, 0), 128), ALU.mult, [M1, cb], [Lm[6]])
            tt("pool", TT[:], Lm[6][:], bc2(ident_b, 128), ALU.add, [Lm[6], cb], [TT])
            tt("pool", Lm[7][:], X[:], bc2(mLT(1 - d_, 0), 128), ALU.mult, [X, cb], [Lm[7]])
            tt("pool", T_[:], Lm[7][:], bc2(ident_b, 128), ALU.add, [Lm[7], cb], [T_])
            yield
            for lev in range(1, 7):
                tt("pool", Lm[lev - 1][:], LTap, bc2(mLT(d_, lev), 128), ALU.mult, [M1, cb], [Lm[lev - 1]])
                if lev % 2 == 0:
                    yield
            for lev in range(1, 7):
                LmT = Lm[lev - 1]
                pa = psum()
                for hh in range(2):
                    mm(pa[:, hh * 128:(hh + 1) * 128], LmT[:, hh, :], T_[:, hh, :], [LmT, T_], [pa], start=True, stop=False)
                    mm(pa[:, hh * 128:(hh + 1) * 128], ident_b, ident_b, [cb], [pa], start=False, stop=True)
                Wt = Q["W"][lev % 2]
                cp("act" if lev % 2 else "dve", Wt[:].rearrange("p h e -> p (h e)"), pa[:, 0:256], [pa], [Wt])
                yield
                if lev < 6:
                    pT = psum()
                    for hh in range(2):
                        mm(pT[:, hh * 128:(hh + 1) * 128], TT[:, hh, :], Wt[:, hh, :], [TT, Wt], [pT])
                    Tn = Q["T"][lev % 2]
                    cp("act", Tn[:].rearrange("p h e -> p (h e)"), pT[:, 0:256], [pT], [Tn])
                pTT = psum()
                for hh in range(2):
                    mm(pTT[:, hh * 128:(hh + 1) * 128], Wt[:, hh, :], TT[:, hh, :], [Wt, TT], [pTT])
                TTn = Q["TT"][lev % 2]
                if lev < 6:
                    cp("dve", TTn[:].rearrange("p h e -> p (h e)"), pTT[:, 0:256], [pTT], [TTn])
                else:
                    cp("act", TTn[:].rearrange("p h e -> p (h e)"), pTT[:, 0:256], [pTT], [TTn])
                TT = TTn
                if lev < 6:
                    T_ = Tn
                yield
            UA, UP, Phi, Psi, QT = Q["UA"], Q["UP"], Q["Phi"], Q["Psi"], Q["QT"]
            pwq = psum()
            for hh in range(2):
                mm(pwq[:, hh * 64:(hh + 1) * 64], M2[:, hh, 0:128], VT[:, ct, HP[hh]], [M2, VT], [pwq])
            cp("act", UA[:, :, 0:64], v3(pwq[:, 0:128], 64), [pwq], [UA])
            cp("pool", UA[:, :, 64:128], v3(TK[:, 0, :], 64), [TK], [UA])
            yield
            pu = psum()
            for hh in range(2):
                mm(pu[:, hh * 128:(hh + 1) * 128], TT[:, hh, :], UA[:, hh, :], [TT, UA], [pu])
            cp("act", UP[:].rearrange("p h e -> p (h e)"), pu[:, 0:256], [pu], [UP])
            yield
            pf = psum()
            for hh in range(2):
                hp = HP[hh]
                mm(pf[hp, 0:64], UP[:, hh, 64:128], TK[:, 1, hp], [UP, TK], [pf])
                mm(pf[hp, 64:128], TK[:, 1, hp], UP[:, hh, 0:64], [TK, UP], [pf], start=True, stop=False)
                mm(pf[hp, 64:128], TK[:, 2, hp], VT[:, ct, hp], [TK, VT], [pf], start=False, stop=True)
                mm(pf[hp, 128:256], UP[:, hh, 64:128], M1[:, hh, 128:256], [UP, M1], [pf], start=True, stop=False)
                mm(pf[hp, 128:256], ident_b[:, hp], AR[:, 1, :], [cb, AR], [pf], start=False, stop=True)
            stt(Phi[:], identfold, sm[:, 1:2], pf[:, 0:64], ALU.mult, ALU.add, [cf, sm, pf], [Phi])
            cp("act", Psi[:], pf[:, 64:128], [pf], [Psi])
            cp("act", QT[:], pf[:, 128:256], [pf], [QT])
            yield
            for hh in range(2):
                hp = HP[hh]
                yc0 = ct * 128 + hh * 64
                mm(ybk[:, yc0:yc0 + 64], M1[:, hh, 128:256], UP[:, hh, 0:64], [M1, UP], [ybk], start=True, stop=False)
                mm(ybk[:, yc0:yc0 + 64], M2[:, hh, 128:256], VT[:, ct, hp], [M2, VT], [ybk], start=False, stop=False)
                sbuf_ = STB[d_][ct][hh]
                mm(ybk[:, yc0:yc0 + 64], QT[hp, :], STb[d_][hp, ct, :], [QT, sbuf_], [ybk], start=False, stop=True)
                pS = psum()
                mm(pS[hp, 0:64], Phi[hp, :], STb[d_][hp, ct, :], [Phi, sbuf_], [pS], start=True, stop=False)
                mm(pS[hp, 0:64], ident_b[hp, hp], Psi[hp, :], [cb, Psi], [pS], start=False, stop=True)
                cp("act" if hh else "dve", STb[d_][hp, ct, :], pS[hp, 0:64], [pS], [sbuf_])
                yield

        def readout(d_, c_, par, ybk):
            t0 = c_ * 128
            VT = VTs[par]
            beta = betas[par]
            if d_ == 0:
                yo = W_yo.get()
                cp("act", yo[:], ybk[:], [ybk], [yo])
                S.dma("pool", yfw[t0:t0 + 128, :], yo[:], [yo], [])
                return
            yo = W_yo.get()
            S.dma("sp", yo[:], yfw[t0:t0 + 128, :], [], [yo])
            zc_ = W_zc.get()
            S.dma("sp", zc_[:], zcv[:, c_, :], [], [zc_])
            y = W_y.get()
            tt("dve", y[:], ybk[:], yo[:], ALU.add, [ybk, yo], [y])
            y3 = y[:].rearrange("p (a b) -> p a b", b=64)
            st = W_st.get()
            S.op("dve", lambda h: h.tensor_reduce(st[:, 0:8], y3, AX.X, ALU.add), [y], [st])
            ysq = W_ysq.get()
            act(ysq[:], y[:], AF.Square, [y], [ysq])
            S.op("dve", lambda h: h.tensor_reduce(st[:, 8:16], ysq[:].rearrange("p (a b) -> p a b", b=64), AX.X, ALU.add),
                 [ysq], [st])
            ts("dve", st[:, 0:16], st[:, 0:16], 1.0 / 64, None, ALU.mult, None, [st], [st])
            tt("dve", st[:, 16:24], st[:, 0:8], st[:, 0:8], ALU.mult, [st], [st])
            tt("dve", st[:, 24:32], st[:, 8:16], st[:, 16:24], ALU.subtract, [st], [st])
            act(st[:, 32:40], st[:, 24:32], AF.Sqrt, [st, epsT], [st], bias=epsT[:, 2:3], scale=1.0)
            recip(st[:, 40:48], st[:, 32:40], [st], [st])
            tt("dve", y3, y3, st[:, 0:8].unsqueeze(2).to_broadcast([128, 8, 64]), ALU.subtract, [y, st], [y])
            tt("dve", y3, y3, st[:, 40:48].unsqueeze(2).to_broadcast([128, 8, 64]), ALU.mult, [y, st], [y])
            tt("pool", y[:], y[:], bcs[:, 1664:2176], ALU.mult, [y, bcs], [y])
            tt("pool", y[:], y[:], bcs[:, 2176:2688], ALU.add, [y, bcs], [y])
            bon = ysq
            tt("dve", bon[:].rearrange("p (a b) -> p a b", b=64), VT[:].rearrange("p c (h e) -> p (c h) e", e=64),
               beta[:, 0:8].unsqueeze(2).to_broadcast([128, 8, 64]), ALU.mult, [VT, beta], [bon])
            tt("pool", y[:], y[:], bon[:], ALU.add, [y, bon], [y])
            ycz = W_ycz.get()
            tt("dve", ycz[:], y[:], zc_[:], ALU.mult, [y, zc_], [ycz])
            ptr = psum()
            ptb = ptr.h[:].bitcast(BF16)
            for ct in range(4):
                tr(ptb[:, ct * 128:(ct + 1) * 128], ycz[:, ct * 128:(ct + 1) * 128], ident_b, [ycz, cb], [ptr])
            yco = W_yco.get()
            cp("act", yco[:].rearrange("p a b -> p (a b)"), ptb[:, 0:512], [ptr], [yco])
            S.dma("pool", ycTv[:, :, t0:t0 + 128], yco[:], [yco], [])

        def lockstep(gens):
            while gens:
                nxt = []
                for g in gens:
                    try:
                        next(g)
                        nxt.append(g)
                    except StopIteration:
                        pass
                gens = nxt

        def preps(d_, c_, par):
            lab = prep_chunk(d_, c_)
            return [prep_gen(d_, c_, par, ct, lab) for ct in range(4)]

        ybank_i = 0
        for d_ in range(2 if upto >= 5 else 0):
            memset("dve", STb[d_][:], 0.0, [STb[d_]] + [STB[d_][c4][h2] for c4 in range(4) for h2 in range(2)])
            order = [0, 1] + list(range(2, NTILE)) if d_ == 0 else [1, 0] + list(range(NTILE - 1, 1, -1))
            order = order[:int(_os.environ.get('RW_CHUNKS', '99'))]
            lockstep(preps(d_, order[0], 0))
            for i, c_ in enumerate(order):
                par = i % 2
                ybk = banks[6 + (ybank_i % 2)]
                ybank_i += 1
                gens = [pair_gen(d_, par, ct, ybk) for ct in range(4)]
                if i + 1 < len(order):
                    gens = gens + preps(d_, order[i + 1], 1 - par)
                lockstep(gens)
                readout(d_, c_, par, ybk)
            S.barrier()
        A.pop()

        A.push()
        pstate["nb"] = 8
        wbr = A.sb([128, 3, 4, D], BF16, "wbr")
        wo = A.sb([128, 8, D], BF16, "wo")
        for i in range(3):
            S.dma("pool", wbr[:, i, :, :], w_br[l, i].rearrange("(ct p) d -> p ct d", p=128), [], [wbr])
        S.dma("pool", wo[:], w_out[l].rearrange("(kc p) e -> p kc e", p=128), [], [wo])
        ybl = [Rot([A.sb([128, 4, 512], BF16, "ybl%d" % i) for _ in range(2)]) for i in range(3)]
        gbl = Rot([A.sb([128, 512], BF16, "gbl") for _ in range(6)])
        mT = Rot([A.sb([128, 8, 512], BF16, "mT") for _ in range(2)])
        macc = Rot([A.sb([128, 512], F32, "macc") for _ in range(3)])
        mtmp = Rot([A.sb([128, 512], F32, "mtmp") for _ in range(3)])
        xin = Rot([A.sb([128, 8, 512], F32, "mxin") for _ in range(2)])
        xo = Rot([A.sb([128, 512], F32, "mxo") for _ in range(3)])
        srcs = [yaT, ybT, ycT]
        xv = xcur.rearrange("(kc p) t -> p kc t", p=128)
        for bi, (t0, n) in enumerate(BLOCKS):
            if (last and bi == 0) or upto < 6:
                continue
            j = 1 if bi == 0 else 0
            yb3 = []
            for i in range(3):
                y_ = ybl[i].get()
                S.dma("sp", y_[:, :, :n], srcs[i].rearrange("(ct p) t -> p ct t", p=128)[:, :, t0:t0 + n], [], [y_])
                yb3.append(y_)
            x_ = xin.get()
            S.dma("sp", x_[:, :, :n], xv[:, :, t0:t0 + n], [], [x_])
            m_ = mT.get()
            for dt_ in range(8):
                ma = macc.get()
                for i in range(3):
                    g_ = gbl.get()
                    r0 = i * D + dt_ * 128
                    S.dma("sp", g_[:, :n], gT[r0:r0 + 128, t0:t0 + n], [], [g_])
                    pb = psum()
                    for ct in range(4):
                        mm(pb[:, :n], wbr[:, i, ct, dt_ * 128:(dt_ + 1) * 128], yb3[i][:, ct, :n], [wbr, yb3[i]], [pb],
                           start=(ct == 0), stop=(ct == 3))
                    if i == 0:
                        tt("dve", ma[:, :n], pb[:, :n], g_[:, :n], ALU.mult, [pb, g_], [ma])
                    else:
                        tp = mtmp.get()
                        tt("dve", tp[:, :n], pb[:, :n], g_[:, :n], ALU.mult, [pb, g_], [tp])
                        if i == 1:
                            tt("pool", ma[:, :n], ma[:, :n], tp[:, :n], ALU.add, [ma, tp], [ma])
                        else:
                            tt("pool", m_[:, dt_, :n], ma[:, :n], tp[:, :n], ALU.add, [ma, tp], [m_])
            for et in range(8):
                pb = psum()
                for dt_ in range(8):
                    mm(pb[:, :n], wo[:, dt_, et * 128:(et + 1) * 128], m_[:, dt_, :n], [wo, m_], [pb],
                       start=(dt_ == 0), stop=(dt_ == 7))
                o = xo.get()
                stt(o[:, :n], pb[:, :n], mod[:, 16 + et, j:j + 1], x_[:, et, :n], ALU.mult, ALU.add, [pb, mod, x_], [o])
                if last:
                    S.dma("pool", yT[et * 128:(et + 1) * 128, t0 - 256:t0 - 256 + n], o[:, :n], [o], [])
                else:
                    S.dma("pool", xnext[et * 128:(et + 1) * 128, t0:t0 + n], o[:, :n], [o], [])
        A.pop()
        A.pop()

    S.barrier()
    if dbg:
        A.push()
        for nm, (src, dst) in dbg_out.items():
            if len(src.shape) == 3:
                for a in range(src.shape[0]):
                    S.dma("sp", dst[a], src[a], [], [])
            else:
                S.dma("sp", dst, src, [], [])
        A.pop()
    if NL < DEPTH and upto >= 6:
        S.dma("sp", yT[:, :], xTs[(NL - 1) % 2][:, 256:TALL], [], [])
    A.pop()
    S.barrier()
    return nc, S


def _consts():
    cf = np.zeros((128, NCF), np.float32)
    cf[:, 0:128] = np.eye(128)
    blk = np.zeros((128, 128), np.float32)
    blk[:64, :64] = 1
    blk[64:, 64:] = 1
    cf[:, 128:256] = blk
    cf[:64, 256] = 1
    cf[64:, 257] = 1
    idx = np.arange(128)
    fw_strict = (idx[:, None] < idx[None, :]).astype(np.float32)
    bw_strict = (idx[:, None] > idx[None, :]).astype(np.float32)
    eye = np.eye(128, dtype=np.float32)
    cf[:, 260:388] = fw_strict
    cf[:, 388:516] = fw_strict + eye
    cf[:, 516:644] = bw_strict
    cf[:, 644:772] = bw_strict + eye
    cf[:, 772:900] = fw_strict.T
    cf[:, 900:1028] = bw_strict.T
    cf[:, 1028:1092] = eye[:, 0:64] + eye[:, 64:128]
    cb = np.zeros((128, NCB), np.float32)
    cb[:, 0:128] = np.eye(128)
    cb[:, 128:256] = 1
    cb[:, 256:384] = blk
    rotT = np.zeros((128, 128), np.float32)
    for m in range(128):
        f = m % 64
        base = m - f
        blk32 = (f // 32) * 32
        g = f % 32
        partner = base + blk32 + (g + 16 if g < 16 else g - 16)
        rotT[partner, m] = 1
    cb[:, 384:512] = rotT
    cb[:64, 512] = 1
    cb[64:, 513] = 1
    for d in range(2):
        for lev in range(7):
            b = 1 << lev
            blk = idx // b
            blk2 = idx // (2 * b)
            same2 = blk2[:, None] == blk2[None, :]
            diff = blk[:, None] != blk[None, :]
            before = (idx[:, None] < idx[None, :]) if d == 0 else (idx[:, None] > idx[None, :])
            m = (same2 & diff & before).astype(np.float32)
            c0 = 516 + (d * 7 + lev) * 128
            cb[:, c0:c0 + 128] = m
    nf = 16
    inv = (10000.0 ** (-np.arange(nf, dtype=np.float32) / nf)).astype(np.float32)
    tpos = np.arange(TLAT)
    row = (tpos // 64).astype(np.float32)
    col = (tpos % 64).astype(np.float32)
    C = np.ones((128, TALL), np.float32)
    Sg = np.zeros((128, TALL), np.float32)
    for p in range(128):
        f = p % 64
        pos = row if f < 32 else col
        g = f % 32
        i = g % 16
        ang = pos * inv[i]
        C[p, 256:] = np.cos(ang)
        Sg[p, 256:] = (-np.sin(ang)) if g < 16 else np.sin(ang)
    return cf, cb, C, Sg


def _host_inputs(inputs):
    f = lambda a: np.ascontiguousarray(np.asarray(a, dtype=np.float32))
    L = DEPTH
    pp = np.zeros((L, 128, NPP), np.float32)
    bc = np.zeros((L, 128, NBC), np.float32)
    b_mod = f(inputs["b_mod"])
    norm_g = f(inputs["norm_g"])
    r_conv = f(inputs["r_conv"])
    for l in range(L):
        pp[l, :, 0:24] = b_mod[l].reshape(24, 128).T
        pp[l, :, 24:32] = norm_g[l].reshape(8, 128).T
        pp[l, :, 32] = np.tile(f(inputs["d_qnorm"])[l], 2)
        pp[l, :, 33] = np.tile(f(inputs["d_knorm"])[l], 2)
        for which in range(3):
            for tap in range(3):
                pp[l, :, 34 + which * 12 + tap * 4:34 + which * 12 + tap * 4 + 4] = r_conv[l, which, tap].reshape(4, 128).T
        for d_ in range(2):
            pp[l, :, 70 + d_ * 4:74 + d_ * 4] = f(inputs["r_w0"])[l, d_].reshape(4, 128).T
            pp[l, :, 78 + d_ * 4:82 + d_ * 4] = f(inputs["r_a0"])[l, d_].reshape(4, 128).T
        pp[l, :, 86:90] = f(inputs["r_kk"])[l].reshape(4, 128).T
        pp[l, :, 90:94] = f(inputs["r_ka"])[l].reshape(4, 128).T
        pp[l, :, 98:102] = f(inputs["r_rk"])[l].reshape(4, 128).T
        row = np.concatenate([f(inputs["a_ln_g"])[l], f(inputs["a_ln_b"])[l], f(inputs["a_bs"])[l].reshape(-1),
                              f(inputs["d_subln_g"])[l], f(inputs["r_ln_g"])[l], f(inputs["r_ln_b"])[l],
                              f(inputs["d_lam"])[l].reshape(-1)])
        bc[l] = np.broadcast_to(row[None, :], (128, NBC))
    cf, cb, C, Sg = _consts()
    shared = dict(pp=pp, bc=bc, cf=cf, cbf=cb, ropeC=C, ropeS=Sg,
                  w_mod=f(inputs["w_mod"]), w_in=f(inputs["w_in"]),
                  a_wsT=np.ascontiguousarray(np.transpose(f(inputs["a_ws"]), (0, 1, 3, 2))),
                  r_w2=f(inputs["r_w2"]), r_a2=f(inputs["r_a2"]), w_br=f(inputs["w_br"]), w_out=f(inputs["w_out"]))
    x = f(inputs["x"])
    c = f(inputs["c"])
    ctx = f(inputs["ctx"])
    c_ctx = f(inputs["c_ctx"])
    maps = []
    for b in range(4):
        xT0 = np.ascontiguousarray(np.concatenate([ctx[b], x[b]], axis=0).T)
        cc = np.stack([c[b].reshape(8, 128).T, c_ctx.reshape(8, 128).T], axis=-1)
        m = dict(shared)
        m["xT0"] = xT0
        m["cc"] = np.ascontiguousarray(cc)
        maps.append(m)
    return maps


_CACHE = {}


def kernel(**inputs):
    maps = _host_inputs(inputs)
    if "nc" not in _CACHE:
        _CACHE["nc"] = build()[0]
    nc = _CACHE["nc"]
    in_maps = [maps[i % 4] for i in range(8)]
    res = run_bass_kernel_spmd(nc, in_maps, core_ids=list(range(8)))
    out = np.stack([np.ascontiguousarray(res.results[b]["yT"].T) for b in range(4)], axis=0)
    return out.astype(np.float32)
```

```python
import numpy as np
import concourse.bass as bass
import concourse.mybir as mybir
from concourse.bass_utils import run_bass_kernel_spmd

F32 = mybir.dt.float32
BF16 = mybir.dt.bfloat16
AF = mybir.ActivationFunctionType
ALU = mybir.AluOpType
AX = mybir.AxisListType

D = 1024
NCTX = 256
TLAT = 4096
TALL = NCTX + TLAT
NTILE = TALL // 128
DEPTH = 4
INC = 8960
C0 = 0.6065306597126334
BLOCKS = [(0, 256)] + [(256 + 512 * i, 512) for i in range(8)]
NPP = 104
NBC = 2944
NCF = 1092
NCB = 2308
NDS = 32


class Buf:
    __slots__ = ("w", "r", "excl")

    def __init__(self):
        self.w = None
        self.r = {}
        self.excl = False


class TileT:
    def __init__(self, h, nb=1):
        self.h = h
        self.b = Buf()

    def __getitem__(self, k):
        return self.h[k]


class Sched:
    def __init__(self, nc):
        self.nc = nc
        self.sems = []
        self.eng = {}
        for name, h in (("pe", nc.tensor), ("act", nc.scalar), ("dve", nc.vector), ("pool", nc.gpsimd), ("sp", nc.sync)):
            sid = len(self.sems)
            self.sems.append(nc.alloc_semaphore("s_" + name))
            self.eng[name] = dict(h=h, sid=sid, n=0, waited={})
        self.dq = {}
        for q in ("sp", "pool", "act"):
            ids = []
            for i in range(NDS):
                ids.append(len(self.sems))
                self.sems.append(nc.alloc_semaphore("d%s%d" % (q, i)))
            self.dq[q] = dict(ids=ids, nxt=0)
        self.dcnt = {}
        self.nins = 0
        import os as _os2
        self.maxops = int(_os2.environ.get('MAXOPS', '1000000000'))
        self.k = 0

    def _wait(self, e, toks):
        E = self.eng[e]
        need = {}
        for t in toks:
            if t is None:
                continue
            sid, val = t
            if sid == E["sid"] and e == "pe":
                continue
            if need.get(sid, 0) < val:
                need[sid] = val
        for sid, val in need.items():
            if E["waited"].get(sid, 0) < val:
                E["h"].wait_ge(self.sems[sid], val)
                E["waited"][sid] = val
                self.nins += 1

    def _deps(self, reads, writes):
        toks = []
        for b in reads:
            toks.append(b.w)
        for b in writes:
            toks.append(b.w)
            toks.extend(b.r.items())
        return toks

    def _mark(self, tok, reads, writes):
        for b in reads:
            if b.r.get(tok[0], 0) < tok[1]:
                b.r[tok[0]] = tok[1]
        for b in writes:
            b.w = tok
            b.r = {}

    def op(self, e, emit, reads=(), writes=()):
        self.k += 1
        if self.k > self.maxops:
            return
        reads = [x.b if isinstance(x, TileT) else x for x in reads]
        writes = [x.b if isinstance(x, TileT) else x for x in writes]
        writes = writes + [b for b in reads if b.excl]
        reads = [b for b in reads if not b.excl]
        self._wait(e, self._deps(reads, writes))
        E = self.eng[e]
        E["n"] += 1
        ins = emit(E["h"])
        ins.then_inc(self.sems[E["sid"]], 1)
        self.nins += 1
        self._mark((E["sid"], E["n"]), reads, writes)

    def dma(self, q, out, in_, reads=(), writes=()):
        self.k += 1
        if self.k > self.maxops:
            return
        reads = [x.b if isinstance(x, TileT) else x for x in reads]
        writes = [x.b if isinstance(x, TileT) else x for x in writes]
        self._wait(q, self._deps(reads, writes))
        Q = self.dq[q]
        sid = Q["ids"][Q["nxt"] % NDS]
        Q["nxt"] += 1
        prev = self.dcnt.get(sid, 0)
        if prev > 0:
            self._wait(q, [(sid, 16 * prev)])
        self.dcnt[sid] = prev + 1
        self.eng[q]["h"].dma_start(out=out, in_=in_).then_inc(self.sems[sid], 16)
        self.nins += 1
        self._mark((sid, 16 * (prev + 1)), reads, writes)

    def barrier(self):
        toks = [(E["sid"], E["n"]) for E in self.eng.values() if E["n"] > 0]
        toks += [(sid, 16 * c) for sid, c in self.dcnt.items()]
        for e in self.eng:
            E = self.eng[e]
            for sid, val in toks:
                if sid == E["sid"]:
                    continue
                if E["waited"].get(sid, 0) < val:
                    E["h"].wait_ge(self.sems[sid], val)
                    E["waited"][sid] = val


class Alloc:
    def __init__(self, nc, S):
        self.nc = nc
        self.S = S
        self.stack = []
        self.cnt = 0

    def push(self):
        self.stack.append([])

    def pop(self):
        self.S.barrier()
        for cm in reversed(self.stack.pop()):
            cm.__exit__(None, None, None)

    def sb(self, shape, dt, name=None):
        self.cnt += 1
        cm = self.nc.sbuf_tensor("%s_%d" % (name or "t", self.cnt), list(shape), dt)
        h = cm.__enter__()
        self.stack[-1].append(cm)
        return TileT(h)


def build(NL=DEPTH, dbg=None, upto=9):
    import os as _os
    lo = int(_os.environ.get('SKIP_TO', '0'))
    RWCUT = float(_os.environ.get('RW_CUT', '99'))
    nc = bass.Bass("TRN2", target_bir_lowering=False)
    S = Sched(nc)
    A = Alloc(nc, S)

    def din(name, shape, dt=F32):
        return nc.dram_tensor(name, list(shape), dt, kind="ExternalInput").ap()

    def dscr(name, shape, dt=F32):
        return nc.dram_tensor(name, list(shape), dt, kind="Internal").ap()

    xT0 = din("xT0", [D, TALL])
    cc_d = din("cc", [128, 8, 2])
    pp_d = din("pp", [DEPTH, 128, NPP])
    bc_d = din("bc", [DEPTH, 128, NBC])
    cf_d = din("cf", [128, NCF])
    cb_d = din("cbf", [128, NCB])
    ropeC_d = din("ropeC", [128, TALL])
    ropeS_d = din("ropeS", [128, TALL])
    w_mod = din("w_mod", [DEPTH, D, 3 * D])
    w_in = din("w_in", [DEPTH, D, INC])
    a_wsT = din("a_wsT", [DEPTH, 4, 128, 128])
    r_w2 = din("r_w2", [DEPTH, 2, 64, 512])
    r_a2 = din("r_a2", [DEPTH, 2, 64, 512])
    w_br = din("w_br", [DEPTH, 3, 512, D])
    w_out = din("w_out", [DEPTH, D, D])
    yT = nc.dram_tensor("yT", [D, TLAT], F32, kind="ExternalOutput").ap()

    xTs = [dscr("xTa", [D, TALL]), dscr("xTb", [D, TALL])]
    qT = dscr("qT", [4, 128, TALL], BF16)
    kT = dscr("kT", [4, 128, TALL], BF16)
    Vd = dscr("Vd", [TALL, 512], BF16)
    krT = dscr("krT", [512, TALL])
    vrT = dscr("vrT", [512, TALL])
    rrT = dscr("rrT", [512, TALL])
    wlT = dscr("wlT", [128, TALL])
    alT = dscr("alT", [128, TALL])
    uzT = dscr("uzT", [512, TALL], BF16)
    vln = dscr("vln", [TALL, 512], BF16)
    zbd = dscr("zbd", [TALL, 512], BF16)
    zcd = dscr("zcd", [TALL, 512], BF16)
    gT = dscr("gT", [3 * D, TALL], BF16)
    yaT = dscr("yaT", [512, TALL], BF16)
    ybT = dscr("ybT", [512, TALL], BF16)
    ycT = dscr("ycT", [512, TALL], BF16)
    yfw = dscr("yfw", [TALL, 512])
    dbg_out = {}
    if dbg:
        for nm in dbg:
            src = dict(qT=qT, kT=kT, Vd=Vd, krT=krT, vrT=vrT, rrT=rrT, wlT=wlT, alT=alT, uzT=uzT, vln=vln, zbd=zbd,
                       zcd=zcd, gT=gT, yaT=yaT, ybT=ybT, ycT=ycT, yfw=yfw, xTa=xTs[0], xTb=xTs[1])[nm]
            dbg_out[nm] = (src, nc.dram_tensor("dbg_" + nm, list(src.shape), src.dtype, kind="ExternalOutput").ap())

    banks = [TileT(nc.alloc_psum_tensor("ps%d" % i, [128, 512], F32)) for i in range(8)]
    for b_k in banks:
        b_k.b.excl = True
    pstate = dict(i=0, nb=8)

    def psum():
        b = banks[pstate["i"] % pstate["nb"]]
        pstate["i"] += 1
        return b

    def mm(out, lhsT, rhs, R, W, start=True, stop=True):
        S.op("pe", lambda h: h.matmul(out, lhsT, rhs, start=start, stop=stop), R, W)

    def tr(out, in_, ident, R, W):
        S.op("pe", lambda h: h.transpose(out, in_, ident), R, W)

    def act(out, in_, func, R, W, bias=None, scale=None, accum=None, eng="act"):
        kw = {}
        if bias is not None:
            kw["bias"] = bias
        if scale is not None:
            kw["scale"] = scale
        if accum is not None:
            kw["accum_out"] = accum
        S.op("act", lambda h: h.activation(out, in_, func, **kw), R, W)

    def tt(e, out, in0, in1, op, R, W):
        S.op(e, lambda h: h.tensor_tensor(out, in0, in1, op), R, W)

    def ts(e, out, in0, s1, s2, op0, op1, R, W):
        if s2 is None:
            S.op(e, lambda h: h.tensor_scalar(out, in0, s1, None, op0), R, W)
        else:
            S.op(e, lambda h: h.tensor_scalar(out, in0, s1, s2, op0, op1), R, W)

    def stt(out, in0, sc, in1, op0, op1, R, W):
        S.op("dve", lambda h: h.scalar_tensor_tensor(out, in0, sc, in1, op0, op1), R, W)

    def cp(e, out, in_, R, W):
        if e == "act":
            S.op("act", lambda h: h.copy(out, in_), R, W)
        else:
            S.op(e, lambda h: h.tensor_copy(out, in_), R, W)

    def recip(out, in_, R, W):
        S.op("dve", lambda h: h.reciprocal(out, in_), R, W)

    def memset(e, ap, val, W):
        S.op(e, lambda h: h.memset(ap, val), [], W)

    class Rot:
        def __init__(self, tiles):
            self.t = tiles
            self.i = 0

        def get(self):
            t = self.t[self.i % len(self.t)]
            self.i += 1
            return t

    A.push()
    cf = A.sb([128, NCF], F32, "cf")
    cb = A.sb([128, NCB], BF16, "cb")
    cc = A.sb([128, 8, 2], F32, "cc")
    cact = A.sb([128, 8, 2], F32, "cact")
    epsT = A.sb([128, 4], F32, "eps")
    onesf = A.sb([128, 128], F32, "onesf")
    S.dma("sp", cf[:], cf_d[:, :], [], [cf])
    S.dma("pool", cb[:], cb_d[:, :], [], [cb])
    S.dma("sp", cc[:], cc_d[:, :, :], [], [cc])
    act(cact[:], cc[:], AF.Silu, [cc], [cact])
    memset("dve", epsT[:, 0:1], 1e-6, [epsT])
    memset("dve", epsT[:, 1:2], 1e-5, [epsT])
    memset("dve", epsT[:, 2:3], 64e-5, [epsT])
    memset("dve", epsT[:, 3:4], 0.0, [epsT])
    memset("dve", onesf[:], 1.0, [onesf])
    ident_f = lambda ps, cs: cf.h[ps, cs]
    blk64_f = cf.h[:, 128:256]
    hsel_f = cf.h[:, 256:258]
    identfold = cf.h[:, 1028:1092]
    maskT = [cf.h[:, 260:516], cf.h[:, 516:772]]
    maskL = [cf.h[:, 772:900], cf.h[:, 900:1028]]
    ident_b = cb.h[:, 0:128]
    ones_b = cb.h[:, 128:256]
    blk64_b = cb.h[:, 256:384]
    rotT_b = cb.h[:, 384:512]
    hsel_b = cb.h[:, 512:514]

    def mLT(d, lev):
        c0 = 516 + (d * 7 + lev) * 128
        return cb.h[:, c0:c0 + 128]

    for l in range(NL):
        lam_init = 0.8 - 0.6 * float(np.exp(-0.3 * l))
        xcur = xT0 if l == 0 else xTs[(l - 1) % 2]
        xnext = xTs[l % 2]
        last = l == DEPTH - 1
        A.push()
        pp = A.sb([128, NPP], F32, "pp")
        bcs = A.sb([128, NBC], F32, "bc")
        mod = A.sb([128, 24, 2], F32, "mod")
        gs = A.sb([128, 8, 2], F32, "gs")
        lamt = A.sb([128, 8], F32, "lam")
        sublng = A.sb([128, 128], F32, "sublng")
        omka = A.sb([128, 4], F32, "omka")
        S.dma("sp", pp[:], pp_d[l, :, :], [], [pp])
        S.dma("sp", bcs[:], bc_d[l, :, :], [], [bcs])

        A.push()
        wm = Rot([A.sb([128, 8, 512], F32, "wm") for _ in range(2)])
        pb = psum()
        wmv = w_mod[l].rearrange("(kc p) n -> p kc n", p=128)
        for g6 in range(6):
            w = wm.get()
            S.dma("sp", w[:], wmv[:, :, g6 * 512:(g6 + 1) * 512], [], [w])
            for nt4 in range(4):
                nt = g6 * 4 + nt4
                for kc in range(8):
                    mm(pb[:, nt * 2:nt * 2 + 2], w[:, kc, nt4 * 128:(nt4 + 1) * 128], cact[:, kc, :], [w, cact], [pb],
                       start=(kc == 0), stop=(kc == 7))
        tt("dve", mod[:], pb[:, 0:48].rearrange("p (a b) -> p a b", b=2),
           pp[:, 0:24].unsqueeze(2).to_broadcast([128, 24, 2]), ALU.add, [pb, pp], [mod])
        ts("dve", gs[:], mod[:, 8:16, :], 1.0, None, ALU.add, None, [mod], [gs])
        tt("dve", gs[:], gs[:], pp[:, 24:32].unsqueeze(2).to_broadcast([128, 8, 2]), ALU.mult, [gs, pp], [gs])
        lamtmp = A.sb([128, 2, 64], F32, "lamtmp")
        dl = bcs[:, 2688:2944].rearrange("p (a b) -> p a b", b=64)
        tt("dve", lamtmp[:, 0, :], dl[:, 0, :], dl[:, 1, :], ALU.mult, [bcs], [lamtmp])
        tt("dve", lamtmp[:, 1, :], dl[:, 2, :], dl[:, 3, :], ALU.mult, [bcs], [lamtmp])
        S.op("dve", lambda h: h.tensor_reduce(lamt[:, 0:2], lamtmp[:], AX.X, ALU.add), [lamtmp], [lamt])
        act(lamt[:, 2:4], lamt[:, 0:2], AF.Exp, [lamt], [lamt])
        tt("dve", lamt[:, 4:5], lamt[:, 2:3], lamt[:, 3:4], ALU.subtract, [lamt], [lamt])
        ts("dve", lamt[:, 5:6], lamt[:, 4:5], lam_init, -1.0, ALU.add, ALU.mult, [lamt], [lamt])
        ts("dve", sublng[:], bcs[:, 1536:1664], 1.0 - lam_init, None, ALU.mult, None, [bcs], [sublng])
        ts("dve", omka[:], pp[:, 90:94], -1.0, 1.0, ALU.mult, ALU.add, [pp], [omka])
        neg_lam = lamt.h[:, 5:6]
        A.pop()

        A.push()
        hT = A.sb([128, 8, TALL], BF16, "hT")
        hTb = [Buf() for _ in BLOCKS]
        A.push()
        xin = Rot([A.sb([128, 8, 512], F32, "xin") for _ in range(2)])
        sqs = Rot([A.sb([128, 8, 512], BF16, "sq") for _ in range(2)])
        rstds = Rot([A.sb([128, 512], F32, "rstd") for _ in range(2)])
        tmps = Rot([A.sb([128, 512], F32, "ntmp") for _ in range(3)])
        xv = xcur.rearrange("(kc p) t -> p kc t", p=128)
        for bi, (t0, n) in enumerate(BLOCKS if lo <= 1 <= upto else []):
            j = 1 if bi == 0 else 0
            x_ = xin.get()
            S.dma("sp", x_[:, :, :n], xv[:, :, t0:t0 + n], [], [x_])
            sq = sqs.get()
            act(sq[:, :, :n], x_[:, :, :n], AF.Square, [x_], [sq])
            pb = psum()
            for kc in range(8):
                mm(pb[:, :n], ones_b, sq[:, kc, :n], [cb, sq], [pb], start=(kc == 0), stop=(kc == 7))
            rs = rstds.get()
            act(rs[:, :n], pb[:, :n], AF.Sqrt, [pb, epsT], [rs], bias=epsT[:, 0:1], scale=1.0 / D)
            recip(rs[:, :n], rs[:, :n], [rs], [rs])
            for kc in range(8):
                tm = tmps.get()
                stt(tm[:, :n], x_[:, kc, :n], gs[:, kc, j:j + 1], rs[:, :n], ALU.mult, ALU.mult, [x_, gs, rs], [tm])
                act(hT[:, kc, t0:t0 + n], tm[:, :n], AF.Identity, [tm, mod], [hTb[bi]], bias=mod[:, kc, j:j + 1])
        A.pop()

        A.push()
        wbufs = Rot([A.sb([128, 8, 512], BF16, "wbuf") for _ in range(4)])
        ostf = Rot([A.sb([128, 512], F32, "ostf") for _ in range(4)])
        ostb = Rot([A.sb([128, 512], BF16, "ostb") for _ in range(4)])
        ptmp = Rot([A.sb([128, 512], F32, "ptmp") for _ in range(6)])
        ptmpb = Rot([A.sb([128, 512], BF16, "ptmpb") for _ in range(4)])
        ropeCs = Rot([A.sb([128, 512], F32, "rC") for _ in range(2)])
        ropeSs = Rot([A.sb([128, 512], F32, "rS") for _ in range(2)])
        stat = Rot([A.sb([128, 16], F32, "stat") for _ in range(4)])
        winv = w_in[l].rearrange("(kc p) n -> p kc n", p=128)

        preq = {}

        def loadw_raw(c0, ncols=512):
            w = wbufs.get()
            S.dma("pool", w[:, :, :ncols], winv[:, :, c0:c0 + ncols], [], [w])
            return w

        def loadw(c0, ncols=512):
            if c0 in preq:
                return preq.pop(c0)
            return loadw_raw(c0, ncols)

        def prefetch(c0, ncols=512):
            preq[c0] = loadw_raw(c0, ncols)

        def fm_mm(w, ct, bi, t0, n):
            pb = psum()
            for kc in range(8):
                mm(pb[:, :n], w[:, kc, ct * 128:(ct + 1) * 128], hT[:, kc, t0:t0 + n], [w, hTb[bi]], [pb],
                   start=(kc == 0), stop=(kc == 7))
            return pb

        def tm_mm(w, t0):
            pb = psum()
            bi = 0 if t0 < 256 else 1 + (t0 - 256) // 512
            for kc in range(8):
                mm(pb[:, :], hT[:, kc, t0:t0 + 128], w[:, kc, :], [w, hTb[bi]], [pb], start=(kc == 0), stop=(kc == 7))
            return pb

        def fm_raw(c0, ncols, dst, alt=[0]):
            w = loadw(c0, ncols)
            for bi, (t0, n) in enumerate(BLOCKS):
                for ct in range(ncols // 128):
                    pb = fm_mm(w, ct, bi, t0, n)
                    o = ostf.get()
                    alt[0] ^= 1
                    cp("act" if alt[0] else "dve", o[:, :n], pb[:, :n], [pb], [o])
                    S.dma("pool", dst[ct * 128:(ct + 1) * 128, t0:t0 + n], o[:, :n], [o], [])

        def fm_qk(c0, dst, gcol):
            w = loadw(c0)
            for bi, (t0, n) in enumerate(BLOCKS):
                rC = ropeCs.get()
                rS = ropeSs.get()
                S.dma("sp", rC[:, :n], ropeC_d[:, t0:t0 + n], [], [rC])
                S.dma("sp", rS[:, :n], ropeS_d[:, t0:t0 + n], [], [rS])
                for ct in range(4):
                    pb = fm_mm(w, ct, bi, t0, n)
                    sqb = ptmpb.get()
                    act(sqb[:, :n], pb[:, :n], AF.Square, [pb], [sqb])
                    p2 = psum()
                    mm(p2[:, :n], blk64_b, sqb[:, :n], [cb, sqb], [p2])
                    rs = ptmp.get()
                    act(rs[:, :n], p2[:, :n], AF.Sqrt, [p2, epsT], [rs], bias=epsT[:, 0:1], scale=1.0 / 64)
                    recip(rs[:, :n], rs[:, :n], [rs], [rs])
                    xn = ptmpb.get()
                    stt(xn[:, :n], pb[:, :n], pp[:, gcol:gcol + 1], rs[:, :n], ALU.mult, ALU.mult, [pb, pp, rs], [xn])
                    p3 = psum()
                    mm(p3[:, :n], rotT_b, xn[:, :n], [cb, xn], [p3])
                    t1 = ptmp.get()
                    tt("pool", t1[:, :n], xn[:, :n], rC[:, :n], ALU.mult, [xn, rC], [t1])
                    t2 = ptmp.get()
                    tt("dve", t2[:, :n], p3[:, :n], rS[:, :n], ALU.mult, [p3, rS], [t2])
                    o = ostb.get()
                    tt("pool", o[:, :n], t1[:, :n], t2[:, :n], ALU.add, [t1, t2], [o])
                    S.dma("pool", dst[ct, :, t0:t0 + n], o[:, :n], [o], [])

        def tm_group(c0, kind, dst):
            w = loadw(c0)
            for ti in range(NTILE):
                t0 = ti * 128
                pb = tm_mm(w, t0)
                o = ostb.get()
                if kind == "copy":
                    cp("act" if ti % 2 else "dve", o[:], pb[:], [pb], [o])
                elif kind == "silu":
                    act(o[:], pb[:], AF.Silu, [pb], [o])
                else:
                    tA = ptmp.get()
                    tB = ptmp.get()
                    g = ptmp.get()
                    act(tA[:], pb[:], AF.Square, [pb], [tA])
                    ts("dve", tA[:], tA[:], 0.044715, 1.0, ALU.mult, ALU.add, [tA], [tA])
                    tt("dve", tA[:], tA[:], pb[:], ALU.mult, [tA, pb], [tA])
                    act(tB[:], tA[:], AF.Sigmoid, [tA], [tB], scale=1.5957691216057308)
                    tt("dve", g[:], tB[:], pb[:], ALU.mult, [tB, pb], [g])
                    st = stat.get()
                    S.op("dve", lambda h: h.bn_stats(st[:, 0:6], g[:]), [g], [st])
                    S.op("dve", lambda h: h.bn_aggr(st[:, 8:10], st[:, 0:6]), [st], [st])
                    act(st[:, 10:11], st[:, 9:10], AF.Sqrt, [st, epsT], [st], bias=epsT[:, 1:2], scale=1.0)
                    recip(st[:, 11:12], st[:, 10:11], [st], [st])
                    ts("dve", g[:], g[:], st[:, 8:9], st[:, 11:12], ALU.subtract, ALU.mult, [g, st], [g])
                    tt("pool", g[:], g[:], bcs[:, 0:512], ALU.mult, [g, bcs], [g])
                    tt("pool", o[:], g[:], bcs[:, 512:1024], ALU.add, [g, bcs], [o])
                S.dma("pool", dst[t0:t0 + 128, :], o[:], [o], [])

        def fm_uz():
            wu = loadw(3328)
            wz = loadw(4352)
            for bi, (t0, n) in enumerate(BLOCKS):
                for ct in range(4):
                    pu = fm_mm(wu, ct, bi, t0, n)
                    pz = fm_mm(wz, ct, bi, t0, n)
                    tA = ptmp.get()
                    tB = ptmp.get()
                    g = ptmp.get()
                    act(tA[:, :n], pu[:, :n], AF.Square, [pu], [tA])
                    ts("dve", tA[:, :n], tA[:, :n], 0.044715, 1.0, ALU.mult, ALU.add, [tA], [tA])
                    tt("dve", tA[:, :n], tA[:, :n], pu[:, :n], ALU.mult, [tA, pu], [tA])
                    act(tB[:, :n], tA[:, :n], AF.Sigmoid, [tA], [tB], scale=1.5957691216057308)
                    tt("dve", g[:, :n], tB[:, :n], pu[:, :n], ALU.mult, [tB, pu], [g])
                    sz = ptmp.get()
                    act(sz[:, :n], pz[:, :n], AF.Silu, [pz], [sz])
                    o = ostb.get()
                    tt("pool", o[:, :n], g[:, :n], sz[:, :n], ALU.mult, [g, sz], [o])
                    S.dma("pool", uzT[ct * 128:(ct + 1) * 128, t0:t0 + n], o[:, :n], [o], [])

        def fm_gl(c0, gi):
            w = loadw(c0)
            for bi, (t0, n) in enumerate(BLOCKS):
                for ct in range(4):
                    pb = fm_mm(w, ct, bi, t0, n)
                    o = ostb.get()
                    act(o[:, :n], pb[:, :n], AF.Sigmoid, [pb], [o])
                    r0 = gi * 512 + ct * 128
                    S.dma("pool", gT[r0:r0 + 128, t0:t0 + n], o[:, :n], [o], [])

        if lo <= 2 <= upto:
            prefetch(0)
            prefetch(512)
            fm_qk(0, kT, 33)
            prefetch(1024)
            tm_group(512, "copy", Vd)
            prefetch(1536)
            fm_raw(1024, 512, krT)
            prefetch(2048, 128)
            fm_raw(1536, 512, vrT)
            prefetch(2176, 128)
            fm_raw(2048, 128, wlT)
            prefetch(2304)
            fm_raw(2176, 128, alT)
            prefetch(2816)
            fm_qk(2304, qT, 32)
            prefetch(3328)
            fm_raw(2816, 512, rrT)
            prefetch(4352)
            fm_uz()
            prefetch(3840)
            prefetch(4864)
            tm_group(3840, "va", vln)
            prefetch(5376)
            tm_group(4864, "silu", zbd)
            prefetch(5888)
            tm_group(5376, "silu", zcd)
            for gi in range(6):
                if gi + 1 < 6:
                    prefetch(5888 + (gi + 1) * 512)
                fm_gl(5888 + gi * 512, gi)
        A.pop()
        A.pop()

        A.push()
        wsT = A.sb([128, 4, 128], BF16, "wsT")
        S.dma("pool", wsT[:], a_wsT[l].rearrange("g q p -> q g p"), [], [wsT])
        vlb = Rot([A.sb([128, 4, 512], BF16, "vlb") for _ in range(2)])
        uzb = Rot([A.sb([128, 4, 512], BF16, "uzb") for _ in range(2)])
        gtmp = Rot([A.sb([128, 512], F32, "gtmp") for _ in range(3)])
        gost = Rot([A.sb([128, 512], BF16, "gost") for _ in range(3)])
        vlnv = vln.rearrange("(t p) c -> p t c", p=128)
        uzv = uzT.rearrange("(ct p) t -> p ct t", p=128)
        for bi, (t0, n) in enumerate(BLOCKS if lo <= 3 <= upto else []):
            nt = n // 128
            vb_ = vlb.get()
            ub_ = uzb.get()
            S.dma("sp", vb_[:, :nt, :], vlnv[:, t0 // 128:t0 // 128 + nt, :], [], [vb_])
            S.dma("sp", ub_[:, :, :n], uzv[:, :, t0:t0 + n], [], [ub_])
            for g in range(4):
                pb = psum()
                for ti in range(nt):
                    mm(pb[:, ti * 128:(ti + 1) * 128], vb_[:, ti, g * 128:(g + 1) * 128], wsT[:, g, :], [vb_, wsT], [pb])
                tg = gtmp.get()
                tt("dve", tg[:, :n].rearrange("p (a b) -> p a b", b=128), pb[:, :n].rearrange("p (a b) -> p a b", b=128),
                   bcs[:, 1024 + g * 128:1024 + (g + 1) * 128].unsqueeze(1).to_broadcast([128, nt, 128]), ALU.add,
                   [pb, bcs], [tg])
                o = gost.get()
                tt("pool", o[:, :n], tg[:, :n], ub_[:, g, :n], ALU.mult, [tg, ub_], [o])
                S.dma("pool", yaT[g * 128:(g + 1) * 128, t0:t0 + n], o[:, :n], [o], [])
        A.pop()

        A.push()
        pstate["nb"] = 6
        kTh = Rot([A.sb([128, TALL], BF16, "kTh") for _ in range(2)])
        Vh = Rot([A.sb([128, NTILE, 132], BF16, "Vh") for _ in range(2)])
        for v_ in Vh.t:
            memset("pool", v_[:, :, 128:132], 1.0, [v_])
        PTs = Rot([A.sb([128, NTILE, 512], BF16, "PT") for _ in range(3)])
        qTb = Rot([A.sb([128, 512], BF16, "qTb") for _ in range(2)])
        zbt = Rot([A.sb([128, 4, 128], BF16, "zbt") for _ in range(2)])
        ao = Rot([A.sb([128, 128], F32, "ao") for _ in range(6)])
        ast = Rot([A.sb([128, 8], F32, "ast") for _ in range(4)])
        ayb = Rot([A.sb([128, 128], BF16, "ayb") for _ in range(3)])
        aost = Rot([A.sb([128, 512], BF16, "aost") for _ in range(2)])
        Vv = Vd.rearrange("(t p) c -> p t c", p=128)
        zbv = zbd.rearrange("(t p) c -> p t c", p=128)
        ao0 = [A.sb([128, 128], F32, "ao0") for _ in range(4)]
        pvbank_i = [0]

        def qke_gen(kh, qb, n, kts, j, PTj):
            js = slice(j * 64, (j + 1) * 64)
            for kt in kts:
                pb = psum()
                mm(pb[:, :n], kh[js, kt * 128:(kt + 1) * 128], qb[js, :n], [kh, qb], [pb])
                act(PTj[:, kt, :n], pb[:, :n], AF.Exp, [pb], [PTj], scale=0.125)
                yield

        def pv_gen(vh, PTj, j, kts, nt, zt, ob, h_, t0, n):
            for ti in range(nt):
                pb = banks[6 + (pvbank_i[0] % 2)]
                pvbank_i[0] += 1
                for ki, kt in enumerate(kts):
                    mm(pb[:, 0:129], PTj[:, kt, ti * 128:(ti + 1) * 128], vh[:, kt, 0:129], [PTj, vh], [pb],
                       start=(ki == 0), stop=(ki == len(kts) - 1))
                    if ki % 6 == 5:
                        yield
                st = ast.get()
                recip(st[:, 0:1], pb[:, 128:129], [pb], [st])
                if j == 0:
                    ts("dve", ao0[ti][:], pb[:, 0:128], st[:, 0:1], None, ALU.mult, None, [pb, st], [ao0[ti]])
                    yield
                    continue
                o1 = ao.get()
                ts("dve", o1[:], pb[:, 0:128], st[:, 0:1], None, ALU.mult, None, [pb, st], [o1])
                od = ao.get()
                stt(od[:], o1[:], neg_lam, ao0[ti][:], ALU.mult, ALU.add, [o1, ao0[ti], lamt], [od])
                st2 = ast.get()
                junk = ao.get()
                act(junk[:], od[:], AF.Square, [od], [junk, st2], accum=st2[:, 0:1])
                yield
                act(st2[:, 1:2], st2[:, 0:1], AF.Sqrt, [st2, epsT], [st2], bias=epsT[:, 0:1], scale=1.0 / 128)
                recip(st2[:, 2:3], st2[:, 1:2], [st2], [st2])
                stt(od[:], od[:], st2[:, 2:3], sublng[:], ALU.mult, ALU.mult, [od, st2, sublng], [od])
                yb_ = ayb.get()
                tt("dve", yb_[:], od[:], zt[:, ti, :], ALU.mult, [od, zt], [yb_])
                pbt_ = psum()
                pbb = pbt_.h[:].bitcast(BF16)
                tr(pbb[:, 0:128], yb_[:], ident_b, [yb_, cb], [pbt_])
                cp("act", ob[:, ti * 128:(ti + 1) * 128], pbb[:, 0:128], [pbt_], [ob])
                yield
            if j == 1:
                S.dma("pool", ybT[h_ * 128:(h_ + 1) * 128, t0:t0 + n], ob[:, :n], [ob], [])

        def run2(g_main, g_side, ratio=2):
            while g_main is not None or g_side is not None:
                if g_main is not None:
                    try:
                        next(g_main)
                    except StopIteration:
                        g_main = None
                if g_side is not None:
                    for _ in range(ratio):
                        try:
                            next(g_side)
                        except StopIteration:
                            g_side = None
                            break

        for h_ in range(4 if lo <= 4 <= upto else 0):
            kh = kTh.get()
            vh = Vh.get()
            S.dma("sp", kh[:], kT[h_, :, :], [], [kh])
            for q0 in range(0, NTILE, 9):
                q1 = min(NTILE, q0 + 9)
                S.dma("sp", vh[:, q0:q1, 0:128], Vv[:, q0:q1, h_ * 128:(h_ + 1) * 128], [], [vh])

            def blk_setup(bi):
                t0, n = BLOCKS[bi]
                nt = n // 128
                kts = [0, 1] if bi == 0 else list(range(NTILE))
                qb = qTb.get()
                S.dma("sp", qb[:, :n], qT[h_, :, t0:t0 + n], [], [qb])
                zt = zbt.get()
                S.dma("sp", zt[:, :nt, :], zbv[:, t0 // 128:t0 // 128 + nt, h_ * 128:(h_ + 1) * 128], [], [zt])
                return dict(t0=t0, n=n, nt=nt, kts=kts, qb=qb, zt=zt, PT=[None, None])

            cur = blk_setup(0)
            for j in range(2):
                cur["PT"][j] = PTs.get()
                run2(qke_gen(kh, cur["qb"], cur["n"], cur["kts"], j, cur["PT"][j]), None)
            for bi in range(len(BLOCKS)):
                nxt = blk_setup(bi + 1) if bi + 1 < len(BLOCKS) else None
                ob = aost.get()
                for j in range(2):
                    side = None
                    if nxt is not None:
                        nxt["PT"][j] = PTs.get()
                        side = qke_gen(kh, nxt["qb"], nxt["n"], nxt["kts"], j, nxt["PT"][j])
                    run2(pv_gen(vh, cur["PT"][j], j, cur["kts"], cur["nt"], cur["zt"], ob, h_, cur["t0"], cur["n"]), side)
                cur = nxt
        A.pop()

        A.push()
        pstate["nb"] = 7
        w2a2 = [A.sb([128, 512], BF16, "w2a2") for _ in range(2)]
        for d_ in range(2):
            S.dma("pool", w2a2[d_][0:64, :], r_w2[l, d_, :, :], [], [w2a2[d_]])
            S.dma("pool", w2a2[d_][64:128, :], r_a2[l, d_, :, :], [], [w2a2[d_]])
        STb = [A.sb([128, 4, 64], BF16, "STb") for _ in range(2)]
        STB = [[[Buf() for _ in range(2)] for _ in range(4)] for _ in range(2)]
        NW = 2

        def wsf(name, shape=(128, 128), dt=F32, k=NW):
            return Rot([A.sb(list(shape), dt, name) for _ in range(k)])

        W_wla = wsf("wla")
        W_lab = wsf("lab", dt=BF16)
        PT = [dict(krc=A.sb([128, 130], F32, "krc"), vrc=A.sb([128, 130], F32, "vrc"), rrc=A.sb([128, 130], F32, "rrc"),
                   **{nm: A.sb([128, 128], F32, nm) for nm in ("k", "v", "r", "sg", "a", "Gs", "G2", "Gx", "eG", "enG", "eGx",
                                                              "eGC", "kk", "sq", "b", "kd")},
                   **{nm: A.sb([128, 128], BF16, nm) for nm in ("sqb", "BhT", "KhT", "vb", "rkk")}) for ct in range(4)]
        PS = [[dict(AR=A.sb([128, 2, 128], BF16, "AR"), Bt=A.sb([128, 128], BF16, "Bt"), Kt=A.sb([128, 128], BF16, "Kt"),
                    TK=A.sb([128, 3, 128], BF16, "TK"), Rtf=A.sb([128, 128], F32, "Rtf"), sm=A.sb([128, 8], F32, "sm"))
               for ct in range(4)] for par in range(2)]
        VTs = [A.sb([128, 4, 128], BF16, "VT") for par in range(2)]
        betas = [A.sb([128, 8], F32, "beta") for par in range(2)]
        SL = [dict(M1=A.sb([128, 2, 256], BF16, "M1"), M2=A.sb([128, 2, 256], BF16, "M2"), X=A.sb([128, 2, 128], BF16, "X"),
                   T=[A.sb([128, 2, 128], BF16, "T") for _ in range(2)], TT=[A.sb([128, 2, 128], BF16, "TT") for _ in range(2)],
                   Lm=[A.sb([128, 2, 128], BF16, "Lm") for _ in range(8)], W=[A.sb([128, 2, 128], BF16, "W") for _ in range(2)],
                   UA=A.sb([128, 2, 128], BF16, "UA"), UP=A.sb([128, 2, 128], BF16, "UP"), Phi=A.sb([128, 64], BF16, "Phi"),
                   Psi=A.sb([128, 64], BF16, "Psi"), QT=A.sb([128, 128], BF16, "QT")) for _ in range(4)]
        W_yo = wsf("yo", (128, 512), k=2)
        W_y = wsf("y", (128, 512), k=2)
        W_ysq = wsf("ysq", (128, 512), k=2)
        W_st = wsf("rst", (128, 48), k=2)
        W_zc = wsf("zc", (128, 512), BF16, 2)
        W_ycz = wsf("ycz", (128, 512), BF16, 2)
        W_yco = wsf("yco", (128, 4, 128), BF16, 2)
        zcv = zcd.rearrange("(t p) c -> p t c", p=128)
        ycTv = ycT.rearrange("(ct p) t -> p ct t", p=128)

        def conv(dst, src, which, ct):
            base = 34 + which * 12 + ct
            ts("dve", dst[:], src[:, 0:128], pp[:, base:base + 1], None, ALU.mult, None, [src, pp], [dst])
            stt(dst[:], src[:, 1:129], pp[:, base + 4:base + 5], dst[:], ALU.mult, ALU.add, [src, pp, dst], [dst])
            stt(dst[:], src[:, 2:130], pp[:, base + 8:base + 9], dst[:], ALU.mult, ALU.add, [src, pp, dst], [dst])

        def load_halo(tile, src, ct, t0):
            left = not (t0 == 0 or t0 == 256)
            right = not (t0 + 128 == 256 or t0 + 128 == TALL)
            lo = t0 - (1 if left else 0)
            hi = t0 + 128 + (1 if right else 0)
            c_lo = 1 - (t0 - lo)
            if not left:
                memset("pool", tile[:, 0:1], 0.0, [tile])
            if not right:
                memset("pool", tile[:, 129:130], 0.0, [tile])
            S.dma("sp", tile[:, c_lo:c_lo + (hi - lo)], src[ct * 128:(ct + 1) * 128, lo:hi], [], [tile])

        def prep_chunk(d_, c_):
            t0 = c_ * 128
            wla = W_wla.get()
            S.dma("sp", wla[0:64, :], wlT[d_ * 64:(d_ + 1) * 64, t0:t0 + 128], [], [wla])
            S.dma("sp", wla[64:128, :], alT[d_ * 64:(d_ + 1) * 64, t0:t0 + 128], [], [wla])
            lab = W_lab.get()
            act(lab[0:64, :], wla[0:64, :], AF.Tanh, [wla], [lab])
            cp("dve", lab[64:128, :], wla[64:128, :], [wla], [lab])
            return lab

        def prep_gen(d_, c_, par, ct, lab):
            t0 = c_ * 128
            VT = VTs[par]
            beta = betas[par]
            P = PS[par][ct]
            AR, Bt, Kt, TK, Rtf, sm = P["AR"], P["Bt"], P["Kt"], P["TK"], P["Rtf"], P["sm"]
            W = PT[ct]
            krc, vrc, rrc = W["krc"], W["vrc"], W["rrc"]
            load_halo(krc, krT, ct, t0)
            load_halo(vrc, vrT, ct, t0)
            load_halo(rrc, rrT, ct, t0)
            pw = psum()
            mm(pw[:, 0:128], w2a2[d_][0:64, ct * 128:(ct + 1) * 128], lab[0:64, :], [w2a2[d_], lab], [pw])
            sg = W["sg"]
            a_ = W["a"]
            act(sg[:], pw[:, 0:128], AF.Sigmoid, [pw, pp], [sg], bias=pp[:, 70 + d_ * 4 + ct:71 + d_ * 4 + ct])
            yield
            pw2 = psum()
            mm(pw2[:, 128:256], w2a2[d_][64:128, ct * 128:(ct + 1) * 128], lab[64:128, :], [w2a2[d_], lab], [pw2])
            act(a_[:], pw2[:, 128:256], AF.Sigmoid, [pw2, pp], [a_], bias=pp[:, 78 + d_ * 4 + ct:79 + d_ * 4 + ct])
            yield
            k_, v_, r_ = W["k"], W["v"], W["r"]
            for (dst, src, which) in ((r_, rrc, 0), (k_, krc, 1), (v_, vrc, 2)):
                base = 34 + which * 12 + ct
                ts("dve", dst[:], src[:, 0:128], pp[:, base:base + 1], None, ALU.mult, None, [src, pp], [dst])
                yield
                stt(dst[:], src[:, 1:129], pp[:, base + 4:base + 5], dst[:], ALU.mult, ALU.add, [src, pp, dst], [dst])
                yield
                stt(dst[:], src[:, 2:130], pp[:, base + 8:base + 9], dst[:], ALU.mult, ALU.add, [src, pp, dst], [dst])
                yield
            Gs = W["Gs"]
            S.op("dve", lambda h: h.tensor_tensor_scan(Gs[:], onesf[:], sg[:], 0.0, ALU.mult, ALU.add), [onesf, sg], [Gs])
            yield
            if d_ == 0:
                G = Gs
            else:
                G = W["G2"]
                stt(G[:], Gs[:], -1.0, sg[:], ALU.mult, ALU.add, [Gs, sg], [G])
                yield
                ts("dve", G[:], G[:], Gs[:, 127:128], None, ALU.add, None, [G, Gs], [G])
            ts("dve", sm[:, 0:1], Gs[:, 127:128], -C0, None, ALU.mult, None, [Gs], [sm])
            yield
            Gx = W["Gx"]
            tt("pool", Gx[:], G[:], sg[:], ALU.subtract, [G, sg], [Gx])
            eG, enG, eGx, eGC = W["eG"], W["enG"], W["eGx"], W["eGC"]
            act(eG[:], G[:], AF.Exp, [G], [eG], scale=-C0)
            yield
            act(enG[:], G[:], AF.Exp, [G], [enG], scale=C0)
            yield
            act(eGx[:], Gx[:], AF.Exp, [Gx], [eGx], scale=-C0)
            yield
            act(eGC[:], G[:], AF.Exp, [G, sm], [eGC], scale=C0, bias=sm[:, 0:1])
            act(sm[:, 1:2], sm[:, 0:1], AF.Exp, [sm], [sm])
            yield
            kk, sq, sqb = W["kk"], W["sq"], W["sqb"]
            ts("dve", kk[:], k_[:], pp[:, 86 + ct:87 + ct], None, ALU.mult, None, [k_, pp], [kk])
            yield
            act(sqb[:], kk[:], AF.Square, [kk], [sqb])
            yield
            pn = psum()
            mm(pn[:, 0:128], blk64_b, sqb[:], [cb, sqb], [pn])
            act(sq[:], pn[:, 0:128], AF.Sqrt, [pn], [sq])
            yield
            ts("dve", sq[:], sq[:], 1e-12, None, ALU.max, None, [sq], [sq])
            yield
            recip(sq[:], sq[:], [sq], [sq])
            yield
            tt("dve", kk[:], kk[:], sq[:], ALU.mult, [kk, sq], [kk])
            yield
            b_, kd = W["b"], W["kd"]
            tt("pool", b_[:], kk[:], a_[:], ALU.mult, [kk, a_], [b_])
            ts("dve", kd[:], a_[:], pp[:, 90 + ct:91 + ct], omka[:, ct:ct + 1], ALU.mult, ALU.add, [a_, pp, omka], [kd])
            yield
            tt("pool", kd[:], kd[:], k_[:], ALU.mult, [kd, k_], [kd])
            stt(AR[:, 0, :], kk[:], -1.0, eGx[:], ALU.mult, ALU.mult, [kk, eGx], [AR])
            yield
            tt("dve", Rtf[:], r_[:], eG[:], ALU.mult, [r_, eG], [Rtf])
            yield
            cp("act", AR[:, 1, :], Rtf[:], [Rtf], [AR])
            BhT, KhT, vb = W["BhT"], W["KhT"], W["vb"]
            tt("pool", Bt[:], b_[:], enG[:], ALU.mult, [b_, enG], [Bt])
            tt("dve", Kt[:], kd[:], enG[:], ALU.mult, [kd, enG], [Kt])
            yield
            tt("pool", BhT[:], b_[:], eGC[:], ALU.mult, [b_, eGC], [BhT])
            tt("dve", KhT[:], kd[:], eGC[:], ALU.mult, [kd, eGC], [KhT])
            cp("act", vb[:], v_[:], [v_], [vb])
            yield
            ptr = psum()
            ptb = ptr.h[:].bitcast(BF16)
            tr(ptb[:, 0:128], AR[:, 0, :], ident_b, [AR, cb], [ptr])
            tr(ptb[:, 128:256], BhT[:], ident_b, [BhT, cb], [ptr])
            tr(ptb[:, 256:384], KhT[:], ident_b, [KhT, cb], [ptr])
            tr(ptb[:, 384:512], vb[:], ident_b, [vb, cb], [ptr])
            cp("dve", TK[:].rearrange("p a b -> p (a b)"), ptb[:, 0:384], [ptr], [TK])
            cp("dve", VT[:, ct, :], ptb[:, 384:512], [ptr], [VT])
            yield
            if d_ == 1:
                rkk = W["rkk"]
                stt(rkk[:], r_[:], pp[:, 98 + ct:99 + ct], k_[:], ALU.mult, ALU.mult, [r_, pp, k_], [rkk])
                pbt = psum()
                mm(pbt[:, 0:2], rkk[:], hsel_b, [rkk, cb], [pbt])
                cp("act", beta[:, ct * 2:ct * 2 + 2], pbt[:, 0:2], [pbt], [beta])
                yield

        def pair_gen(d_, par, ct, ybk):
            P = PS[par][ct]
            AR, Bt, Kt, TK, Rtf, sm = P["AR"], P["Bt"], P["Kt"], P["TK"], P["Rtf"], P["sm"]
            VT = VTs[par]
            Q = SL[ct]
            M1, M2, X = Q["M1"], Q["M2"], Q["X"]
            ARf = AR[:].rearrange("p a b -> p (a b)")
            HP = [slice(0, 64), slice(64, 128)]
            v3 = lambda ap, e: ap.rearrange("p (h e) -> p h e", e=e)
            bc2 = lambda ap, e: ap.unsqueeze(1).to_broadcast([128, 2, e])
            for hh in range(2):
                hp = HP[hh]
                p1 = psum()
                mm(p1[:, 0:256], Bt[hp, :], ARf[hp, :], [Bt, AR], [p1])
                cp("act", M1[:, hh, :], p1[:, 0:256], [p1], [M1])
                yield
            for hh in range(2):
                hp = HP[hh]
                p2 = psum()
                mm(p2[:, 0:256], Kt[hp, :], ARf[hp, :], [Kt, AR], [p2])
                tt("dve", M2[:, hh, :], p2[:, 0:256], maskT[d_], ALU.mult, [p2, cf], [M2])
                yield
            for hh in range(2):
                hp = HP[hh]
                p3 = psum()
                mm(p3[:, 0:128], AR[hp, 0, :], Bt[hp, :], [AR, Bt], [p3])
                cp("act", X[:, hh, :], p3[:, 0:128], [p3], [X])
                yield
            tt("pool", M1[:, :, 128:256], M1[:, :, 128:256], bc2(maskT[d_][:, 128:256], 128), ALU.mult, [M1, cf], [M1])
            LTap = M1[:, :, 0:128]
            T_ = Q["T"][0]
            TT = Q["TT"][0]
            Lm = Q["Lm"]
            tt("pool", Lm[6][:], LTap, bc2(mLT(d_, 0), 128), ALU.mult, [M1, cb], [Lm[6]])
            tt("pool", TT[:], Lm[6][:], bc2(ident_b, 128), ALU.add, [Lm[6], cb], [TT])
            tt("pool", Lm[7][:], X[:], bc2(mLT(1 - d_, 0), 128), ALU.mult, [X, cb], [Lm[7]])
            tt("pool", T_[:], Lm[7][:], bc2(ident_b, 128), ALU.add, [Lm[7], cb], [T_])
            yield
            for lev in range(1, 7):
                tt("pool", Lm[lev - 1][:], LTap, bc2(mLT(d_, lev), 128), ALU.mult, [M1, cb], [Lm[lev - 1]])
                if lev % 2 == 0:
                    yield
            for lev in range(1, 7):
                LmT = Lm[lev - 1]
                pa = psum()
                for hh in range(2):
                    mm(pa[:, hh * 128:(hh + 1) * 128], LmT[:, hh, :], T_[:, hh, :], [LmT, T_], [pa], start=True, stop=False)
                    mm(pa[:, hh * 128:(hh + 1) * 128], ident_b, ident_b, [cb], [pa], start=False, stop=True)
                Wt = Q["W"][lev % 2]
                cp("act" if lev % 2 else "dve", Wt[:].rearrange("p h e -> p (h e)"), pa[:, 0:256], [pa], [Wt])
                yield
                if lev < 6:
                    pT = psum()
                    for hh in range(2):
                        mm(pT[:, hh * 128:(hh + 1) * 128], TT[:, hh, :], Wt[:, hh, :], [TT, Wt], [pT])
                    Tn = Q["T"][lev % 2]
                    cp("act", Tn[:].rearrange("p h e -> p (h e)"), pT[:, 0:256], [pT], [Tn])
                pTT = psum()
                for hh in range(2):
                    mm(pTT[:, hh * 128:(hh + 1) * 128], Wt[:, hh, :], TT[:, hh, :], [Wt, TT], [pTT])
                TTn = Q["TT"][lev % 2]
                if lev < 6:
                    cp("dve", TTn[:].rearrange("p h e -> p (h e)"), pTT[:, 0:256], [pTT], [TTn])
                else:
                    cp("act", TTn[:].rearrange("p h e -> p (h e)"), pTT[:, 0:256], [pTT], [TTn])
                TT = TTn
                if lev < 6:
                    T_ = Tn
                yield
            UA, UP, Phi, Psi, QT = Q["UA"], Q["UP"], Q["Phi"], Q["Psi"], Q["QT"]
            pwq = psum()
            for hh in range(2):
                mm(pwq[:, hh * 64:(hh + 1) * 64], M2[:, hh, 0:128], VT[:, ct, HP[hh]], [M2, VT], [pwq])
            cp("act", UA[:, :, 0:64], v3(pwq[:, 0:128], 64), [pwq], [UA])
            cp("pool", UA[:, :, 64:128], v3(TK[:, 0, :], 64), [TK], [UA])
            yield
            pu = psum()
            for hh in range(2):
                mm(pu[:, hh * 128:(hh + 1) * 128], TT[:, hh, :], UA[:, hh, :], [TT, UA], [pu])
            cp("act", UP[:].rearrange("p h e -> p (h e)"), pu[:, 0:256], [pu], [UP])
            yield
            pf = psum()
            for hh in range(2):
                hp = HP[hh]
                mm(pf[hp, 0:64], UP[:, hh, 64:128], TK[:, 1, hp], [UP, TK], [pf])
                mm(pf[hp, 64:128], TK[:, 1, hp], UP[:, hh, 0:64], [TK, UP], [pf], start=True, stop=False)
                mm(pf[hp, 64:128], TK[:, 2, hp], VT[:, ct, hp], [TK, VT], [pf], start=False, stop=True)
                mm(pf[hp, 128:256], UP[:, hh, 64:128], M1[:, hh, 128:256], [UP, M1], [pf], start=True, stop=False)
                mm(pf[hp, 128:256], ident_b[:, hp], AR[:, 1, :], [cb, AR], [pf], start=False, stop=True)
            stt(Phi[:], identfold, sm[:, 1:2], pf[:, 0:64], ALU.mult, ALU.add, [cf, sm, pf], [Phi])
            cp("act", Psi[:], pf[:, 64:128], [pf], [Psi])
            cp("act", QT[:], pf[:, 128:256], [pf], [QT])
            yield
            for hh in range(2):
                hp = HP[hh]
                yc0 = ct * 128 + hh * 64
                mm(ybk[:, yc0:yc0 + 64], M1[:, hh, 128:256], UP[:, hh, 0:64], [M1, UP], [ybk], start=True, stop=False)
                mm(ybk[:, yc0:yc0 + 64], M2[:, hh, 128:256], VT[:, ct, hp], [M2, VT], [ybk], start=False, stop=False)
                sbuf_ = STB[d_][ct][hh]
                mm(ybk[:, yc0:yc0 + 64], QT[hp, :], STb[d_][hp, ct, :], [QT, sbuf_], [ybk], start=False, stop=True)
                pS = psum()
                mm(pS[hp, 0:64], Phi[hp, :], STb[d_][hp, ct, :], [Phi, sbuf_], [pS], start=True, stop=False)
                mm(pS[hp, 0:64], ident_b[hp, hp], Psi[hp, :], [cb, Psi], [pS], start=False, stop=True)
                cp("act" if hh else "dve", STb[d_][hp, ct, :], pS[hp, 0:64], [pS], [sbuf_])
                yield

        def readout(d_, c_, par, ybk):
            t0 = c_ * 128
            VT = VTs[par]
            beta = betas[par]
            if d_ == 0:
                yo = W_yo.get()
                cp("act", yo[:], ybk[:], [ybk], [yo])
                S.dma("pool", yfw[t0:t0 + 128, :], yo[:], [yo], [])
                return
            yo = W_yo.get()
            S.dma("sp", yo[:], yfw[t0:t0 + 128, :], [], [yo])
            zc_ = W_zc.get()
            S.dma("sp", zc_[:], zcv[:, c_, :], [], [zc_])
            y = W_y.get()
            tt("dve", y[:], ybk[:], yo[:], ALU.add, [ybk, yo], [y])
            y3 = y[:].rearrange("p (a b) -> p a b", b=64)
            st = W_st.get()
            S.op("dve", lambda h: h.tensor_reduce(st[:, 0:8], y3, AX.X, ALU.add), [y], [st])
            ysq = W_ysq.get()
            act(ysq[:], y[:], AF.Square, [y], [ysq])
            S.op("dve", lambda h: h.tensor_reduce(st[:, 8:16], ysq[:].rearrange("p (a b) -> p a b", b=64), AX.X, ALU.add),
                 [ysq], [st])
            ts("dve", st[:, 0:16], st[:, 0:16], 1.0 / 64, None, ALU.mult, None, [st], [st])
            tt("dve", st[:, 16:24], st[:, 0:8], st[:, 0:8], ALU.mult, [st], [st])
            tt("dve", st[:, 24:32], st[:, 8:16], st[:, 16:24], ALU.subtract, [st], [st])
            act(st[:, 32:40], st[:, 24:32], AF.Sqrt, [st, epsT], [st], bias=epsT[:, 2:3], scale=1.0)
            recip(st[:, 40:48], st[:, 32:40], [st], [st])
            tt("dve", y3, y3, st[:, 0:8].unsqueeze(2).to_broadcast([128, 8, 64]), ALU.subtract, [y, st], [y])
            tt("dve", y3, y3, st[:, 40:48].unsqueeze(2).to_broadcast([128, 8, 64]), ALU.mult, [y, st], [y])
            tt("pool", y[:], y[:], bcs[:, 1664:2176], ALU.mult, [y, bcs], [y])
            tt("pool", y[:], y[:], bcs[:, 2176:2688], ALU.add, [y, bcs], [y])
            bon = ysq
            tt("dve", bon[:].rearrange("p (a b) -> p a b", b=64), VT[:].rearrange("p c (h e) -> p (c h) e", e=64),
               beta[:, 0:8].unsqueeze(2).to_broadcast([128, 8, 64]), ALU.mult, [VT, beta], [bon])
            tt("pool", y[:], y[:], bon[:], ALU.add, [y, bon], [y])
            ycz = W_ycz.get()
            tt("dve", ycz[:], y[:], zc_[:], ALU.mult, [y, zc_], [ycz])
            ptr = psum()
            ptb = ptr.h[:].bitcast(BF16)
            for ct in range(4):
                tr(ptb[:, ct * 128:(ct + 1) * 128], ycz[:, ct * 128:(ct + 1) * 128], ident_b, [ycz, cb], [ptr])
            yco = W_yco.get()
            cp("act", yco[:].rearrange("p a b -> p (a b)"), ptb[:, 0:512], [ptr], [yco])
            S.dma("pool", ycTv[:, :, t0:t0 + 128], yco[:], [yco], [])

        def lockstep(gens):
            while gens:
                nxt = []
                for g in gens:
                    try:
                        next(g)
                        nxt.append(g)
                    except StopIteration:
                        pass
                gens = nxt

        def preps(d_, c_, par):
            lab = prep_chunk(d_, c_)
            return [prep_gen(d_, c_, par, ct, lab) for ct in range(4)]

        ybank_i = 0
        for d_ in range(2 if upto >= 5 else 0):
            memset("dve", STb[d_][:], 0.0, [STb[d_]] + [STB[d_][c4][h2] for c4 in range(4) for h2 in range(2)])
            order = [0, 1] + list(range(2, NTILE)) if d_ == 0 else [1, 0] + list(range(NTILE - 1, 1, -1))
            order = order[:int(_os.environ.get('RW_CHUNKS', '99'))]
            lockstep(preps(d_, order[0], 0))
            for i, c_ in enumerate(order):
                par = i % 2
                ybk = banks[7]
                ybank_i += 1
                gens = [pair_gen(d_, par, ct, ybk) for ct in range(4)]
                if i + 1 < len(order):
                    gens = gens + preps(d_, order[i + 1], 1 - par)
                lockstep(gens)
                readout(d_, c_, par, ybk)
            S.barrier()
        A.pop()

        A.push()
        pstate["nb"] = 8
        wbr = A.sb([128, 3, 4, D], BF16, "wbr")
        wo = A.sb([128, 8, D], BF16, "wo")
        for i in range(3):
            S.dma("pool", wbr[:, i, :, :], w_br[l, i].rearrange("(ct p) d -> p ct d", p=128), [], [wbr])
        S.dma("pool", wo[:], w_out[l].rearrange("(kc p) e -> p kc e", p=128), [], [wo])
        ybl = [Rot([A.sb([128, 4, 512], BF16, "ybl%d" % i) for _ in range(2)]) for i in range(3)]
        gbl = Rot([A.sb([128, 512], BF16, "gbl") for _ in range(6)])
        mT = Rot([A.sb([128, 8, 512], BF16, "mT") for _ in range(2)])
        macc = Rot([A.sb([128, 512], F32, "macc") for _ in range(3)])
        mtmp = Rot([A.sb([128, 512], F32, "mtmp") for _ in range(3)])
        xin = Rot([A.sb([128, 8, 512], F32, "mxin") for _ in range(2)])
        xo = Rot([A.sb([128, 512], F32, "mxo") for _ in range(3)])
        srcs = [yaT, ybT, ycT]
        xv = xcur.rearrange("(kc p) t -> p kc t", p=128)
        for bi, (t0, n) in enumerate(BLOCKS):
            if (last and bi == 0) or upto < 6:
                continue
            j = 1 if bi == 0 else 0
            yb3 = []
            for i in range(3):
                y_ = ybl[i].get()
                S.dma("sp", y_[:, :, :n], srcs[i].rearrange("(ct p) t -> p ct t", p=128)[:, :, t0:t0 + n], [], [y_])
                yb3.append(y_)
            x_ = xin.get()
            S.dma("sp", x_[:, :, :n], xv[:, :, t0:t0 + n], [], [x_])
            m_ = mT.get()
            for dt_ in range(8):
                ma = macc.get()
                for i in range(3):
                    g_ = gbl.get()
                    r0 = i * D + dt_ * 128
                    S.dma("sp", g_[:, :n], gT[r0:r0 + 128, t0:t0 + n], [], [g_])
                    pb = psum()
                    for ct in range(4):
                        mm(pb[:, :n], wbr[:, i, ct, dt_ * 128:(dt_ + 1) * 128], yb3[i][:, ct, :n], [wbr, yb3[i]], [pb],
                           start=(ct == 0), stop=(ct == 3))
                    if i == 0:
                        tt("dve", ma[:, :n], pb[:, :n], g_[:, :n], ALU.mult, [pb, g_], [ma])
                    else:
                        tp = mtmp.get()
                        tt("dve", tp[:, :n], pb[:, :n], g_[:, :n], ALU.mult, [pb, g_], [tp])
                        if i == 1:
                            tt("pool", ma[:, :n], ma[:, :n], tp[:, :n], ALU.add, [ma, tp], [ma])
                        else:
                            tt("pool", m_[:, dt_, :n], ma[:, :n], tp[:, :n], ALU.add, [ma, tp], [m_])
            for et in range(8):
                pb = psum()
                for dt_ in range(8):
                    mm(pb[:, :n], wo[:, dt_, et * 128:(et + 1) * 128], m_[:, dt_, :n], [wo, m_], [pb],
                       start=(dt_ == 0), stop=(dt_ == 7))
                o = xo.get()
                stt(o[:, :n], pb[:, :n], mod[:, 16 + et, j:j + 1], x_[:, et, :n], ALU.mult, ALU.add, [pb, mod, x_], [o])
                if last:
                    S.dma("pool", yT[et * 128:(et + 1) * 128, t0 - 256:t0 - 256 + n], o[:, :n], [o], [])
                else:
                    S.dma("pool", xnext[et * 128:(et + 1) * 128, t0:t0 + n], o[:, :n], [o], [])
        A.pop()
        A.pop()

    S.barrier()
    if dbg:
        A.push()
        for nm, (src, dst) in dbg_out.items():
            if len(src.shape) == 3:
                for a in range(src.shape[0]):
                    S.dma("sp", dst[a], src[a], [], [])
            else:
                S.dma("sp", dst, src, [], [])
        A.pop()
    if NL < DEPTH and upto >= 6:
        S.dma("sp", yT[:, :], xTs[(NL - 1) % 2][:, 256:TALL], [], [])
    A.pop()
    S.barrier()
    return nc, S


def _consts():
    cf = np.zeros((128, NCF), np.float32)
    cf[:, 0:128] = np.eye(128)
    blk = np.zeros((128, 128), np.float32)
    blk[:64, :64] = 1
    blk[64:, 64:] = 1
    cf[:, 128:256] = blk
    cf[:64, 256] = 1
    cf[64:, 257] = 1
    idx = np.arange(128)
    fw_strict = (idx[:, None] < idx[None, :]).astype(np.float32)
    bw_strict = (idx[:, None] > idx[None, :]).astype(np.float32)
    eye = np.eye(128, dtype=np.float32)
    cf[:, 260:388] = fw_strict
    cf[:, 388:516] = fw_strict + eye
    cf[:, 516:644] = bw_strict
    cf[:, 644:772] = bw_strict + eye
    cf[:, 772:900] = fw_strict.T
    cf[:, 900:1028] = bw_strict.T
    cf[:, 1028:1092] = eye[:, 0:64] + eye[:, 64:128]
    cb = np.zeros((128, NCB), np.float32)
    cb[:, 0:128] = np.eye(128)
    cb[:, 128:256] = 1
    cb[:, 256:384] = blk
    rotT = np.zeros((128, 128), np.float32)
    for m in range(128):
        f = m % 64
        base = m - f
        blk32 = (f // 32) * 32
        g = f % 32
        partner = base + blk32 + (g + 16 if g < 16 else g - 16)
        rotT[partner, m] = 1
    cb[:, 384:512] = rotT
    cb[:64, 512] = 1
    cb[64:, 513] = 1
    for d in range(2):
        for lev in range(7):
            b = 1 << lev
            blk = idx // b
            blk2 = idx // (2 * b)
            same2 = blk2[:, None] == blk2[None, :]
            diff = blk[:, None] != blk[None, :]
            before = (idx[:, None] < idx[None, :]) if d == 0 else (idx[:, None] > idx[None, :])
            m = (same2 & diff & before).astype(np.float32)
            c0 = 516 + (d * 7 + lev) * 128
            cb[:, c0:c0 + 128] = m
    nf = 16
    inv = (10000.0 ** (-np.arange(nf, dtype=np.float32) / nf)).astype(np.float32)
    tpos = np.arange(TLAT)
    row = (tpos // 64).astype(np.float32)
    col = (tpos % 64).astype(np.float32)
    C = np.ones((128, TALL), np.float32)
    Sg = np.zeros((128, TALL), np.float32)
    for p in range(128):
        f = p % 64
        pos = row if f < 32 else col
        g = f % 32
        i = g % 16
        ang = pos * inv[i]
        C[p, 256:] = np.cos(ang)
        Sg[p, 256:] = (-np.sin(ang)) if g < 16 else np.sin(ang)
    return cf, cb, C, Sg


def _host_inputs(inputs):
    f = lambda a: np.ascontiguousarray(np.asarray(a, dtype=np.float32))
    L = DEPTH
    pp = np.zeros((L, 128, NPP), np.float32)
    bc = np.zeros((L, 128, NBC), np.float32)
    b_mod = f(inputs["b_mod"])
    norm_g = f(inputs["norm_g"])
    r_conv = f(inputs["r_conv"])
    for l in range(L):
        pp[l, :, 0:24] = b_mod[l].reshape(24, 128).T
        pp[l, :, 24:32] = norm_g[l].reshape(8, 128).T
        pp[l, :, 32] = np.tile(f(inputs["d_qnorm"])[l], 2)
        pp[l, :, 33] = np.tile(f(inputs["d_knorm"])[l], 2)
        for which in range(3):
            for tap in range(3):
                pp[l, :, 34 + which * 12 + tap * 4:34 + which * 12 + tap * 4 + 4] = r_conv[l, which, tap].reshape(4, 128).T
        for d_ in range(2):
            pp[l, :, 70 + d_ * 4:74 + d_ * 4] = f(inputs["r_w0"])[l, d_].reshape(4, 128).T
            pp[l, :, 78 + d_ * 4:82 + d_ * 4] = f(inputs["r_a0"])[l, d_].reshape(4, 128).T
        pp[l, :, 86:90] = f(inputs["r_kk"])[l].reshape(4, 128).T
        pp[l, :, 90:94] = f(inputs["r_ka"])[l].reshape(4, 128).T
        pp[l, :, 98:102] = f(inputs["r_rk"])[l].reshape(4, 128).T
        row = np.concatenate([f(inputs["a_ln_g"])[l], f(inputs["a_ln_b"])[l], f(inputs["a_bs"])[l].reshape(-1),
                              f(inputs["d_subln_g"])[l], f(inputs["r_ln_g"])[l], f(inputs["r_ln_b"])[l],
                              f(inputs["d_lam"])[l].reshape(-1)])
        bc[l] = np.broadcast_to(row[None, :], (128, NBC))
    cf, cb, C, Sg = _consts()
    shared = dict(pp=pp, bc=bc, cf=cf, cbf=cb, ropeC=C, ropeS=Sg,
                  w_mod=f(inputs["w_mod"]), w_in=f(inputs["w_in"]),
                  a_wsT=np.ascontiguousarray(np.transpose(f(inputs["a_ws"]), (0, 1, 3, 2))),
                  r_w2=f(inputs["r_w2"]), r_a2=f(inputs["r_a2"]), w_br=f(inputs["w_br"]), w_out=f(inputs["w_out"]))
    x = f(inputs["x"])
    c = f(inputs["c"])
    ctx = f(inputs["ctx"])
    c_ctx = f(inputs["c_ctx"])
    maps = []
    for b in range(4):
        xT0 = np.ascontiguousarray(np.concatenate([ctx[b], x[b]], axis=0).T)
        cc = np.stack([c[b].reshape(8, 128).T, c_ctx.reshape(8, 128).T], axis=-1)
        m = dict(shared)
        m["xT0"] = xT0
        m["cc"] = np.ascontiguousarray(cc)
        maps.append(m)
    return maps


_CACHE = {}


def kernel(**inputs):
    maps = _host_inputs(inputs)
    if "nc" not in _CACHE:
        _CACHE["nc"] = build()[0]
    nc = _CACHE["nc"]
    in_maps = [maps[i % 4] for i in range(8)]
    res = run_bass_kernel_spmd(nc, in_maps, core_ids=list(range(8)))
    out = np.stack([np.ascontiguousarray(res.results[b]["yT"].T) for b in range(4)], axis=0)
    return out.astype(np.float32)
```
